# Optimizing a Trainium2 kernel written in Bass

```python
import math
import jax
import jax.numpy as jnp
from jax import lax
import numpy as np

D_MODEL = 2048
BATCH = 32
SEQ = 256
DEPTH = 2
DEC_BATCH = 4
DEC_SEQ = 4096
PAST_LEN = 512

GRID_W = 64
EPS = 1e-6
GN_EPS = 64e-5
ROPE_BASE = 10000.0
N_BRANCH = 4
BRANCH_W = D_MODEL // 4

A_HD = 64
A_HEADS = BRANCH_W // A_HD
A_W = A_HEADS * A_HD
A_DECAY_LORA = 64
A_ICLR_LORA = 64
A_GATE_LORA = 128
A_COLS = 3 * A_W + 2 * A_DECAY_LORA + 2 * A_ICLR_LORA + A_GATE_LORA

B_KD = 128
B_VD = 128
B_HEADS = BRANCH_W // B_VD
B_WK = B_HEADS * B_KD
B_W = B_HEADS * B_VD
B_CHUNK = 32
B_COLS = 3 * B_WK + 2 * B_W

C_KD = 128
C_VD = 128
C_HEADS = BRANCH_W // C_VD
C_WK = C_HEADS * C_KD
C_W = C_HEADS * C_VD
C_CONV = 3
C_CHUNK = 64
C_COLS = 2 * C_WK + 2 * C_W + 4 * C_HEADS

D_VD = 128
D_KD = 64
D_HEADS = BRANCH_W // D_VD
D_WK = D_HEADS * D_KD
D_W = D_HEADS * D_VD
D_CHUNK = 64
D_COLS = 2 * D_WK + 2 * D_W

IN_COLS = A_COLS + B_COLS + C_COLS + D_COLS
D_FF = 5632
FFN_CONV = 3

kernel_name = 'bidir_hybrid_flow_trunk'
F32 = jnp.float32


def _rms(x, gain):
    xf = x.astype(F32)
    y = xf * lax.rsqrt(jnp.mean(xf * xf, axis=-1, keepdims=True) + EPS)
    return (y * gain.astype(F32)).astype(x.dtype)


def _heads(x, h):
    return x.reshape(x.shape[:-1] + (h, x.shape[-1] // h))


def _head_rms(o, gain):
    h, dh = o.shape[-2:]
    return _rms(o, gain.reshape(h, dh)).reshape(o.shape[:-2] + (h * dh,))


def _head_layernorm(o, w, b):
    h, dh = o.shape[-2:]
    of = o.astype(F32)
    mu = jnp.mean(of, axis=-1, keepdims=True)
    var = jnp.mean(jnp.square(of - mu), axis=-1, keepdims=True)
    y = (of - mu) * lax.rsqrt(var + GN_EPS) * w.reshape(h, dh).astype(F32) + b.reshape(h, dh).astype(F32)
    return y.reshape(o.shape[:-2] + (h * dh,))


def _l2norm(x):
    xf = x.astype(F32)
    return (xf * lax.rsqrt(jnp.sum(xf * xf, axis=-1, keepdims=True) + EPS)).astype(x.dtype)


def _flip(x):
    return jnp.flip(x, axis=1)


def _masked_exp(mask, diff):
    return jnp.where(mask, jnp.exp(jnp.where(mask, diff, 0.0)), 0.0)


def _dwconv(x, w, b=None):
    ch = x.shape[-1]
    width = w.shape[0]
    y = lax.conv_general_dilated(x, w[:, None, :].astype(x.dtype), window_strides=(1,),
                                 padding=[(width // 2, width // 2)],
                                 dimension_numbers=('NWC', 'WIO', 'NWC'), feature_group_count=ch)
    return y if b is None else y + b


def _token_shift(p, mu):
    prev = jnp.pad(p, ((0, 0), (1, 0), (0, 0)))[:, :-1]
    nxt = jnp.pad(p, ((0, 0), (0, 1), (0, 0)))[:, 1:]
    return p + mu[0] * (prev - p) + mu[1] * (nxt - p)


def _grid_rotary(length):
    rows = length // GRID_W
    row = jnp.broadcast_to(jnp.arange(rows, dtype=F32)[:, None], (rows, GRID_W)).reshape(-1)
    col = jnp.broadcast_to(jnp.arange(GRID_W, dtype=F32)[None, :], (rows, GRID_W)).reshape(-1)
    n_pairs = D_KD // 4
    inv = ROPE_BASE ** (-jnp.arange(n_pairs, dtype=F32) / n_pairs)
    ang = jnp.concatenate([row[:, None] * inv, col[:, None] * inv], axis=-1)
    return jnp.cos(ang), jnp.sin(ang)


def _rotate(x, cos, sin):
    half = x.shape[-1] // 2
    xf = x.astype(F32)
    x1, x2 = xf[..., :half], xf[..., half:]
    c, s = cos[None, :, None, :], sin[None, :, None, :]
    return jnp.concatenate([x1 * c - x2 * s, x1 * s + x2 * c], axis=-1).astype(x.dtype)


def _to_chunks(t, c):
    b, l, h = t.shape[:3]
    t = t.astype(F32).reshape((b, l // c, c, h) + t.shape[3:])
    return jnp.moveaxis(jnp.moveaxis(t, 3, 2), 1, 0)


def _from_chunks(o):
    n, b, h, c, v = o.shape
    return jnp.moveaxis(jnp.moveaxis(o, 0, 1), 2, 3).reshape(b, n * c, h, v)


def _rwkv7_scan(r, log_w, k, v, kk, a, s0):
    seq = lambda t: jnp.moveaxis(t.astype(F32), 1, 0)

    def step(s, xs):
        r_t, lw_t, k_t, v_t, kk_t, a_t = xs
        s = (jnp.exp(lw_t)[..., None] * s
             - (a_t * kk_t)[..., None] * jnp.einsum('bhk,bhkv->bhv', kk_t, s)[..., None, :]
             + k_t[..., None] * v_t[..., None, :])
        return s, jnp.einsum('bhk,bhkv->bhv', r_t, s)

    s, o = lax.scan(step, s0.astype(F32), (seq(r), seq(log_w), seq(k), seq(v), seq(kk), seq(a)))
    return jnp.moveaxis(o, 0, 1), s


def _gla_chunked(q, k, v, log_f, s0):
    qc, kc, vc, fc = (_to_chunks(t, B_CHUNK) for t in (q, k, v, log_f))
    incl = jnp.tril(jnp.ones((B_CHUNK, B_CHUNK), bool))[:, :, None]

    def step(s, xs):
        q_i, k_i, v_i, f_i = xs
        cum = jnp.cumsum(f_i, axis=-2)
        decay = _masked_exp(incl, cum[..., :, None, :] - cum[..., None, :, :])
        scores = jnp.einsum('bhtk,bhtsk,bhsk->bhts', q_i, decay, k_i)
        o = jnp.einsum('bhts,bhsv->bhtv', scores, v_i) + jnp.einsum('bhtk,bhkv->bhtv', q_i * jnp.exp(cum), s)
        last = cum[..., -1:, :]
        s = jnp.exp(last[..., 0, :])[..., None] * s + jnp.einsum('bhsk,bhsv->bhkv', k_i * jnp.exp(last - cum), v_i)
        return s, o

    s, o = lax.scan(step, s0.astype(F32), (qc, kc, vc, fc))
    return _from_chunks(o), s


def _gated_delta_chunked(q, k, v, g, beta, s0):
    kdim = q.shape[-1]
    vdim = v.shape[-1]
    qc = _to_chunks(q, C_CHUNK) * kdim ** -0.5
    kc, vc, gc, bc = (_to_chunks(t, C_CHUNK) for t in (k, v, g, beta))
    cum = jnp.cumsum(gc, axis=-1)
    incl = jnp.tril(jnp.ones((C_CHUNK, C_CHUNK), bool))
    strict = jnp.tril(jnp.ones((C_CHUNK, C_CHUNK), bool), -1)
    decay = _masked_exp(incl, cum[..., :, None] - cum[..., None, :])
    k_beta = kc * bc[..., None]
    lower = jnp.where(strict, jnp.einsum('nbhsk,nbhtk->nbhst', k_beta, kc) * decay, 0.0)
    rhs = jnp.concatenate([vc * bc[..., None], k_beta * jnp.exp(cum)[..., None]], axis=-1)
    sol = lax.linalg.triangular_solve(lower, rhs, left_side=True, lower=True, unit_diagonal=True)
    u, w = sol[..., :vdim], sol[..., vdim:]
    attn = jnp.where(incl, jnp.einsum('nbhtk,nbhsk->nbhts', qc, kc) * decay, 0.0)

    def step(s, xs):
        q_i, k_i, u_i, w_i, c_i, a_i = xs
        v_new = u_i - jnp.einsum('bhck,bhkv->bhcv', w_i, s)
        o = (jnp.einsum('bhck,bhkv->bhcv', q_i * jnp.exp(c_i)[..., None], s)
             + jnp.einsum('bhts,bhsv->bhtv', a_i, v_new))
        last = c_i[..., -1:]
        s = jnp.exp(last)[..., None] * s + jnp.einsum('bhck,bhcv->bhkv', k_i * jnp.exp(last - c_i)[..., None], v_new)
        return s, o

    s, o = lax.scan(step, s0.astype(F32), (qc, kc, u, w, cum, attn))
    return _from_chunks(o), s


def _retention_chunked(q, k, v, log_gamma, s0):
    qc, kc, vc = (_to_chunks(t, D_CHUNK) for t in (q, k, v))
    lg = log_gamma.astype(F32)
    pos = jnp.arange(D_CHUNK, dtype=F32)
    rel = pos[:, None] - pos[None, :]
    decay = jnp.where(rel >= 0, jnp.exp(jnp.maximum(rel, 0.0)[None] * lg[:, None, None]), 0.0)
    scores = jnp.einsum('nbhtk,nbhsk->nbhts', qc, kc) * decay
    o_intra = jnp.einsum('nbhts,nbhsv->nbhtv', scores, vc)
    k_to_end = jnp.exp((D_CHUNK - 1 - pos)[None, :] * lg[:, None])
    q_from_start = jnp.exp((pos + 1.0)[None, :] * lg[:, None])
    kv = jnp.einsum('nbhsk,hs,nbhsv->nbhkv', kc, k_to_end, vc)
    g_chunk = jnp.exp(D_CHUNK * lg)[None, :, None, None]

    def step(s, kv_i):
        return g_chunk * s + kv_i, s

    s_fin, s_before = lax.scan(step, s0.astype(F32), kv)
    o_inter = jnp.einsum('nbhtk,ht,nbhkv->nbhtv', qc, q_from_start, s_before)
    return _from_chunks(o_intra + o_inter), s_fin


def _mixing_block(u, states, rot, p):
    s_a, s_b, s_c, s_d = states
    proj = u @ p['w_in']
    pa, pb, pc, pd = jnp.split(proj, [A_COLS, A_COLS + B_COLS, A_COLS + B_COLS + C_COLS], axis=-1)

    pa = _token_shift(pa, p['rwkv_mu'])
    cuts = np.cumsum([A_W, A_W, A_W, A_DECAY_LORA, A_DECAY_LORA, A_ICLR_LORA, A_ICLR_LORA]).tolist()
    r, k, v, wl_f, wl_b, al_f, al_b, gl = jnp.split(pa, cuts, axis=-1)
    kk = _l2norm(_heads(k * p['rwkv_k_k'], A_HEADS))
    rh, kh, vh = _heads(r, A_HEADS), _heads(k, A_HEADS), _heads(v, A_HEADS)

    def rwkv_dir(d, wl, al):
        w_raw = p['rwkv_w0'][d] + jnp.tanh(wl) @ p['rwkv_w_up'][d]
        log_w = -jnp.exp(-jax.nn.softplus(-w_raw) - 0.5)
        a = jax.nn.sigmoid(p['rwkv_a0'][d] + al @ p['rwkv_a_up'][d])
        k_d = k * (1.0 + (a - 1.0) * p['rwkv_k_a'])
        return _heads(log_w, A_HEADS), _heads(k_d, A_HEADS), _heads(a, A_HEADS)

    lw_f, kd_f, a_f = rwkv_dir(0, wl_f, al_f)
    lw_b, kd_b, a_b = rwkv_dir(1, wl_b, al_b)
    o_f, sa_f = _rwkv7_scan(rh, lw_f, kd_f, vh, kk, a_f, s_a[:, 0])
    o_b, sa_b = _rwkv7_scan(_flip(rh), _flip(lw_b), _flip(kd_b), _flip(vh), _flip(kk), _flip(a_b), s_a[:, 1])
    o_a = (o_f + _flip(o_b)).astype(u.dtype)
    bonus = (jnp.sum(rh * kh * p['rwkv_r_k'], axis=-1, keepdims=True) * vh).reshape(u.shape[:2] + (A_W,))
    g_a = jax.nn.sigmoid(gl) @ p['rwkv_g_up']
    out_a = ((_head_layernorm(o_a, p['rwkv_ln_w'], p['rwkv_ln_b']).astype(u.dtype) + bonus) * g_a)
    new_a = jnp.stack([sa_f, sa_b], axis=1)

    q_b, fl_f, fl_b, i_b, g_b = jnp.split(pb, [B_WK, 2 * B_WK, 3 * B_WK, 3 * B_WK + B_W], axis=-1)
    q_b = _heads(jax.nn.silu(q_b), B_HEADS)
    i_b = _heads(i_b, B_HEADS)
    lb = p['hgrn_lb']

    def hgrn_dir(fl):
        fl = fl.astype(F32)
        f = lb + (1.0 - lb) * jax.nn.sigmoid(fl)
        log_f = jnp.log(f)
        k_in = (1.0 - lb) * jax.nn.sigmoid(-fl)
        return _heads(log_f, B_HEADS), _heads(k_in, B_HEADS)

    lf_f, ki_f = hgrn_dir(fl_f)
    lf_b, ki_b = hgrn_dir(fl_b)
    ob_f, sb_f = _gla_chunked(q_b, ki_f, i_b, lf_f, s_b[:, 0])
    ob_b, sb_b = _gla_chunked(_flip(q_b), _flip(ki_b), _flip(i_b), _flip(lf_b), s_b[:, 1])
    o_bb = (ob_f + _flip(ob_b)).astype(u.dtype)
    out_b = _head_rms(o_bb, p['hgrn_norm']) * jax.nn.silu(g_b)
    new_b = jnp.stack([sb_f, sb_b], axis=1)

    qkv, gate_c, ab = jnp.split(pc, [2 * C_WK + C_W, 2 * C_WK + 2 * C_W], axis=-1)
    qkv = jax.nn.silu(_dwconv(qkv, p['gdn_conv_w']))
    q_c, k_c, v_c = jnp.split(qkv, [C_WK, 2 * C_WK], axis=-1)
    q_c, k_c, v_c = _l2norm(_heads(q_c, C_HEADS)), _l2norm(_heads(k_c, C_HEADS)), _heads(v_c, C_HEADS)
    ga_f, ga_b, gb_f, gb_b = jnp.split(ab, 4, axis=-1)

    def gdn_dir(d, a_logit, b_logit):
        g = -jnp.exp(p['gdn_a_log'][d]) * jax.nn.softplus(a_logit + p['gdn_dt_bias'][d])
        return g, jax.nn.sigmoid(b_logit)

    gc_f, bc_f = gdn_dir(0, ga_f, gb_f)
    gc_b, bc_b = gdn_dir(1, ga_b, gb_b)
    oc_f, sc_f = _gated_delta_chunked(q_c, k_c, v_c, gc_f, bc_f, s_c[:, 0])
    oc_b, sc_b = _gated_delta_chunked(_flip(q_c), _flip(k_c), _flip(v_c), _flip(gc_b), _flip(bc_b), s_c[:, 1])
    o_cc = (oc_f + _flip(oc_b)).astype(u.dtype)
    out_c = _head_rms(o_cc, p['gdn_norm']) * jax.nn.silu(gate_c)
    new_c = jnp.stack([sc_f, sc_b], axis=1)

    q_d, k_d, v_d, gate_d = jnp.split(pd, [D_WK, 2 * D_WK, 2 * D_WK + D_W], axis=-1)
    q_d, k_d, v_d = _heads(q_d, D_HEADS), _heads(k_d, D_HEADS) * D_KD ** -0.5, _heads(v_d, D_HEADS)
    if rot is not None:
        q_d = _rotate(q_d, rot[0], rot[1])
        k_d = _rotate(k_d, rot[0], rot[1])
    lg = jax.nn.log_sigmoid(p['ret_decay_logit'].astype(F32))
    od_f, sd_f = _retention_chunked(q_d, k_d, v_d, lg[0], s_d[:, 0])
    od_b, sd_b = _retention_chunked(_flip(q_d), _flip(k_d), _flip(v_d), lg[1], s_d[:, 1])
    o_dd = (od_f + _flip(od_b)).astype(u.dtype)
    out_d = _head_rms(o_dd, p['ret_norm']) * jax.nn.silu(gate_d)
    new_d = jnp.stack([sd_f, sd_b], axis=1)

    h = None
    for n, o_n in enumerate((out_a, out_b, out_c, out_d)):
        gate_n = jax.nn.sigmoid(u @ p['w_merge'][:, n * D_MODEL:(n + 1) * D_MODEL])
        term = gate_n * (o_n @ p['w_branch'][n])
        h = term if h is None else h + term
    return h @ p['w_out'], (new_a, new_b, new_c, new_d)


def _conv_ffn(u, w_up, conv_w, conv_b, w_down):
    h = _dwconv(u @ w_up, conv_w, conv_b)
    val, gate = jnp.split(h, 2, axis=-1)
    return (jax.nn.gelu(gate, approximate=True) * val) @ w_down


def _trunk_layer(x, mod, states, rot, p):
    sh1, sc1, g1, sh2, sc2, g2 = jnp.split(mod[:, None, :], 6, axis=-1)
    u = _rms(x, p['norm_mix_pre']) * (1.0 + sc1) + sh1
    h, new_states = _mixing_block(u, states, rot, p)
    x = x + g1 * _rms(h, p['norm_mix_post'])
    u = _rms(x, p['norm_ffn_pre']) * (1.0 + sc2) + sh2
    h = _conv_ffn(u, p['ffn_up'], p['ffn_conv_w'], p['ffn_conv_b'], p['ffn_down'])
    x = x + g2 * _rms(h, p['norm_ffn_post'])
    return x, new_states


def setup_inputs(seed: int = 0) -> dict:
    key = jax.random.key(seed)
    k = jax.random.split(key, 48)

    def nrm(i, shape, scale):
        return scale * jax.random.normal(k[i], shape, F32)

    def gain(i, shape):
        return 1.0 + 0.02 * jax.random.normal(k[i], shape, F32)

    L = DEPTH
    dt = jnp.exp(jax.random.uniform(k[30], (L, 2, C_HEADS), F32, minval=math.log(1e-3), maxval=math.log(1e-1)))
    ret_base = (5.0 + jnp.arange(D_HEADS, dtype=F32)) * math.log(2.0)
    return {
        'x_prompt': nrm(0, (BATCH, SEQ, D_MODEL), 1.0),
        'x_sample': nrm(1, (DEC_BATCH, DEC_SEQ, D_MODEL), 1.0),
        'state_rwkv': nrm(2, (DEC_BATCH, L, 2, A_HEADS, A_HD, A_HD), 0.5),
        'state_hgrn': nrm(3, (DEC_BATCH, L, 2, B_HEADS, B_KD, B_VD), 0.5),
        'state_gdn': nrm(4, (DEC_BATCH, L, 2, C_HEADS, C_KD, C_VD), 0.5),
        'state_ret': nrm(5, (DEC_BATCH, L, 2, D_HEADS, D_KD, D_VD), 0.5),
        'c': nrm(6, (DEC_BATCH, D_MODEL), 1.0),
        'c_ctx': nrm(7, (D_MODEL,), 1.0),
        'ada_w': nrm(8, (L, D_MODEL, 6 * D_MODEL), 0.5 * D_MODEL ** -0.5),
        'ada_b': nrm(9, (L, 6 * D_MODEL), 0.02),
        'norm_mix_pre': gain(10, (L, D_MODEL)),
        'norm_mix_post': gain(11, (L, D_MODEL)),
        'norm_ffn_pre': gain(12, (L, D_MODEL)),
        'norm_ffn_post': gain(13, (L, D_MODEL)),
        'w_in': nrm(14, (L, D_MODEL, IN_COLS), D_MODEL ** -0.5),
        'rwkv_mu': jax.random.uniform(k[15], (L, 2, A_COLS), F32, minval=0.0, maxval=0.5),
        'rwkv_w_up': nrm(16, (L, 2, A_DECAY_LORA, A_W), A_DECAY_LORA ** -0.5),
        'rwkv_w0': nrm(17, (L, 2, A_W), 0.5),
        'rwkv_a_up': nrm(18, (L, 2, A_ICLR_LORA, A_W), A_ICLR_LORA ** -0.5),
        'rwkv_a0': nrm(19, (L, 2, A_W), 0.5),
        'rwkv_g_up': nrm(20, (L, A_GATE_LORA, A_W), A_GATE_LORA ** -0.5),
        'rwkv_k_k': 1.0 + nrm(21, (L, A_W), 0.1),
        'rwkv_k_a': 1.0 + nrm(22, (L, A_W), 0.1),
        'rwkv_r_k': nrm(23, (L, A_HEADS, A_HD), 0.1),
        'rwkv_ln_w': gain(24, (L, A_W)),
        'rwkv_ln_b': nrm(25, (L, A_W), 0.02),
        'hgrn_lb_logits': nrm(26, (L, B_WK), 0.5),
        'hgrn_norm': gain(27, (L, B_W)),
        'gdn_conv_w': nrm(28, (L, C_CONV, 2 * C_WK + C_W), C_CONV ** -0.5),
        'gdn_a_log': jnp.log(jax.random.uniform(k[29], (L, 2, C_HEADS), F32, minval=1.0, maxval=16.0)),
        'gdn_dt_bias': jnp.log(jnp.expm1(dt)),
        'gdn_norm': gain(31, (L, C_W)),
        'ret_decay_logit': ret_base + nrm(32, (L, 2, D_HEADS), 0.1),
        'ret_norm': gain(33, (L, D_W)),
        'w_branch': nrm(34, (L, N_BRANCH, BRANCH_W, D_MODEL), BRANCH_W ** -0.5),
        'w_merge': nrm(35, (L, D_MODEL, N_BRANCH * D_MODEL), D_MODEL ** -0.5),
        'w_out': nrm(36, (L, D_MODEL, D_MODEL), D_MODEL ** -0.5),
        'ffn_up': nrm(37, (L, D_MODEL, 2 * D_FF), D_MODEL ** -0.5),
        'ffn_conv_w': nrm(38, (L, FFN_CONV, 2 * D_FF), FFN_CONV ** -0.5),
        'ffn_conv_b': nrm(39, (L, 2 * D_FF), 0.02),
        'ffn_down': nrm(40, (L, D_FF, D_MODEL), D_FF ** -0.5),
    }


def reference(x_prompt, x_sample, state_rwkv, state_hgrn, state_gdn, state_ret, c, c_ctx,
              ada_w, ada_b, norm_mix_pre, norm_mix_post, norm_ffn_pre, norm_ffn_post, w_in,
              rwkv_mu, rwkv_w_up, rwkv_w0, rwkv_a_up, rwkv_a0, rwkv_g_up, rwkv_k_k, rwkv_k_a, rwkv_r_k,
              rwkv_ln_w, rwkv_ln_b, hgrn_lb_logits, hgrn_norm, gdn_conv_w, gdn_a_log, gdn_dt_bias, gdn_norm,
              ret_decay_logit, ret_norm, w_branch, w_merge, w_out, ffn_up, ffn_conv_w, ffn_conv_b, ffn_down):
    sm = jax.nn.softmax(hgrn_lb_logits.astype(F32), axis=0)
    lower_bounds = jnp.cumsum(sm, axis=0) - sm[0]
    rot = _grid_rotary(x_sample.shape[1])
    bp = x_prompt.shape[0]
    zero_states = (jnp.zeros((bp, 2, A_HEADS, A_HD, A_HD), F32),
                   jnp.zeros((bp, 2, B_HEADS, B_KD, B_VD), F32),
                   jnp.zeros((bp, 2, C_HEADS, C_KD, C_VD), F32),
                   jnp.zeros((bp, 2, D_HEADS, D_KD, D_VD), F32))
    xp, xs = x_prompt, x_sample
    st_a, st_b, st_c, st_d = [], [], [], []
    for l in range(DEPTH):
        p = {'w_in': w_in[l], 'rwkv_mu': rwkv_mu[l], 'rwkv_w_up': rwkv_w_up[l], 'rwkv_w0': rwkv_w0[l],
             'rwkv_a_up': rwkv_a_up[l], 'rwkv_a0': rwkv_a0[l], 'rwkv_g_up': rwkv_g_up[l],
             'rwkv_k_k': rwkv_k_k[l], 'rwkv_k_a': rwkv_k_a[l], 'rwkv_r_k': rwkv_r_k[l],
             'rwkv_ln_w': rwkv_ln_w[l], 'rwkv_ln_b': rwkv_ln_b[l], 'hgrn_lb': lower_bounds[l],
             'hgrn_norm': hgrn_norm[l], 'gdn_conv_w': gdn_conv_w[l], 'gdn_a_log': gdn_a_log[l],
             'gdn_dt_bias': gdn_dt_bias[l], 'gdn_norm': gdn_norm[l], 'ret_decay_logit': ret_decay_logit[l],
             'ret_norm': ret_norm[l], 'w_branch': w_branch[l], 'w_merge': w_merge[l], 'w_out': w_out[l],
             'norm_mix_pre': norm_mix_pre[l], 'norm_mix_post': norm_mix_post[l],
             'norm_ffn_pre': norm_ffn_pre[l], 'norm_ffn_post': norm_ffn_post[l],
             'ffn_up': ffn_up[l], 'ffn_conv_w': ffn_conv_w[l], 'ffn_conv_b': ffn_conv_b[l],
             'ffn_down': ffn_down[l]}
        mod_ctx = jax.nn.silu(c_ctx)[None, :] @ ada_w[l] + ada_b[l]
        mod_lat = jax.nn.silu(c) @ ada_w[l] + ada_b[l]
        xp, ctx_states = _trunk_layer(xp, mod_ctx, zero_states, None, p)
        st_a.append(ctx_states[0])
        st_b.append(ctx_states[1])
        st_c.append(ctx_states[2])
        st_d.append(ctx_states[3])
        cached = (state_rwkv[:, l], state_hgrn[:, l], state_gdn[:, l], state_ret[:, l])
        xs, _ = _trunk_layer(xs, mod_lat, cached, rot, p)
    new_rwkv = jnp.stack(st_a, axis=1).astype(x_prompt.dtype)
    new_hgrn = jnp.stack(st_b, axis=1).astype(x_prompt.dtype)
    new_gdn = jnp.stack(st_c, axis=1).astype(x_prompt.dtype)
    new_ret = jnp.stack(st_d, axis=1).astype(x_prompt.dtype)
    return (xp, xs, new_rwkv, new_hgrn, new_gdn, new_ret)
```

```python
import numpy as np
import ml_dtypes
from contextlib import ExitStack
import concourse.bass as bass
import concourse.mybir as mybir
from concourse.bass_utils import run_bass_kernel_spmd

F32 = mybir.dt.float32
BF16 = mybir.dt.bfloat16
AF = mybir.ActivationFunctionType
ALU = mybir.AluOpType
AX = mybir.AxisListType

D = 2048
IN_COLS = 8080
A_COLS, B_COLS, C_COLS, D_COLS = 1920, 2560, 2064, 1536
OFF_A, OFF_B, OFF_C, OFF_D = 0, 1920, 4480, 6544
D_FF = 5632
EPS = 1e-6
GN_EPS = 64e-5
NEG = -30000.0


class Buf:
    __slots__ = ("name", "w", "r")

    def __init__(self, name=""):
        self.name = name
        self.w = {}
        self.r = {}


class V:
    __slots__ = ("ap", "b")

    def __init__(self, ap, b):
        self.ap = ap
        self.b = b

    def __getitem__(self, idx):
        return V(self.ap[idx], self.b)


class T:
    def __init__(self, t, name=""):
        self.t = t
        self.b = Buf(name)

    def __getitem__(self, idx):
        return V(self.t[idx], self.b)

    def all(self):
        return V(self.t[:], self.b)


class FW:
    EPOCH = 30000

    def __init__(self, nc):
        self.nc = nc
        self.eng = {"pe": nc.tensor, "dve": nc.vector, "act": nc.scalar, "pool": nc.gpsimd, "sp": nc.sync}
        self.sems = {}
        self.cur = {}
        self.epoch = {e: 0 for e in self.eng}
        for e in self.eng:
            self._new_epoch(e)
        self.waited = {e: {} for e in self.eng}
        self.dmaq = {}
        for q, n in (("sp", 14), ("pool", 8)):
            lst = []
            for i in range(n):
                k = f"d_{q}_{i}"
                self.sems[k] = nc.alloc_semaphore(name=k)
                lst.append([k, 0])
            self.dmaq[q] = [lst, 0]
        self.n_inst = {e: 0 for e in self.eng}

    def _new_epoch(self, e):
        k = f"p_{e}_{self.epoch[e]}"
        self.epoch[e] += 1
        self.sems[k] = self.nc.alloc_semaphore(name=k)
        self.cur[e] = [k, 0]

    def _wait(self, e, deps):
        for k, v in deps.items():
            if self.waited[e].get(k, 0) >= v:
                continue
            if k == self.cur[e][0]:
                if e == "pe":
                    continue
                if v > self.cur[e][1]:
                    continue
            self.eng[e].wait_ge(self.sems[k], v)
            self.waited[e][k] = v

    @staticmethod
    def _merge(dst, src):
        for k, v in src.items():
            if dst.get(k, 0) < v:
                dst[k] = v

    def _collect(self, reads, writes):
        deps = {}
        for b in reads:
            self._merge(deps, b.w)
        for b in writes:
            self._merge(deps, b.w)
            self._merge(deps, b.r)
        return deps

    def op(self, e, fn, reads=(), writes=(), inc=True):
        deps = self._collect(reads, writes)
        self._wait(e, deps)
        ins = fn()
        k, c = self.cur[e]
        if inc:
            c += 1
            ins.then_inc(self.sems[k], 1)
            self.cur[e][1] = c
            dep = {k: c}
        else:
            dep = {k: c + 1}
        for b in writes:
            b.w = dict(dep)
            b.r = {}
        for b in reads:
            self._merge(b.r, dep)
        self.n_inst[e] += 1
        if inc and c >= self.EPOCH:
            self._new_epoch(e)
        return ins

    def dma(self, q, out, in_, **kw):
        deps = self._collect([in_.b], [out.b])
        lst, idx = self.dmaq[q]
        slot = lst[idx % len(lst)]
        self.dmaq[q][1] = idx + 1
        k, v = slot
        if v > 0 and deps.get(k, 0) < v:
            deps[k] = v
        self._wait(q, deps)
        ins = self.eng[q].dma_start(out=out.ap, in_=in_.ap, **kw)
        ins.then_inc(self.sems[k], 16)
        slot[1] = v + 16
        dep = {k: v + 16}
        out.b.w = dict(dep)
        out.b.r = {}
        self._merge(in_.b.r, dep)
        self.n_inst[q] += 1

    def barrier(self):
        deps = {}
        for e in self.eng:
            k, c = self.cur[e]
            if c > 0:
                deps[k] = c
        for q in self.dmaq:
            for k, v in self.dmaq[q][0]:
                if v > 0:
                    deps[k] = v
        for e in self.eng:
            self._wait(e, deps)


def make_consts(dec_seq):
    c = 128
    s = np.arange(c)[:, None]
    t = np.arange(c)[None, :]
    out = {}
    out["ident_f"] = np.eye(c, dtype=np.float32)
    out["ident_b"] = np.eye(c, dtype=np.float32).astype(ml_dtypes.bfloat16)
    out["ones_f"] = np.ones((c, c), np.float32)
    cm = np.zeros((8, c, c), np.float32)
    nm = np.zeros((4, c, c), np.float32)
    dm = np.zeros((2, c, c), np.float32)
    sel3 = np.zeros((2, c, 4), np.float32)
    pos = np.zeros((2, c, 2), np.float32)
    for d in range(2):
        incl = (s <= t) if d == 0 else (s >= t)
        strict = (s < t) if d == 0 else (s > t)
        sel = np.zeros(c, np.float32)
        if d == 0:
            sel[: c // 2] = 1
        else:
            sel[c // 2:] = 1
        cm[d * 4 + 0] = incl
        cm[d * 4 + 1] = strict
        cm[d * 4 + 2] = incl - sel[:, None]
        cm[d * 4 + 3] = strict - sel[:, None]
        nm[d * 2 + 0] = np.where(incl, 0.0, NEG)
        nm[d * 2 + 1] = np.where(strict, 0.0, NEG)
        dist = (t - s) if d == 0 else (s - t)
        dm[d] = np.where(incl, dist, 1.0e6)
        sel3[d, :, 0] = sel
        sel3[d, :, 1] = 1.0
        sel3[d, :, 2] = 1.0 - sel
        tt = np.arange(c)
        pos[d, :, 0] = (tt + 1) if d == 0 else (c - tt)
        pos[d, :, 1] = (c - 1 - tt) if d == 0 else tt
    blk = lambda b: (s // b) == (t // b)
    bm = np.zeros((4, c, c), np.float32)
    bm[0] = blk(16)
    for i, b in enumerate((16, 32, 64)):
        bm[i + 1] = blk(2 * b) & ~blk(b)
    out["bmask"] = bm
    out["cmask"] = cm
    out["nmask"] = nm
    out["dmat"] = dm
    out["sel3"] = sel3
    out["pos"] = pos
    rows = dec_seq // 64
    row = np.broadcast_to(np.arange(rows, dtype=np.float32)[:, None], (rows, 64)).reshape(-1)
    col = np.broadcast_to(np.arange(64, dtype=np.float32)[None, :], (rows, 64)).reshape(-1)
    inv = (10000.0 ** (-np.arange(16, dtype=np.float32) / 16)).astype(np.float32)
    ang = np.concatenate([row[:, None] * inv, col[:, None] * inv], axis=-1).astype(np.float32)
    out["rot_cos"] = np.cos(ang).astype(np.float32)
    out["rot_sin"] = np.sin(ang).astype(np.float32)
    return out


class Cfg:
    def __init__(self, n_pseq=4, pseq=256, dseq=4096, depth=2, debug=None):
        self.n_pseq, self.pseq, self.dseq, self.depth, self.debug = n_pseq, pseq, dseq, depth, debug


W_SPECS = [
    ("ada_w", (D, 6 * D)), ("w_in", (D, IN_COLS)), ("w_merge", (D, 4 * D)), ("w_out", (D, D)),
    ("ffn_up", (D, 2 * D_FF)), ("ffn_down", (D_FF, D)),
]
SMALL = [("ada_b", (6 * D,)), ("norm_mix_pre", (D,)), ("norm_mix_post", (D,)), ("norm_ffn_pre", (D,)),
         ("norm_ffn_post", (D,)), ("rwkv_mu", (2, 1920)), ("rwkv_w_up", (2, 64, 512)), ("rwkv_w0", (2, 512)),
         ("rwkv_a_up", (2, 64, 512)), ("rwkv_a0", (2, 512)), ("rwkv_g_up", (128, 512)), ("rwkv_k_k", (512,)),
         ("rwkv_k_a", (512,)), ("rwkv_r_k", (8, 64)), ("rwkv_ln_w", (512,)), ("rwkv_ln_b", (512,)),
         ("hgrn_lb_logits", (512,)), ("hgrn_norm", (512,)), ("gdn_conv_w", (3, 1536)), ("gdn_a_log", (2, 4)),
         ("gdn_dt_bias", (2, 4)), ("gdn_norm", (512,)), ("ret_decay_logit", (2, 4)), ("ret_norm", (512,)),
         ("ffn_conv_w", (3, 2 * D_FF)), ("ffn_conv_b", (2 * D_FF,))]


class KB:
    def __init__(self, cfg):
        self.cfg = cfg
        nc = self.nc = bass.Bass("TRN2", target_bir_lowering=False)
        self.fw = FW(nc)
        self.E = self.fw.eng
        self.dr = {}
        self.uid = 0

    def dram(self, name, shape, dt, kind="Internal"):
        t = self.nc.dram_tensor(name, list(shape), dt, kind=kind)
        d = T(t, name)
        self.dr[name] = d
        return d

    def sb(self, es, shape, dt, name=None):
        self.uid += 1
        name = f"{name or 's'}_{self.uid}"
        return T(es.enter_context(self.nc.sbuf_tensor(name, list(shape), dt)), name)

    def ps(self, es, shape, dt, name=None):
        self.uid += 1
        name = f"{name or 'p'}_{self.uid}"
        return T(es.enter_context(self.nc.psum_tensor(name, list(shape), dt)), name)

    def mm(self, out, lhsT, rhs, start=True, stop=True, inc=True):
        self.fw.op("pe", lambda: self.nc.tensor.matmul(out.ap, lhsT=lhsT.ap, rhs=rhs.ap, start=start, stop=stop),
                   reads=[lhsT.b, rhs.b], writes=[out.b], inc=inc)

    def tr(self, out, in_, ident, inc=True):
        self.fw.op("pe", lambda: self.nc.tensor.transpose(out=out.ap, in_=in_.ap, identity=ident.ap),
                   reads=[in_.b, ident.b], writes=[out.b], inc=inc)

    def act(self, out, in_, func, scale=1.0, bias=0.0, accum=None):
        reads = [in_.b]
        kw = {}
        if isinstance(scale, V):
            reads.append(scale.b)
            kw["scale"] = scale.ap
        elif scale != 1.0:
            kw["scale"] = scale
        if isinstance(bias, V):
            reads.append(bias.b)
            kw["bias"] = bias.ap
        elif bias != 0.0:
            kw["bias"] = bias
        writes = [out.b]
        if accum is not None:
            kw["accum_out"] = accum.ap
            writes.append(accum.b)
        self.fw.op("act", lambda: self.nc.scalar.activation(out=out.ap, in_=in_.ap, func=func, **kw),
                   reads=reads, writes=writes)

    def _e(self, e):
        return {"dve": self.nc.vector, "pool": self.nc.gpsimd}[e]

    def tt(self, out, a, b, op, e="dve"):
        self.fw.op(e, lambda: self._e(e).tensor_tensor(out=out.ap, in0=a.ap, in1=b.ap, op=op),
                   reads=[a.b, b.b], writes=[out.b])

    def ts(self, out, a, s1, op0, s2=None, op1=None, e="dve", accum=None):
        reads = [a.b]
        kw = {}
        if isinstance(s1, V):
            reads.append(s1.b)
            s1 = s1.ap
        if isinstance(s2, V):
            reads.append(s2.b)
            s2 = s2.ap
        if op1 is not None:
            kw["op1"] = op1
        writes = [out.b]
        if accum is not None:
            kw["accum_out"] = accum.ap
            writes.append(accum.b)
        self.fw.op(e, lambda: self._e(e).tensor_scalar(out=out.ap, in0=a.ap, scalar1=s1, scalar2=s2, op0=op0, **kw),
                   reads=reads, writes=writes)

    def stt(self, out, a, s, b, op0, op1):
        reads = [a.b, b.b]
        if isinstance(s, V):
            reads.append(s.b)
            s = s.ap
        self.fw.op("dve", lambda: self.nc.vector.scalar_tensor_tensor(out=out.ap, in0=a.ap, scalar=s, in1=b.ap,
                                                                       op0=op0, op1=op1),
                   reads=reads, writes=[out.b])

    def cp(self, out, in_, e="dve"):
        if e == "act":
            self.fw.op("act", lambda: self.nc.scalar.copy(out=out.ap, in_=in_.ap), reads=[in_.b], writes=[out.b])
        else:
            self.fw.op(e, lambda: self._e(e).tensor_copy(out=out.ap, in_=in_.ap), reads=[in_.b], writes=[out.b])

    def red(self, out, in_, op=ALU.add, e="dve"):
        self.fw.op(e, lambda: self._e(e).tensor_reduce(out=out.ap, in_=in_.ap, op=op, axis=AX.X),
                   reads=[in_.b], writes=[out.b])

    def recip(self, out, in_):
        self.fw.op("dve", lambda: self.nc.vector.reciprocal(out=out.ap, in_=in_.ap), reads=[in_.b], writes=[out.b])

    def memset(self, out, val, e="dve"):
        self.fw.op(e, lambda: self._e(e).memset(out.ap, val), writes=[out.b])

    def dma(self, out, in_, q="sp", **kw):
        self.fw.dma(q, out, in_, **kw)

    def U(self, ap):
        return V(ap, Buf())

    def rsqrt(self, out, in_, scale, eps):
        self.act(out, in_, AF.Sqrt, scale=scale, bias=eps)
        self.recip(out, out)


class Prog(KB):
    def __init__(self, cfg):
        super().__init__(cfg)
        nc = self.nc
        dbg = cfg.debug or set()
        self.dbg = dbg
        NP, LP, LS, DEP = cfg.n_pseq, cfg.pseq, cfg.dseq, cfg.depth
        self.groups = [dict(name="p", nseq=NP, L=LP, rot=False, init=False),
                       dict(name="s", nseq=1, L=LS, rot=True, init=True)]
        if "only_p" in dbg:
            self.groups = self.groups[:1]
        if "only_s" in dbg:
            self.groups = self.groups[1:]
        ext_in = lambda n, s, dt=F32: self.dram(n, s, dt, kind="ExternalInput")
        ext_out = lambda n, s, dt=F32: self.dram(n, s, dt, kind="ExternalOutput")
        self.x_in = {"p": ext_in("x_p", [NP * LP, D]), "s": ext_in("x_s", [LS, D])}
        self.st_in = {"a": ext_in("st_a", [DEP, 2, 8, 64, 64]), "b": ext_in("st_b", [DEP, 2, 4, 128, 128]),
                      "c": ext_in("st_c", [DEP, 2, 4, 128, 128]), "d": ext_in("st_d", [DEP, 2, 4, 64, 128])}
        self.cvec = ext_in("cvec", [2, D])
        self.w = {}
        for n, s in W_SPECS:
            self.w[n] = ext_in(n, [DEP] + list(s))
        self.w["w_branch"] = ext_in("w_branch", [DEP, 4, 512, D])
        for n, s in SMALL:
            self.w[n] = ext_in(n, [DEP] + list(s))
        self.cst = {}
        for n, a in make_consts(LS).items():
            self.cst[n] = ext_in("c_" + n, list(a.shape), BF16 if a.dtype == ml_dtypes.bfloat16 else F32)
        self.y = {"p": ext_out("y_p", [NP * LP, D]), "s": ext_out("y_s", [LS, D])}
        self.ns = {"a": ext_out("ns_a", [NP, DEP, 2, 8, 64, 64]), "b": ext_out("ns_b", [NP, DEP, 2, 4, 128, 128]),
                   "c": ext_out("ns_c", [NP, DEP, 2, 4, 128, 128]), "d": ext_out("ns_d", [NP, DEP, 2, 4, 64, 128])}
        self.wb = {}
        for n, s in W_SPECS:
            self.wb[n] = self.dram("wb_" + n, [DEP] + list(s), BF16)
        self.wb["w_branch"] = self.dram("wb_w_branch", [DEP, 4, 512, D], BF16)
        kind = lambda f: "ExternalOutput" if f in dbg else "Internal"
        self.scr = {}
        for g in self.groups:
            gn, nt = g["name"], g["nseq"] * g["L"]
            self.scr[gn] = dict(
                proj=self.dram(f"proj_{gn}", [g["nseq"] * (g["L"] + 2), IN_COLS], F32, kind=kind("proj")),
                of=self.dram(f"of_{gn}", [nt, D], F32, kind=kind("of")),
                mix=self.dram(f"mix_{gn}", [nt, D], BF16, kind="ExternalInput" if "mix_in" in dbg else kind("mix")),
                xmid=self.dram(f"xmid_{gn}", [nt, D], F32, kind=kind("xmid")),
                x1=self.dram(f"x1_{gn}", [nt, D], F32, kind=kind("x1")),
                mod=self.dram(f"mod_{gn}", [DEP, 128, 6 * D], F32, kind=kind("mod")),
            )
        self.build()

    def cast_weights(self):
        for n in self.wb:
            src, dst = self.w[n], self.wb[n]
            sh = src.t.shape
            if len(sh) == 4:
                s2 = src.t.ap().rearrange("l n r c -> (l n r) c")
                d2 = dst.t.ap().rearrange("l n r c -> (l n r) c")
            else:
                s2 = src.t.ap().rearrange("l r c -> (l r) c")
                d2 = dst.t.ap().rearrange("l r c -> (l r) c")
            rows = s2.shape[0]
            step = 256
            for r0 in range(0, rows, step):
                r1 = min(rows, r0 + step)
                self.dma(V(d2[r0:r1, :], dst.b), V(s2[r0:r1, :], src.b), q="pool")

    def load_consts(self, es):
        c = {}
        for n in ("ident_f", "ident_b", "ones_f"):
            t = self.sb(es, [128, 128], BF16 if n == "ident_b" else F32, n)
            self.dma(t.all(), V(self.cst[n].t.ap(), self.cst[n].b))
            c[n] = t
        self.C = c

    def bc_load(self, dst, src_ap, src_b, P=128):
        self.dma(dst, V(src_ap.partition_broadcast(P), src_b))

    def setup_mods(self):
        DEP = self.cfg.depth
        with ExitStack() as es:
            cv = self.sb(es, [128, 16], F32, "cv")
            sg = self.sb(es, [128, 16], F32, "sg")
            scb = self.sb(es, [128, 16, 128], BF16, "scb")
            modt = self.sb(es, [128, 6 * D], F32, "modt")
            gt = self.sb(es, [128, D], F32, "gt")
            bt = [self.sb(es, [128, 512], F32, "bt") for _ in range(2)]
            wbuf = [self.sb(es, [128, 16, 512], BF16, "wb") for _ in range(2)]
            pp = [self.ps(es, [128, 512], F32, "pm") for _ in range(2)]
            for g in self.groups:
                gi = 0 if g["name"] == "p" else 1
                with self.nc.allow_non_contiguous_dma(reason="tiny feature-major load of conditioning vector"):
                    self.dma(cv.all(), V(self.cvec.t.ap()[gi, :].rearrange("(k p) -> p k", p=128), self.cvec.b))
                self.act(sg.all(), cv.all(), AF.Sigmoid)
                self.tt(sg.all(), sg.all(), cv.all(), ALU.mult)
                self.cp(scb.all(), V(sg.t[:, :].unsqueeze(2).to_broadcast([128, 16, 128]), sg.b))
                for l in range(DEP):
                    for j in range(24):
                        wt = wbuf[j % 2]
                        self.dma(wt.all(), V(self.wb["ada_w"].t.ap()[l, :, j * 512:(j + 1) * 512]
                                             .rearrange("(k p) c -> p k c", p=128), self.wb["ada_w"].b))
                        self.bc_load(bt[j % 2].all(), self.w["ada_b"].t.ap()[l, j * 512:(j + 1) * 512], self.w["ada_b"].b)
                        p = pp[j % 2]
                        for k in range(16):
                            self.mm(p.all(), scb[:, k, :], wt[:, k, :], start=(k == 0), stop=(k == 15), inc=(k == 15))
                        self.tt(modt[:, j * 512:(j + 1) * 512], p.all(), bt[j % 2].all(), ALU.add)
                    for (ch, gname) in ((1, "norm_mix_pre"), (4, "norm_ffn_pre")):
                        self.bc_load(gt.all(), self.w[gname].t.ap()[l, :], self.w[gname].b)
                        sl = modt[:, ch * D:(ch + 1) * D]
                        self.stt(sl, sl, 1.0, gt.all(), ALU.add, ALU.mult)
                    for (ch, gname) in ((2, "norm_mix_post"), (5, "norm_ffn_post")):
                        self.bc_load(gt.all(), self.w[gname].t.ap()[l, :], self.w[gname].b)
                        sl = modt[:, ch * D:(ch + 1) * D]
                        self.tt(sl, sl, gt.all(), ALU.mult)
                    md = self.scr[g["name"]]["mod"]
                    self.dma(self.U(md.t.ap()[l]), modt.all())
        self.fw.barrier()

    def norm_rows(self, x, G, sh, u_out, P, tmp, ss, rstd):
        self.act(tmp[:P, :], x, AF.Square, accum=ss[:P, :])
        self.rsqrt(rstd[:P, :], ss[:P, :], 1.0 / D, EPS)
        self.stt(tmp[:P, :], x, rstd[:P, :], G[:P, :], ALU.mult, ALU.mult)
        self.tt(u_out, tmp[:P, :], sh[:P, :], ALU.add, e="pool")

    def to_featT(self, u, P, dst, c0, ptr, KC=16):
        idb = self.C["ident_b"]
        for k0 in range(0, KC, 8):
            pt = ptr[(self.trc) % len(ptr)]
            self.trc += 1
            n = min(8, KC - k0)
            for k in range(n):
                self.tr(pt[:, k, :P], u[:, (k0 + k) * 128:(k0 + k + 1) * 128], idb[:P, :P], inc=(k == n - 1))
            self.cp(dst[:, k0:k0 + n, c0:c0 + P], pt[:, 0:n, :P], e="act")

    def load_w(self, wbufs, Wd, lsel, c0, cw, KC):
        wt = wbufs[self.wrr % len(wbufs)]
        self.wrr += 1
        src = Wd.t.ap()[lsel][:, c0:c0 + cw].rearrange("(k p) c -> p k c", p=128)
        view = V(wt.t[:, 0:KC * cw].rearrange("p (k c) -> p k c", c=cw), wt.b)
        self.dma(view, V(src, Wd.b))
        return view

    def tile_rows(self, g, ti):
        L = g["L"]
        tps = L // 128
        return ti // tps, (ti % tps) * 128

    def phase_a(self, g, l):
        gn = g["name"]
        nt = g["nseq"] * g["L"]
        xsrc = self.x_in[gn] if l == 0 else self.scr[gn]["x1"]
        proj = self.scr[gn]["proj"]
        mod = self.scr[gn]["mod"]
        TB = min(512, nt)
        nsub = TB // 128
        self.trc = 0
        self.wrr = 0
        with ExitStack() as es:
            G1 = self.sb(es, [128, D], F32, "G1")
            sh1 = self.sb(es, [128, D], F32, "sh1")
            self.dma(sh1.all(), V(mod.t.ap()[l, :, 0:D], mod.b))
            self.dma(G1.all(), V(mod.t.ap()[l, :, D:2 * D], mod.b))
            xt = [self.sb(es, [128, D], F32, "xt") for _ in range(2)]
            tmp = self.sb(es, [128, D], F32, "tmp")
            ss = self.sb(es, [128, 1], F32, "ss")
            rstd = self.sb(es, [128, 1], F32, "rstd")
            ub = [self.sb(es, [128, D], BF16, "ub") for _ in range(2)]
            uT = self.sb(es, [128, 16, TB], BF16, "uT")
            wbufs = [self.sb(es, [128, 8192], BF16, "wbuf") for _ in range(2)]
            stg = [self.sb(es, [128, 512], F32, "stg") for _ in range(4)]
            zrow = self.sb(es, [2, IN_COLS], F32, "zrow")
            ptr = [self.ps(es, [128, 8, 128], BF16, "ptr") for _ in range(2)]
            pbig = [self.ps(es, [128, 512], F32, "pbig") for _ in range(4)]
            self.memset(zrow.all(), 0.0)
            for s in range(g["nseq"]):
                r0 = s * (g["L"] + 2)
                for r in (r0, r0 + g["L"] + 1):
                    self.dma(self.U(proj.t.ap()[r:r + 1, :]), zrow[0:1, :])
            k_ = 0
            for b0 in range(0, nt, TB):
                for sub in range(nsub):
                    t0 = b0 + sub * 128
                    x = xt[sub % 2]
                    self.dma(x.all(), V(xsrc.t.ap()[t0:t0 + 128, :], xsrc.b))
                    u = ub[sub % 2]
                    self.norm_rows(x.all(), G1, sh1, u.all(), 128, tmp, ss, rstd)
                    self.to_featT(u.all(), 128, uT, sub * 128, ptr)
                for c0 in range(0, IN_COLS, 512):
                    cw = min(512, IN_COLS - c0)
                    wv = self.load_w(wbufs, self.wb["w_in"], l, c0, cw, 16)
                    for sub in range(nsub):
                        p = pbig[k_ % 4]
                        st = stg[k_ % 4]
                        k_ += 1
                        for k in range(16):
                            self.mm(p[:, 0:cw], uT[:, k, sub * 128:(sub + 1) * 128], wv[:, k, :],
                                    start=(k == 0), stop=(k == 15), inc=(k == 15))
                        self.cp(st[:, 0:cw], p[:, 0:cw], e=("act" if k_ % 2 else "dve"))
                        seq, pos = self.tile_rows(g, (b0 + sub * 128) // 128)
                        r0 = seq * (g["L"] + 2) + 1 + pos
                        self.dma(self.U(proj.t.ap()[r0:r0 + 128, c0:c0 + cw]), st[:, 0:cw])
        self.fw.barrier()

    def phase_c1(self, g, l):
        gn = g["name"]
        nt = g["nseq"] * g["L"]
        xsrc = self.x_in[gn] if l == 0 else self.scr[gn]["x1"]
        mix, xmid, mod = self.scr[gn]["mix"], self.scr[gn]["xmid"], self.scr[gn]["mod"]
        TB = min(256, nt)
        nsub = TB // 128
        self.trc = 0
        self.wrr = 0
        with ExitStack() as es:
            sh1 = self.sb(es, [128, D], F32, "sh1")
            G1 = self.sb(es, [128, D], F32, "G1")
            GP1 = self.sb(es, [128, D], F32, "GP1")
            self.dma(sh1.all(), self.U(mod.t.ap()[l, :, 0:D]))
            self.dma(G1.all(), self.U(mod.t.ap()[l, :, D:2 * D]))
            self.dma(GP1.all(), self.U(mod.t.ap()[l, :, 2 * D:3 * D]))
            xres = [self.sb(es, [128, D], F32, "xres") for _ in range(nsub)]
            ytmp = [self.sb(es, [128, D], F32, "ytmp") for _ in range(nsub)]
            tmp = self.sb(es, [128, D], F32, "tmp")
            ss = self.sb(es, [128, 1], F32, "ss")
            rstd = self.sb(es, [128, 1], F32, "rstd")
            ub = [self.sb(es, [128, D], BF16, "ub") for _ in range(2)]
            uT = self.sb(es, [128, 16, TB], BF16, "uT")
            mixT = self.sb(es, [128, 16, TB], BF16, "mixT")
            hT = self.sb(es, [128, 16, TB], F32, "hT")
            hTb = self.sb(es, [128, 16, TB], BF16, "hTb")
            gsb = [self.sb(es, [128, TB], F32, "gsb") for _ in range(2)]
            tsb = [self.sb(es, [128, TB], F32, "tsb") for _ in range(2)]
            wbufs = [self.sb(es, [128, 8192], BF16, "wbuf") for _ in range(2)]
            wbr = [self.sb(es, [128, 8192], BF16, "wbr") for _ in range(2)]
            ptr = [self.ps(es, [128, 8, 128], BF16, "ptr") for _ in range(2)]
            pbig = [self.ps(es, [128, 512], F32, "pbig") for _ in range(6)]
            k_ = 0
            for b0 in range(0, nt, TB):
                for sub in range(nsub):
                    t0 = b0 + sub * 128
                    self.dma(xres[sub].all(), self.U(xsrc.t.ap()[t0:t0 + 128, :]))
                    u = ub[sub % 2]
                    self.norm_rows(xres[sub].all(), G1, sh1, u.all(), 128, tmp, ss, rstd)
                    self.to_featT(u.all(), 128, uT, sub * 128, ptr)
                for sub in range(nsub):
                    t0 = b0 + sub * 128
                    u = ub[sub % 2]
                    self.dma(u.all(), self.U(mix.t.ap()[t0:t0 + 128, :]))
                    self.to_featT(u.all(), 128, mixT, sub * 128, ptr)
                for n in range(4):
                    wb_t = wbr[n % 2]
                    wbv = V(wb_t.t[:, :].rearrange("p (j c) -> p j c", c=D), wb_t.b)
                    self.dma(wbv, self.U(self.wb["w_branch"].t.ap()[l, n].rearrange("(j p) c -> p j c", p=128)))
                    for cg in range(4):
                        wv = self.load_w(wbufs, self.wb["w_merge"], l, n * D + cg * 512, 512, 16)
                        for f4 in range(4):
                            fo = cg * 4 + f4
                            p1 = pbig[k_ % 6]
                            p2 = pbig[(k_ + 1) % 6]
                            gs = gsb[(k_ // 2) % 2]
                            tsq = tsb[(k_ // 2) % 2]
                            k_ += 2
                            for k in range(16):
                                self.mm(p1[:, 0:TB], wv[:, k, f4 * 128:(f4 + 1) * 128], uT[:, k, :],
                                        start=(k == 0), stop=(k == 15), inc=(k == 15))
                            for j in range(4):
                                self.mm(p2[:, 0:TB], wbv[:, j, fo * 128:(fo + 1) * 128], mixT[:, n * 4 + j, :],
                                        start=(j == 0), stop=(j == 3), inc=(j == 3))
                            self.act(gs.all(), p1[:, 0:TB], AF.Sigmoid)
                            if n == 0:
                                self.tt(hT[:, fo, :], gs.all(), p2[:, 0:TB], ALU.mult)
                            else:
                                self.tt(tsq.all(), gs.all(), p2[:, 0:TB], ALU.mult)
                                self.tt(hT[:, fo, :], hT[:, fo, :], tsq.all(), ALU.add, e="pool")
                for q4 in range(4):
                    self.cp(hTb[:, q4 * 4:(q4 + 1) * 4, :], hT[:, q4 * 4:(q4 + 1) * 4, :], e=("act" if q4 % 2 else "dve"))
                for cg in range(4):
                    wv = self.load_w(wbufs, self.wb["w_out"], l, cg * 512, 512, 16)
                    for sub in range(nsub):
                        p = pbig[k_ % 6]
                        k_ += 1
                        for k in range(16):
                            self.mm(p.all(), hTb[:, k, sub * 128:(sub + 1) * 128], wv[:, k, :],
                                    start=(k == 0), stop=(k == 15), inc=(k == 15))
                        self.cp(ytmp[sub][:, cg * 512:(cg + 1) * 512], p.all(), e=("act" if k_ % 2 else "dve"))
                for sub in range(nsub):
                    t0 = b0 + sub * 128
                    self.residual(xres[sub], ytmp[sub], GP1, tmp, ss, rstd)
                    self.dma(self.U(xmid.t.ap()[t0:t0 + 128, :]), xres[sub].all())
        self.fw.barrier()

    def residual(self, xres, y, GP, tmp, ss, rstd):
        self.act(tmp.all(), y.all(), AF.Square, accum=ss.all())
        self.rsqrt(rstd.all(), ss.all(), 1.0 / D, EPS)
        self.stt(tmp.all(), y.all(), rstd.all(), GP.all(), ALU.mult, ALU.mult)
        self.tt(xres.all(), xres.all(), tmp.all(), ALU.add, e="pool")

    def phase_c2(self, g, l, last):
        gn = g["name"]
        L = g["L"]
        xmid, mod = self.scr[gn]["xmid"], self.scr[gn]["mod"]
        dst = self.y[gn] if last else self.scr[gn]["x1"]
        TBL = min(256, L)
        nsub = TBL // 128
        NH = 1 if TBL + 2 <= 512 else 2
        HW = (TBL + 2) // NH
        self.trc = 0
        self.wrr = 0
        with ExitStack() as es:
            sh2 = self.sb(es, [128, D], F32, "sh2")
            G2 = self.sb(es, [128, D], F32, "G2")
            GP2 = self.sb(es, [128, D], F32, "GP2")
            self.dma(sh2.all(), self.U(mod.t.ap()[l, :, 3 * D:4 * D]))
            self.dma(G2.all(), self.U(mod.t.ap()[l, :, 4 * D:5 * D]))
            self.dma(GP2.all(), self.U(mod.t.ap()[l, :, 5 * D:6 * D]))
            cw = self.sb(es, [128, 3, 88], F32, "cw")
            cb = self.sb(es, [128, 88], F32, "cb")
            with self.nc.allow_non_contiguous_dma(reason="one-off feature-major load of depthwise conv params"):
                for j in range(3):
                    self.dma(cw[:, j, :], self.U(self.w["ffn_conv_w"].t.ap()[l, j, :].rearrange("(k p) -> p k", p=128)))
                self.dma(cb.all(), self.U(self.w["ffn_conv_b"].t.ap()[l, :].rearrange("(k p) -> p k", p=128)))
            xres = [self.sb(es, [128, D], F32, "xres") for _ in range(nsub)]
            ytmp = [self.sb(es, [128, D], F32, "ytmp") for _ in range(nsub)]
            tmp = self.sb(es, [128, D], F32, "tmp")
            ss = self.sb(es, [128, 1], F32, "ss")
            rstd = self.sb(es, [128, 1], F32, "rstd")
            ub = [self.sb(es, [128, D], BF16, "ub") for _ in range(2)]
            hl = self.sb(es, [2, D], F32, "hl")
            hu = self.sb(es, [2, D], BF16, "hu")
            hT2 = self.sb(es, [128, 16, 2], BF16, "hT2")
            u2T = self.sb(es, [128, 16, TBL + 2], BF16, "u2T")
            aT = self.sb(es, [128, 44, TBL], BF16, "aT")
            hsb = [[self.sb(es, [128, TBL + 2], F32, "hsb") for _ in range(2)] for _ in range(2)]
            cv = [[self.sb(es, [128, TBL], F32, "cv") for _ in range(2)] for _ in range(2)]
            g1 = [self.sb(es, [128, TBL], F32, "g1") for _ in range(2)]
            g2 = [self.sb(es, [128, TBL], F32, "g2") for _ in range(2)]
            wbufs = [self.sb(es, [128, 11264], BF16, "wbuf") for _ in range(3)]
            ptr = [self.ps(es, [128, 8, 128], BF16, "ptr") for _ in range(2)]
            pbig = [self.ps(es, [128, 512], F32, "pbig") for _ in range(6)]
            k_ = 0
            it = 0
            for seq in range(g["nseq"]):
                for p0 in range(0, L, TBL):
                    tb0 = seq * L + p0
                    for sub in range(nsub):
                        t0 = tb0 + sub * 128
                        self.dma(xres[sub].all(), self.U(xmid.t.ap()[t0:t0 + 128, :]))
                        u = ub[sub % 2]
                        self.norm_rows(xres[sub].all(), G2, sh2, u.all(), 128, tmp, ss, rstd)
                        self.to_featT(u.all(), 128, u2T, 1 + sub * 128, ptr)
                    has_l, has_r = p0 > 0, p0 + TBL < L
                    self.memset(hl.all(), 0.0)
                    if has_l:
                        self.dma(hl[0:1, :], self.U(xmid.t.ap()[tb0 - 1:tb0, :]))
                    if has_r:
                        self.dma(hl[1:2, :], self.U(xmid.t.ap()[tb0 + TBL:tb0 + TBL + 1, :]))
                    self.norm_rows(hl.all(), G2, sh2, hu.all(), 2, tmp, ss, rstd)
                    self.to_featT(hu.all(), 2, hT2, 0, ptr)
                    for (c_src, c_dst, ok) in ((0, 0, has_l), (1, TBL + 1, has_r)):
                        if ok:
                            self.cp(u2T[:, :, c_dst:c_dst + 1], hT2[:, :, c_src:c_src + 1])
                        else:
                            self.memset(u2T[:, :, c_dst:c_dst + 1], 0.0)
                    for cgp in range(11):
                        wvs = [self.load_w(wbufs, self.wb["ffn_up"], l, cgp * 512, 512, 16),
                               self.load_w(wbufs, self.wb["ffn_up"], l, D_FF + cgp * 512, 512, 16)]
                        for f4 in range(4):
                            fo = cgp * 4 + f4
                            par = it % 2
                            it += 1
                            for wi in range(2):
                                ch = fo + 44 * wi
                                hs = hsb[wi][par]
                                for half in range(NH):
                                    p = pbig[k_ % 6]
                                    k_ += 1
                                    for k in range(16):
                                        self.mm(p[:, 0:HW], wvs[wi][:, k, f4 * 128:(f4 + 1) * 128],
                                                u2T[:, k, half * HW:(half + 1) * HW],
                                                start=(k == 0), stop=(k == 15), inc=(k == 15))
                                    self.cp(hs[:, half * HW:(half + 1) * HW], p[:, 0:HW], e="act")
                                c = cv[wi][par]
                                self.ts(c.all(), hs[:, 0:TBL], cw[:, 0, ch:ch + 1], ALU.mult, cb[:, ch:ch + 1], ALU.add)
                                self.stt(c.all(), hs[:, 1:TBL + 1], cw[:, 1, ch:ch + 1], c.all(), ALU.mult, ALU.add)
                                self.stt(c.all(), hs[:, 2:TBL + 2], cw[:, 2, ch:ch + 1], c.all(), ALU.mult, ALU.add)
                            val, gt = cv[0][par], cv[1][par]
                            a1, a2 = g1[par], g2[par]
                            self.tt(a1.all(), gt.all(), gt.all(), ALU.mult, e="pool")
                            self.ts(a1.all(), a1.all(), 0.044715, ALU.mult, 1.0, ALU.add, e="pool")
                            self.tt(a1.all(), a1.all(), gt.all(), ALU.mult, e="pool")
                            self.act(a2.all(), a1.all(), AF.Sigmoid, scale=1.5957691216057308)
                            self.tt(a2.all(), a2.all(), gt.all(), ALU.mult, e="pool")
                            self.tt(aT[:, fo, :], a2.all(), val.all(), ALU.mult, e="pool")
                    for cg in range(8):
                        wv = self.load_w(wbufs, self.wb["ffn_down"], l, cg * 256, 256, 44)
                        for sub in range(nsub):
                            p = pbig[k_ % 6]
                            k_ += 1
                            for k in range(44):
                                self.mm(p[:, 0:256], aT[:, k, sub * 128:(sub + 1) * 128], wv[:, k, :],
                                        start=(k == 0), stop=(k == 43), inc=(k == 43))
                            self.cp(ytmp[sub][:, cg * 256:(cg + 1) * 256], p[:, 0:256], e=("act" if k_ % 2 else "dve"))
                    for sub in range(nsub):
                        t0 = tb0 + sub * 128
                        self.residual(xres[sub], ytmp[sub], GP2, tmp, ss, rstd)
                        self.dma(self.U(dst.t.ap()[t0:t0 + 128, :]), xres[sub].all())
        self.fw.barrier()

    def build(self):
        dbg = self.dbg
        with ExitStack() as es:
            sc = (lambda n: self.nc.named_scope(n)) if "scopes" in dbg else (lambda n: ExitStack())
            self.load_consts(es)
            with sc("cast"):
                self.cast_weights()
                self.fw.barrier()
            with sc("mods"):
                self.setup_mods()
            for g in self.groups:
                for l in range(self.cfg.depth):
                    with sc(f"A_{g['name']}{l}"):
                        self.phase_a(g, l)
                    if "mix_in" not in dbg:
                        with sc(f"M_{g['name']}{l}"):
                            self.phase_m(g, l)
                    with sc(f"C1_{g['name']}{l}"):
                        self.phase_c1(g, l)
                    with sc(f"C2_{g['name']}{l}"):
                        self.phase_c2(g, l, last=(l == self.cfg.depth - 1))
            self.fw.barrier()


_PROG_CACHE = {}


def _prog(cfg_key):
    if cfg_key not in _PROG_CACHE:
        _PROG_CACHE[cfg_key] = Prog(Cfg(*cfg_key))
    return _PROG_CACHE[cfg_key]


def run(inputs, n_cores=8, debug=None, extra_in=None):
    x_prompt, x_sample = inputs["x_prompt"], inputs["x_sample"]
    B, LP = x_prompt.shape[0], x_prompt.shape[1]
    DB, LS = x_sample.shape[0], x_sample.shape[1]
    depth = inputs["w_in"].shape[0]
    NP = B // n_cores
    prog = _prog((NP, LP, LS, depth, frozenset(debug) if debug else None))
    consts = make_consts(LS)
    f32 = lambda a: np.ascontiguousarray(np.asarray(a, dtype=np.float32))
    shared = {}
    for n, _ in W_SPECS:
        shared[n] = f32(inputs[n])
    shared["w_branch"] = f32(inputs["w_branch"])
    for n, _ in SMALL:
        shared[n] = f32(inputs[n])
    for n, a in consts.items():
        shared["c_" + n] = a
    in_maps = []
    for c in range(n_cores):
        b = c % DB
        m = dict(shared)
        m["x_p"] = f32(x_prompt[c * NP:(c + 1) * NP]).reshape(NP * LP, D)
        m["x_s"] = f32(x_sample[b])
        m["st_a"] = f32(inputs["state_rwkv"][b])
        m["st_b"] = f32(inputs["state_hgrn"][b])
        m["st_c"] = f32(inputs["state_gdn"][b])
        m["st_d"] = f32(inputs["state_ret"][b])
        m["cvec"] = np.stack([f32(inputs["c_ctx"]), f32(inputs["c"][b])], 0)
        if extra_in:
            for k, v in extra_in(c).items():
                m[k] = v
        in_maps.append(m)
    res = run_bass_kernel_spmd(prog.nc, in_maps, core_ids=list(range(n_cores)))
    return res.results


def kernel(**inputs):
    r = run(inputs, 8)
    B, LP = inputs["x_prompt"].shape[:2]
    DB, LS = inputs["x_sample"].shape[:2]
    NP = B // 8
    y_p = np.concatenate([r[c]["y_p"].reshape(NP, LP, D) for c in range(8)], 0)
    y_s = np.stack([r[b]["y_s"] for b in range(DB)], 0)
    outs = [y_p.astype(np.float32), y_s.astype(np.float32)]
    for k in "abcd":
        outs.append(np.concatenate([r[c]["ns_" + k] for c in range(8)], 0).astype(np.float32))
    return tuple(outs)


class _NS:
    pass


def _b3(v, n, w):
    return V(v.ap.unsqueeze(2).to_broadcast([v.ap.shape[0], n, w]), v.b)


def _r3(v, n, w):
    return V(v.ap.rearrange("p (n w) -> p n w", w=w), v.b)


def m_alloc(self, es, g, l):
    M = _NS()
    nc = self.nc
    M.es = es
    sb = lambda shape, dt, n: self.sb(es, shape, dt, n)
    M.cm = [sb([128, 128], F32, "cm") for _ in range(8)]
    for i in range(8):
        self.dma(M.cm[i].all(), self.U(self.cst["cmask"].t.ap()[i]))
    M.nm = [sb([128, 128], F32, "nm") for _ in range(4)]
    for i in range(4):
        self.dma(M.nm[i].all(), self.U(self.cst["nmask"].t.ap()[i]))
    M.bm = [sb([128, 128], F32, "bm") for _ in range(4)]
    for i in range(4):
        self.dma(M.bm[i].all(), self.U(self.cst["bmask"].t.ap()[i]))
    M.dmat = [sb([128, 128], F32, "dmat") for _ in range(2)]
    M.sel3 = [sb([128, 4], F32, "sel3") for _ in range(2)]
    M.pos = [sb([128, 2], F32, "pos") for _ in range(2)]
    for d in range(2):
        self.dma(M.dmat[d].all(), self.U(self.cst["dmat"].t.ap()[d]))
        self.dma(M.sel3[d].all(), self.U(self.cst["sel3"].t.ap()[d]))
        self.dma(M.pos[d].all(), self.U(self.cst["pos"].t.ap()[d]))
    M.maskI = lambda d: M.cm[d * 4 + 0]
    M.maskS = lambda d: M.cm[d * 4 + 1]
    M.TmS = lambda d: M.cm[d * 4 + 2]
    M.TsS = lambda d: M.cm[d * 4 + 3]
    M.nmI = lambda d: M.nm[d * 2 + 0]
    M.nmS = lambda d: M.nm[d * 2 + 1]

    def bc(name, idx, n, dt=F32, P=128, q="sp"):
        t = sb([P, n], dt, "bc_" + name)
        src = self.w[name].t.ap()[l]
        for i in idx:
            src = src[i]
        self.dma(t.all(), self.U(src.partition_broadcast(P)), q=q)
        return t
    M.bc = bc
    M.mu = [bc("rwkv_mu", (d,), 1920, BF16, q="pool") for d in range(2)]
    M.k_k = bc("rwkv_k_k", (), 512)
    M.k_a = bc("rwkv_k_a", (), 512)
    t = sb([128, 512], F32, "bc_r_k")
    self.dma(t.all(), self.U(self.w["rwkv_r_k"].t.ap()[l].rearrange("h k -> (h k)").partition_broadcast(128)))
    M.r_k = t
    M.ln_w = bc("rwkv_ln_w", (), 512)
    M.ln_b = bc("rwkv_ln_b", (), 512)
    M.w0 = [bc("rwkv_w0", (d,), 512) for d in range(2)]
    M.a0 = [bc("rwkv_a0", (d,), 512) for d in range(2)]
    M.w_up, M.a_up = [], []
    for d in range(2):
        for (lst, nm_) in ((M.w_up, "rwkv_w_up"), (M.a_up, "rwkv_a_up")):
            t = sb([64, 512], BF16, nm_)
            self.dma(t.all(), self.U(self.w[nm_].t.ap()[l, d]), q="pool")
            lst.append(t)
    M.g_up = sb([128, 512], BF16, "g_up")
    self.dma(M.g_up.all(), self.U(self.w["rwkv_g_up"].t.ap()[l]), q="pool")
    M.hnorm = bc("hgrn_norm", (), 512, P=32)
    M.lb = sb([32, 512], F32, "lb")
    M.oml = sb([32, 512], F32, "oml")
    if l == 0:
        self.memset(M.lb.all(), 0.0)
        self.memset(M.oml.all(), 1.0)
    else:
        self.dma(M.lb.all(), self.U(self.w["hgrn_lb_logits"].t.ap()[1].partition_broadcast(32)))
        self.dma(M.oml.all(), self.U(self.w["hgrn_lb_logits"].t.ap()[0].partition_broadcast(32)))
        self.tt(M.lb.all(), M.lb.all(), M.oml.all(), ALU.subtract)
        self.act(M.lb.all(), M.lb.all(), AF.Sigmoid)
        self.ts(M.oml.all(), M.lb.all(), -1.0, ALU.mult, 1.0, ALU.add)
    M.cw = [bc("gdn_conv_w", (j,), 1536, BF16, q="pool") for j in range(3)]
    M.gnorm = bc("gdn_norm", (), 512)
    t = sb([128, 8], F32, "negA")
    self.dma(t.all(), self.U(self.w["gdn_a_log"].t.ap()[l].rearrange("d h -> (d h)").partition_broadcast(128)))
    self.act(t.all(), t.all(), AF.Exp)
    self.ts(t.all(), t.all(), -1.0, ALU.mult)
    M.negA = t
    t = sb([128, 8], F32, "dtb")
    self.dma(t.all(), self.U(self.w["gdn_dt_bias"].t.ap()[l].rearrange("d h -> (d h)").partition_broadcast(128)))
    M.dtb = t
    M.rnorm = bc("ret_norm", (), 512)
    lg = sb([128, 8], F32, "lg")
    self.dma(lg.all(), self.U(self.w["ret_decay_logit"].t.ap()[l].rearrange("d h -> (d h)").partition_broadcast(128)))
    self.act(lg.all(), lg.all(), AF.Exp, scale=-1.0)
    self.act(lg.all(), lg.all(), AF.Ln, bias=1.0)
    self.ts(lg.all(), lg.all(), -1.0, ALU.mult)
    M.lg = lg
    M.tabR, M.tabK, M.f1r, M.GR = [], [], [], []
    for d in range(2):
        tr_ = sb([128, 4], F32, "tabR")
        tk_ = sb([128, 4], F32, "tabK")
        f1_ = sb([128, 4], F32, "f1r")
        self.ts(tr_.all(), lg[:, 4 * d:4 * d + 4], M.pos[d][:, 0:1], ALU.mult)
        self.act(tr_.all(), tr_.all(), AF.Exp)
        self.ts(tk_.all(), lg[:, 4 * d:4 * d + 4], M.pos[d][:, 1:2], ALU.mult)
        self.act(tk_.all(), tk_.all(), AF.Exp)
        self.act(f1_.all(), lg[:, 4 * d:4 * d + 4], AF.Exp, scale=128.0)
        M.tabR.append(tr_)
        M.tabK.append(tk_)
        M.f1r.append(f1_)
        gr = sb([128, 4, 128], F32, "GR")
        for h in range(4):
            self.act(gr[:, h, :], M.dmat[d].all(), AF.Exp, scale=lg[:, 4 * d + h:4 * d + h + 1])
        M.GR.append(gr)
    M.W = [sb([128, 1920], F32, "W") for _ in range(3)]
    M.F = [sb([128, 512], F32, "F") for _ in range(13)]
    M.H = [sb([128, 512], BF16, "H") for _ in range(10)]
    M.TP = [sb([128, 8, 128], BF16, "TP") for _ in range(6)]
    M.sm = [sb([128, 16], F32, "sm") for _ in range(6)]
    M.lT = [sb([128, 128], BF16, "lT") for _ in range(3)]
    M.cols = sb([128, 32], F32, "cols")
    M.Gia = sb([128, 4, 128], F32, "Gia")
    M.Gea = sb([128, 4, 128], F32, "Gea")
    M.Gb = [sb([128, 128], F32, "Gb") for _ in range(2)]
    M.rot = [sb([128, 32], F32, "rot") for _ in range(2)]
    mk = lambda n: sb([128, 8, 128], BF16, n)
    M.AkT, M.NT, M.AbT, M.Lf, M.LT, M.TT, M.NoT, M.A1 = [mk(n) for n in ("AkT", "NT", "AbT", "Lf", "LT", "TT", "NoT", "A1")]
    M.Q = [mk("Q0"), mk("Q1")]
    M.QT = [mk("QT0"), mk("QT1")]
    M.P = [mk("P0"), mk("P1")]
    M.Xa = sb([128, 512], BF16, "Xa")
    M.nEa = sb([128, 512], BF16, "nEa")
    M.tSa = sb([128, 512], F32, "tSa")
    M.SA = sb([64, 8, 64], F32, "SA")
    M.SAp = sb([64, 8, 64], BF16, "SAp")
    M.SB = sb([128, 4, 128], F32, "SB")
    M.SBp = sb([128, 4, 128], BF16, "SBp")
    M.SC = sb([128, 4, 128], F32, "SC")
    M.SCp = sb([128, 4, 128], BF16, "SCp")
    M.SD = sb([64, 4, 128], F32, "SD")
    M.SDp = sb([64, 4, 128], BF16, "SDp")
    M.pw = [self.ps(es, [128, 512], F32, "pw") for _ in range(2)]
    M.ptr = [self.ps(es, [128, 8, 128], BF16, "ptr") for _ in range(2)]
    M.slots = []
    for i in range(4):
        bank = es.enter_context(nc.psum_tensor(f"slotbank_{l}_{g['name']}_{i}", [128, 4, 128], F32))
        M.slots.append(V(bank[:, :, :], Buf()))
    M.si = 0
    M.pti = 0
    M.pwi = 0
    M.fi = 0
    M.hi = 0
    return M


def m_slot(self, M):
    s = M.slots[M.si % len(M.slots)]
    M.si += 1
    return s


def m_pw(self, M):
    p = M.pw[M.pwi % 2]
    M.pwi += 1
    return p


def m_trT(self, M, src, c, nblk, w, dst):
    idb = self.C["ident_b"]
    pt = M.ptr[M.pti % 2]
    M.pti += 1
    for k in range(nblk):
        self.tr(pt[:w, k, :c], src[:, k * w:(k + 1) * w], idb[:c, :c], inc=(k == nblk - 1))
    self.cp(dst[:w, 0:nblk, :c], pt[:w, 0:nblk, :c], e="act")


def _bm(mask_v, c, n):
    return V(mask_v.ap.unsqueeze(1).to_broadcast([c, n, c]), mask_v.b)


def m_core(self, M, c, K, Vd, Hn, delta, RaT, RbT, KaT, BaT, CaT, CbT, Kc, Bc, Vv, Mii, Mei, f1, f2, S, S0p, O):
    idb = self.C["ident_b"]
    groups = [(h0, min(4, Hn - h0)) for h0 in range(0, Hn, 4)]
    sl = lambda: m_slot(self, M)

    def vbank():
        b = sl()
        return V(b.ap.rearrange("p a b -> p (a b)").rearrange("p (h v) -> p h v", v=Vd), b.b)

    def scores(dst, lf, rf, mult, neg=False):
        for (h0, n) in groups:
            b = sl()
            for i in range(n):
                self.mm(b[:c, i, :c], lf(h0 + i), rf(h0 + i), inc=(i == n - 1))
            if neg:
                self.stt(dst[:c, h0:h0 + n, :c], b[:c, 0:n, :c], -1.0, mult(h0, n), ALU.mult, ALU.mult)
            else:
                self.tt(dst[:c, h0:h0 + n, :c], b[:c, 0:n, :c], mult(h0, n), ALU.mult)

    def mm_stage(dst, lt, rt, eng):
        for gi, (h0, n) in enumerate(groups):
            b = sl()
            for i in range(n):
                self.mm(b[:, i, :], lt[:, h0 + i, :], rt[:, h0 + i, :], inc=(i == n - 1))
            self.cp(dst[:, h0:h0 + n, :], b[:, 0:n, :], e=(eng if gi % 2 == 0 else ("dve" if eng == "act" else "act")))

    def upd_stage(dst, lt, rt, base):
        for gi, (h0, n) in enumerate(groups):
            b = sl()
            for i in range(n):
                self.mm(b[:, i, :], lt[:, h0 + i, :], rt[:, h0 + i, :], inc=(i == n - 1))
            self.tt(dst[:, h0:h0 + n, :], base[:, h0:h0 + n, :], b[:, 0:n, :], ALU.add)

    def transp(dst, src):
        for h0 in range(0, Hn, 8):
            n = min(8, Hn - h0)
            pt = M.ptr[M.pti % 2]
            M.pti += 1
            for i in range(n):
                self.tr(pt[:, i, :], src[:, h0 + i, :], idb.all(), inc=(i == n - 1))
            self.cp(dst[:, h0:h0 + n, :], pt[:, 0:n, :], e="act")

    scores(M.AkT, KaT, RaT, Mii)
    if delta:
        assert c == 128
        scores(M.NT, KaT, CaT, Mei)
        scores(M.Lf, BaT, CaT, Mei, neg=True)
        transp(M.LT, M.Lf)
        hs = slice(0, Hn)
        self.tt(M.Q[0][:, hs, :], M.Lf[:, hs, :], _bm(M.bm[0].all(), 128, Hn), ALU.mult)
        self.tt(M.QT[0][:, hs, :], M.LT[:, hs, :], _bm(M.bm[0].all(), 128, Hn), ALU.mult, e="pool")
        self.tt(M.P[0][:, hs, :], M.Q[0][:, hs, :], _bm(idb.all(), 128, Hn), ALU.add, e="pool")
        cur = 0
        for i in range(1, 4):
            nxt = 1 - cur
            if i < 3:
                mm_stage(M.Q[nxt], M.QT[cur], M.Q[cur], "dve")
            mm_stage(M.QT[nxt], M.Q[cur], M.QT[cur], "act")
            upd_stage(M.P[nxt], M.QT[nxt], M.P[cur], M.P[cur])
            cur = nxt
        transp(M.TT, M.P[cur])
        for bi in range(3):
            nxt = 1 - cur
            self.tt(M.NoT[:, hs, :], M.LT[:, hs, :], _bm(M.bm[bi + 1].all(), 128, Hn), ALU.mult, e="pool")
            mm_stage(M.A1, M.NoT, M.P[cur], "act")
            upd_stage(M.P[nxt], M.TT, M.A1, M.P[cur])
            cur = nxt
            if bi < 2:
                transp(M.TT, M.P[cur])
        Pf = M.P[cur]
        Xa = V(M.Xa.t[:, :].rearrange("p (h v) -> p h v", v=Vd), M.Xa.b)
        nE = V(M.nEa.t[:, :].rearrange("p (h v) -> p h v", v=Vd), M.nEa.b)
        bv = vbank()
        for h in range(Hn):
            self.mm(bv[:c, h, :], CbT(h), S0p(h), start=True, stop=False, inc=False)
            self.mm(bv[:c, h, :], M.NT[:c, h, :c], Vv(h), start=False, stop=True, inc=(h == Hn - 1))
        self.cp(Xa[:c, :, :], bv[:c, :, :], e="act")
        bv = vbank()
        for h in range(Hn):
            self.mm(bv[:c, h, :], Pf[:c, h, :c], Xa[:c, h, :], inc=(h == Hn - 1))
        self.ts(nE[:c, :, :], bv[:c, :, :], -1.0, ALU.mult)
        scores(M.AbT, BaT, RaT, Mii)
    bv = vbank()
    for h in range(Hn):
        self.mm(bv[:c, h, :], RbT(h), S0p(h), start=True, stop=False, inc=False)
        self.mm(bv[:c, h, :], M.AkT[:c, h, :c], Vv(h), start=False, stop=(not delta), inc=((not delta) and h == Hn - 1))
        if delta:
            self.mm(bv[:c, h, :], M.AbT[:c, h, :c], nE[:c, h, :], start=False, stop=True, inc=(h == Hn - 1))
    self.cp(O, bv[:c, :, :], e="act")
    bv = vbank()
    for h in range(Hn):
        self.mm(bv[:K, h, :], Kc(h), Vv(h), start=True, stop=(not delta), inc=((not delta) and h == Hn - 1))
        if delta:
            self.mm(bv[:K, h, :], Bc(h), nE[:c, h, :], start=False, stop=True, inc=(h == Hn - 1))
    f1b = _b3(f1, Hn, Vd)
    if f2 is not None:
        tS = V(M.tSa.t[:K, :].rearrange("p (h v) -> p h v", v=Vd), M.tSa.b)
        self.tt(tS, bv[:K, :, :], _b3(f2, Hn, Vd), ALU.mult)
        self.tt(S, S, f1b, ALU.mult, e="pool")
        self.tt(S, S, tS, ALU.add, e="pool")
    else:
        self.tt(S, S, f1b, ALU.mult, e="pool")
        self.tt(S, S, bv[:K, :, :], ALU.add)


def m_F(self, M):
    t = M.F[M.fi % len(M.F)]
    M.fi += 1
    return t


def m_H(self, M):
    t = M.H[M.hi % len(M.H)]
    M.hi += 1
    return t


def m_out(self, M, g, d, t0, P, col0, O, fin, rows=None):
    gn = g["name"]
    of, mix = self.scr[gn]["of"], self.scr[gn]["mix"]
    key = (gn, t0, col0)
    if d == 0:
        b = Buf()
        self.ofb[key] = b
        self.dma(V(of.t.ap()[t0:t0 + P, col0:col0 + 512], b), O)
    else:
        b = self.ofb[key]
        ofl = m_F(self, M)
        self.dma(ofl[:P, :], V(of.t.ap()[t0:t0 + P, col0:col0 + 512], b))
        self.tt(O, O, ofl[:P, :], ALU.add)
        ob = fin()
        self.dma(self.U(mix.t.ap()[t0:t0 + P, col0:col0 + 512]), ob)


def m_headnorm_gate(self, M, O, P, gate, normw):
    sq = m_F(self, M)
    self.tt(sq[:P, :], O, O, ALU.mult, e="pool")
    s4 = M.sm[5]
    self.red(s4[:P, 0:4], _r3(sq[:P, :], 4, 128))
    self.rsqrt(s4[:P, 0:4], s4[:P, 0:4], 1.0 / 128, EPS)
    self.tt(_r3(O, 4, 128), _r3(O, 4, 128), _b3(s4[:P, 0:4], 4, 128), ALU.mult)
    self.tt(O, O, normw[:P, :], ALU.mult)
    sg = m_F(self, M)
    self.act(sg[:P, :], gate, AF.Sigmoid)
    self.tt(sg[:P, :], sg[:P, :], gate, ALU.mult, e="pool")
    ob = m_H(self, M)
    self.tt(ob[:P, :], O, sg[:P, :], ALU.mult)
    return ob[:P, :]


def m_rwkv(self, M, g, l, d, r0, t0):
    proj = self.scr[g["name"]]["proj"]
    W0, W1, W2 = M.W
    for j, Wt in enumerate(M.W):
        self.dma(Wt[:, 0:1920], self.U(proj.t.ap()[r0 - 1 + j:r0 - 1 + j + 128, OFF_A:OFF_A + 1920]))
    self.tt(W0.all(), W0.all(), W1.all(), ALU.subtract, e="pool")
    self.tt(W0.all(), W0.all(), M.mu[0].all(), ALU.mult, e="pool")
    self.tt(W2.all(), W2.all(), W1.all(), ALU.subtract)
    self.tt(W2.all(), W2.all(), M.mu[1].all(), ALU.mult)
    self.tt(W1.all(), W1.all(), W0.all(), ALU.add, e="pool")
    self.tt(W1.all(), W1.all(), W2.all(), ALU.add)
    r, k, v = W1[:, 0:512], W1[:, 512:1024], W1[:, 1024:1536]
    wl = W1[:, 1536 + 64 * d:1600 + 64 * d]
    al = W1[:, 1664 + 64 * d:1728 + 64 * d]
    gl = W1[:, 1792:1920]
    F = lambda: m_F(self, M)
    Hh = lambda: m_H(self, M)
    idb = self.C["ident_b"]
    kk = F()
    self.tt(kk.all(), k, M.k_k.all(), ALU.mult)
    sq = F()
    self.tt(sq.all(), kk.all(), kk.all(), ALU.mult, e="pool")
    s8 = M.sm[0]
    self.red(s8[:, 0:8], _r3(sq.all(), 8, 64))
    self.rsqrt(s8[:, 0:8], s8[:, 0:8], 1.0, EPS)
    self.tt(_r3(kk.all(), 8, 64), _r3(kk.all(), 8, 64), _b3(s8[:, 0:8], 8, 64), ALU.mult)
    th = M.lT[0]
    self.act(th[:, 0:64], wl, AF.Tanh)
    pt = M.ptr[M.pti % 2]
    M.pti += 1
    self.tr(pt[:64, 0, :], th[:, 0:64], idb.all())
    thT = M.lT[1]
    self.cp(thT[:64, :], pt[:64, 0, :], e="act")
    pw = m_pw(self, M)
    self.mm(pw.all(), thT[:64, :], M.w_up[d].all())
    LW = F()
    self.tt(LW.all(), pw.all(), M.w0[d].all(), ALU.add)
    self.act(LW.all(), LW.all(), AF.Sigmoid)
    self.ts(LW.all(), LW.all(), -float(np.exp(-0.5)), ALU.mult, e="pool")
    ab_ = M.lT[0]
    self.cp(ab_[:, 64:128], al)
    pt = M.ptr[M.pti % 2]
    M.pti += 1
    self.tr(pt[:64, 0, :], ab_[:, 64:128], idb.all())
    alT = M.lT[2]
    self.cp(alT[:64, :], pt[:64, 0, :], e="act")
    pw = m_pw(self, M)
    self.mm(pw.all(), alT[:64, :], M.a_up[d].all())
    a = F()
    self.tt(a.all(), pw.all(), M.a0[d].all(), ALU.add)
    self.act(a.all(), a.all(), AF.Sigmoid)
    kd = F()
    self.stt(kd.all(), a.all(), -1.0, M.k_a.all(), ALU.add, ALU.mult)
    self.stt(kd.all(), kd.all(), 1.0, k, ALU.add, ALU.mult)
    bb = F()
    self.tt(bb.all(), a.all(), kk.all(), ALU.mult, e="pool")
    pw = m_pw(self, M)
    self.mm(pw.all(), M.TmS(d).all(), LW.all())
    e1, e1n = F(), F()
    self.act(e1.all(), pw.all(), AF.Exp)
    self.act(e1n.all(), pw.all(), AF.Exp, scale=-1.0)
    pw = m_pw(self, M)
    self.mm(pw.all(), M.TsS(d).all(), LW.all())
    e2 = sq
    self.act(e2.all(), pw.all(), AF.Exp)
    pc = m_slot(self, M)
    for h in range(8):
        self.mm(pc[:64, 0, h * 4:h * 4 + 4], LW[:, h * 64:(h + 1) * 64], M.sel3[d].all())
    self.act(M.cols[:64, :], pc[:64, 0, 0:32], AF.Exp)
    colv = V(M.cols.t[:64, 0:32].rearrange("p (h j) -> p h j", j=4), M.cols.b)
    rt, kt, bt, ct, vb = Hh(), Hh(), Hh(), Hh(), Hh()
    self.tt(rt.all(), r, e1.all(), ALU.mult)
    self.tt(kt.all(), kd.all(), e1n.all(), ALU.mult)
    self.tt(bt.all(), bb.all(), e1n.all(), ALU.mult, e="pool")
    self.tt(ct.all(), kk.all(), e2.all(), ALU.mult, e="pool")
    self.cp(vb.all(), v, e="act")
    rtT, ktT, btT, ctT = M.TP[0], M.TP[1], M.TP[2], M.TP[3]
    for (src, dst) in ((rt, rtT), (kt, ktT), (bt, btT), (ct, ctT)):
        m_trT(self, M, src.all(), 128, 8, 64, dst)
    self.tt(M.SAp.all(), M.SA.all(), V(M.cols.t[:64, :].rearrange("p (h j) -> p h j", j=4)[:, :, 0:1].to_broadcast([64, 8, 64]), M.cols.b), ALU.mult)
    O = F()
    m_core(self, M, 128, 64, 64, 8, True,
           RaT=lambda h: rtT[:64, h, :], RbT=lambda h: rtT[:64, h, :], KaT=lambda h: ktT[:64, h, :],
           BaT=lambda h: btT[:64, h, :], CaT=lambda h: ctT[:64, h, :], CbT=lambda h: ctT[:64, h, :],
           Kc=lambda h: kt[:, h * 64:(h + 1) * 64], Bc=lambda h: bt[:, h * 64:(h + 1) * 64],
           Vv=lambda h: vb[:, h * 64:(h + 1) * 64],
           Mii=lambda h0, n: _bm(M.maskI(d).all(), 128, n), Mei=lambda h0, n: _bm(M.maskS(d).all(), 128, n),
           f1=colv[:, :, 1], f2=colv[:, :, 2],
           S=M.SA.all(), S0p=lambda h: M.SAp[:, h, :],
           O=_r3(O.all(), 8, 64))

    def fin():
        m8 = M.sm[1]
        self.red(m8[:, 0:8], _r3(O.all(), 8, 64))
        self.ts(m8[:, 0:8], m8[:, 0:8], 1.0 / 64, ALU.mult)
        cen = F()
        self.tt(_r3(cen.all(), 8, 64), _r3(O.all(), 8, 64), _b3(m8[:, 0:8], 8, 64), ALU.subtract)
        sq2 = F()
        self.tt(sq2.all(), cen.all(), cen.all(), ALU.mult, e="pool")
        v8 = M.sm[2]
        self.red(v8[:, 0:8], _r3(sq2.all(), 8, 64))
        self.rsqrt(v8[:, 0:8], v8[:, 0:8], 1.0 / 64, GN_EPS)
        self.tt(_r3(cen.all(), 8, 64), _r3(cen.all(), 8, 64), _b3(v8[:, 0:8], 8, 64), ALU.mult)
        self.tt(cen.all(), cen.all(), M.ln_w.all(), ALU.mult)
        self.tt(cen.all(), cen.all(), M.ln_b.all(), ALU.add, e="pool")
        rk = sq2
        self.tt(rk.all(), r, k, ALU.mult)
        self.tt(rk.all(), rk.all(), M.r_k.all(), ALU.mult, e="pool")
        b8 = M.sm[3]
        self.red(b8[:, 0:8], _r3(rk.all(), 8, 64))
        self.tt(_r3(rk.all(), 8, 64), _r3(v, 8, 64), _b3(b8[:, 0:8], 8, 64), ALU.mult)
        self.tt(cen.all(), cen.all(), rk.all(), ALU.add, e="pool")
        sg = M.lT[0]
        self.act(sg.all(), gl, AF.Sigmoid)
        pt_ = M.ptr[M.pti % 2]
        M.pti += 1
        self.tr(pt_[:, 0, :], sg.all(), idb.all())
        sgT = M.lT[1]
        self.cp(sgT.all(), pt_[:, 0, :], e="act")
        pw_ = m_pw(self, M)
        self.mm(pw_.all(), sgT.all(), M.g_up.all())
        ob = Hh()
        self.tt(ob.all(), cen.all(), pw_.all(), ALU.mult)
        return ob.all()
    m_out(self, M, g, d, t0, 128, 0, O.all(), fin)


def m_hgrn(self, M, g, l, d, r0, t0):
    proj = self.scr[g["name"]]["proj"]
    F = lambda: m_F(self, M)
    Hh = lambda: m_H(self, M)
    WB = M.W[0]
    WB2 = M.W[1]
    subs = range(4) if d == 0 else range(3, -1, -1)
    c = 32
    for j in subs:
        rr = r0 + 32 * j
        self.dma(WB[:32, 0:1920], self.U(proj.t.ap()[rr:rr + 32, OFF_B:OFF_B + 1920]))
        self.dma(WB2[:32, 0:640], self.U(proj.t.ap()[rr:rr + 32, OFF_B + 1920:OFF_B + 2560]))
        qx = WB[:32, 0:512]
        fl = WB[:32, 512 + 512 * d:1024 + 512 * d]
        iv = WB[:32, 1536:1920]
        q = F()
        self.act(q[:32, :], qx, AF.Sigmoid)
        self.tt(q[:32, :], q[:32, :], qx, ALU.mult)
        f = F()
        self.act(f[:32, :], fl, AF.Sigmoid)
        self.tt(f[:32, :], f[:32, :], M.oml.all(), ALU.mult, e="pool")
        self.tt(f[:32, :], f[:32, :], M.lb.all(), ALU.add, e="pool")
        LF = F()
        self.act(LF[:32, :], f[:32, :], AF.Ln)
        kin = F()
        self.act(kin[:32, :], fl, AF.Sigmoid, scale=-1.0)
        self.tt(kin[:32, :], kin[:32, :], M.oml.all(), ALU.mult, e="pool")
        pw = m_pw(self, M)
        self.mm(pw[:32, :], M.maskI(d)[:32, :32], LF[:32, :])
        e1, e1n = F(), F()
        self.act(e1[:32, :], pw[:32, :], AF.Exp)
        self.ts(e1n[:32, :], pw[:32, :], -1.0, ALU.mult, 80.0, ALU.min)
        self.act(e1n[:32, :], e1n[:32, :], AF.Exp)
        rt, kt, vb = Hh(), Hh(), Hh()
        self.tt(rt[:32, :], q[:32, :], e1[:32, :], ALU.mult)
        self.tt(kt[:32, :], kin[:32, :], e1n[:32, :], ALU.mult)
        self.cp(vb[:32, 0:384], iv, e="act")
        self.cp(vb[:32, 384:512], WB2[:32, 0:128], e="act")
        pc = m_slot(self, M)
        for h in range(4):
            self.mm(pc[:, 0, 2 * h:2 * h + 2], LF[:32, h * 128:(h + 1) * 128], self.C["ones_f"][:32, 0:2])
        self.act(M.cols[:, 0:8], pc[:, 0, 0:8], AF.Exp)
        etot = V(M.cols.t[:, 0:8].rearrange("p (h j) -> p h j", j=2), M.cols.b)[:, :, 0]
        rtT, ktT = M.TP[4], M.TP[5]
        m_trT(self, M, rt[:32, :], 32, 4, 128, rtT)
        m_trT(self, M, kt[:32, :], 32, 4, 128, ktT)
        self.cp(M.SBp.all(), M.SB.all(), e="pool")
        O = F()
        m_core(self, M, 32, 128, 128, 4, False,
               RaT=lambda h: rtT[:, h, :32], RbT=lambda h: rtT[:, h, :32], KaT=lambda h: ktT[:, h, :32],
               BaT=None, CaT=None, CbT=None,
               Kc=lambda h: kt[:32, h * 128:(h + 1) * 128], Bc=None,
               Vv=lambda h: vb[:32, h * 128:(h + 1) * 128],
               Mii=lambda h0, n: _bm(M.maskI(d)[:32, :32], 32, n), Mei=None,
               f1=etot, f2=etot,
               S=M.SB.all(), S0p=lambda h: M.SBp[:, h, :],
               O=_r3(O[:32, :], 4, 128))
        gb = WB2[:32, 128:640]
        m_out(self, M, g, d, t0 + 32 * j, 32, 512, O[:32, :],
              lambda: m_headnorm_gate(self, M, O[:32, :], 32, gb, M.hnorm))


def m_gdn(self, M, g, l, d, r0, t0):
    proj = self.scr[g["name"]]["proj"]
    F = lambda: m_F(self, M)
    Hh = lambda: m_H(self, M)
    W0, W1, W2 = M.W
    for j, Wt in enumerate(M.W):
        self.dma(Wt[:, 0:1536], self.U(proj.t.ap()[r0 - 1 + j:r0 - 1 + j + 128, OFF_C:OFF_C + 1536]))
    gatec = F()
    self.dma(gatec.all(), self.U(proj.t.ap()[r0:r0 + 128, OFF_C + 1536:OFF_C + 2048]))
    ab = M.sm[0]
    self.dma(ab[:, 0:16], self.U(proj.t.ap()[r0:r0 + 128, OFF_C + 2048:OFF_C + 2064]))
    self.tt(W0[:, 0:1536], W0[:, 0:1536], M.cw[0].all(), ALU.mult, e="pool")
    self.tt(W2[:, 0:1536], W2[:, 0:1536], M.cw[2].all(), ALU.mult)
    self.tt(W1[:, 0:1536], W1[:, 0:1536], M.cw[1].all(), ALU.mult)
    self.tt(W1[:, 0:1536], W1[:, 0:1536], W0[:, 0:1536], ALU.add, e="pool")
    self.tt(W1[:, 0:1536], W1[:, 0:1536], W2[:, 0:1536], ALU.add)
    self.act(W0[:, 0:1536], W1[:, 0:1536], AF.Sigmoid)
    self.tt(W1[:, 0:1536], W1[:, 0:1536], W0[:, 0:1536], ALU.mult)
    q, k, v = W1[:, 0:512], W1[:, 512:1024], W1[:, 1024:1536]
    qn, kn = F(), F()
    sq = F()
    s4 = M.sm[1]
    for (x, xn, scl, c0) in ((q, qn, 128.0 ** -0.5, 0), (k, kn, 1.0, 4)):
        self.tt(sq.all(), x, x, ALU.mult, e="pool")
        self.red(s4[:, c0:c0 + 4], _r3(sq.all(), 4, 128))
        self.rsqrt(s4[:, c0:c0 + 4], s4[:, c0:c0 + 4], 1.0, EPS)
        if scl != 1.0:
            self.ts(s4[:, c0:c0 + 4], s4[:, c0:c0 + 4], scl, ALU.mult)
        self.tt(_r3(xn.all(), 4, 128), _r3(x, 4, 128), _b3(s4[:, c0:c0 + 4], 4, 128), ALU.mult)
    gs = M.sm[2]
    gg, beta, eg = gs[:, 0:4], gs[:, 4:8], gs[:, 8:12]
    self.tt(gg, ab[:, 4 * d:4 * d + 4], M.dtb[:, 4 * d:4 * d + 4], ALU.add)
    self.act(gg, gg, AF.Exp)
    self.act(gg, gg, AF.Ln, bias=1.0)
    self.tt(gg, gg, M.negA[:, 4 * d:4 * d + 4], ALU.mult)
    self.act(beta, ab[:, 8 + 4 * d:12 + 4 * d], AF.Sigmoid)
    self.act(eg, gg, AF.Exp)
    kp, bp = F(), F()
    self.tt(_r3(kp.all(), 4, 128), _r3(kn.all(), 4, 128), _b3(beta, 4, 128), ALU.mult)
    self.tt(_r3(bp.all(), 4, 128), _r3(kp.all(), 4, 128), _b3(eg, 4, 128), ALU.mult)
    pc = m_slot(self, M)
    self.mm(pc[:, 0, 0:4], M.maskI(d).all(), gg)
    self.mm(pc[:, 0, 4:8], M.maskS(d).all(), gg)
    self.mm(pc[:, 0, 8:12], self.C["ones_f"].all(), gg)
    cs = M.sm[3]
    self.cp(cs[:, 0:12], pc[:, 0, 0:12])
    cum, cume, tot = cs[:, 0:4], cs[:, 4:8], cs[:, 8:12]
    ex = M.sm[4]
    ecum, ecume, etc_, f1c = ex[:, 0:4], ex[:, 4:8], ex[:, 8:12], ex[:, 12:16]
    self.act(ex[:, 0:8], cs[:, 0:8], AF.Exp)
    self.tt(etc_, tot, cum, ALU.subtract)
    self.act(etc_, etc_, AF.Exp)
    self.act(f1c, tot, AF.Exp)
    negcum = cs[:, 12:16]
    self.ts(negcum, cum, -1.0, ALU.mult)
    Ra, Rb, Ca, Cb, Ka, Kc, Ba, Bc, vb = [Hh() for _ in range(9)]
    self.cp(Ra.all(), qn.all(), e="act")
    self.tt(_r3(Rb.all(), 4, 128), _r3(qn.all(), 4, 128), _b3(ecum, 4, 128), ALU.mult)
    self.cp(Ca.all(), kn.all(), e="act")
    self.tt(_r3(Cb.all(), 4, 128), _r3(kn.all(), 4, 128), _b3(ecume, 4, 128), ALU.mult)
    self.cp(Ka.all(), kp.all(), e="act")
    self.tt(_r3(Kc.all(), 4, 128), _r3(kp.all(), 4, 128), _b3(etc_, 4, 128), ALU.mult)
    self.cp(Ba.all(), bp.all(), e="act")
    self.tt(_r3(Bc.all(), 4, 128), _r3(bp.all(), 4, 128), _b3(etc_, 4, 128), ALU.mult)
    self.cp(vb.all(), v, e="act")
    for h in range(4):
        Gb = M.Gb[h % 2]
        self.ts(Gb.all(), self.C["ones_f"].all(), gg[:, h:h + 1], ALU.mult)
        for (mask, nmask, dst) in ((M.maskI(d), M.nmI(d), M.Gia[:, h, :]), (M.maskS(d), M.nmS(d), M.Gea[:, h, :])):
            p = m_slot(self, M)
            self.mm(p[:, 0, :], Gb.all(), mask.all())
            self.stt(dst, p[:, 0, :], negcum[:, h:h + 1], nmask.all(), ALU.add, ALU.add)
    self.act(M.Gia.all(), M.Gia.all(), AF.Exp)
    self.act(M.Gea.all(), M.Gea.all(), AF.Exp)
    if True:
        if True:
            pass
    TPs = M.TP
    for (src, dst) in ((Ra, TPs[0]), (Rb, TPs[1]), (Ka, TPs[2]), (Ba, TPs[3]), (Ca, TPs[4]), (Cb, TPs[5])):
        m_trT(self, M, src.all(), 128, 4, 128, dst)
    self.cp(M.SCp.all(), M.SC.all(), e="pool")
    O = F()
    m_core(self, M, 128, 128, 128, 4, True,
           RaT=lambda h: TPs[0][:, h, :], RbT=lambda h: TPs[1][:, h, :], KaT=lambda h: TPs[2][:, h, :],
           BaT=lambda h: TPs[3][:, h, :], CaT=lambda h: TPs[4][:, h, :], CbT=lambda h: TPs[5][:, h, :],
           Kc=lambda h: Kc[:, h * 128:(h + 1) * 128], Bc=lambda h: Bc[:, h * 128:(h + 1) * 128],
           Vv=lambda h: vb[:, h * 128:(h + 1) * 128],
           Mii=lambda h0, n: M.Gia[:, h0:h0 + n, :], Mei=lambda h0, n: M.Gea[:, h0:h0 + n, :],
           f1=f1c, f2=None,
           S=M.SC.all(), S0p=lambda h: M.SCp[:, h, :],
           O=_r3(O.all(), 4, 128))
    m_out(self, M, g, d, t0, 128, 1024, O.all(),
          lambda: m_headnorm_gate(self, M, O.all(), 128, gatec.all(), M.gnorm))


def m_ret(self, M, g, l, d, r0, t0, pos0):
    proj = self.scr[g["name"]]["proj"]
    F = lambda: m_F(self, M)
    Hh = lambda: m_H(self, M)
    WD = M.W[2]
    self.dma(WD[:, 0:1536], self.U(proj.t.ap()[r0:r0 + 128, OFF_D:OFF_D + 1536]))
    q, k, v, gate = WD[:, 0:256], WD[:, 256:512], WD[:, 512:1024], WD[:, 1024:1536]
    if g["rot"]:
        cs_, sn_ = M.rot
        self.dma(cs_.all(), self.U(self.cst["rot_cos"].t.ap()[pos0:pos0 + 128, :]))
        self.dma(sn_.all(), self.U(self.cst["rot_sin"].t.ap()[pos0:pos0 + 128, :]))
        cb_ = V(cs_.t[:, :].unsqueeze(1).to_broadcast([128, 4, 32]), cs_.b)
        sb_ = V(sn_.t[:, :].unsqueeze(1).to_broadcast([128, 4, 32]), sn_.b)
        qk = F()
        t1, t2 = F(), F()
        for (x, c0) in ((q, 0), (k, 256)):
            x3 = _r3(x, 4, 64)
            o3 = _r3(qk[:, c0:c0 + 256], 4, 64)
            a3 = _r3(t1[:, 0:128], 4, 32)
            b3 = _r3(t2[:, 0:128], 4, 32)
            self.tt(a3, x3[:, :, 0:32], cb_, ALU.mult)
            self.tt(b3, x3[:, :, 32:64], sb_, ALU.mult, e="pool")
            self.tt(o3[:, :, 0:32], a3, b3, ALU.subtract)
            a3 = _r3(t1[:, 128:256], 4, 32)
            b3 = _r3(t2[:, 128:256], 4, 32)
            self.tt(a3, x3[:, :, 0:32], sb_, ALU.mult)
            self.tt(b3, x3[:, :, 32:64], cb_, ALU.mult, e="pool")
            self.tt(o3[:, :, 32:64], a3, b3, ALU.add)
        q, k = qk[:, 0:256], qk[:, 256:512]
    Ra, Rb, Ka, Kc, vb = Hh(), Hh(), Hh(), Hh(), Hh()
    self.cp(Ra[:, 0:256], q, e="act")
    self.tt(_r3(Rb[:, 0:256], 4, 64), _r3(q, 4, 64), _b3(M.tabR[d].all(), 4, 64), ALU.mult)
    self.ts(Ka[:, 0:256], k, 0.125, ALU.mult, e="pool")
    self.stt(_r3(Kc[:, 0:256], 4, 64), _r3(k, 4, 64), 0.125, _b3(M.tabK[d].all(), 4, 64), ALU.mult, ALU.mult)
    self.cp(vb.all(), v, e="act")
    RaT, RbT, KaT = M.TP[0], M.TP[1], M.TP[2]
    for (src, dst) in ((Ra, RaT), (Rb, RbT), (Ka, KaT)):
        m_trT(self, M, src[:, 0:256], 128, 4, 64, dst)
    self.cp(M.SDp.all(), M.SD.all(), e="pool")
    O = F()
    m_core(self, M, 128, 64, 128, 4, False,
           RaT=lambda h: RaT[:64, h, :], RbT=lambda h: RbT[:64, h, :], KaT=lambda h: KaT[:64, h, :],
           BaT=None, CaT=None, CbT=None,
           Kc=lambda h: Kc[:, h * 64:(h + 1) * 64], Bc=None,
           Vv=lambda h: vb[:, h * 128:(h + 1) * 128],
           Mii=lambda h0, n: M.GR[d][:, h0:h0 + n, :], Mei=None,
           f1=M.f1r[d][:64, :], f2=None,
           S=M.SD.all(), S0p=lambda h: M.SDp[:, h, :],
           O=_r3(O.all(), 4, 128))
    m_out(self, M, g, d, t0, 128, 1536, O.all(),
          lambda: m_headnorm_gate(self, M, O.all(), 128, gate, M.rnorm))


def phase_m(self, g, l):
    gn = g["name"]
    L = g["L"]
    nt = L // 128
    self.ofb = {}
    which = [m for m in "ABCD" if ("m" + m) in self.dbg] or list("ABCD")
    states = {"A": ("a", "SA", "h k v -> k h v"), "B": ("b", "SB", "h k v -> k h v"),
              "C": ("c", "SC", "h k v -> k h v"), "D": ("d", "SD", "h k v -> k h v")}
    with ExitStack() as es:
        M = m_alloc(self, es, g, l)
        for seq in range(g["nseq"]):
            for d in range(2):
                for m in which:
                    key, tn, pat = states[m]
                    St = getattr(M, tn)
                    if g["init"]:
                        self.dma(St.all(), self.U(self.st_in[key].t.ap()[l, d].rearrange(pat)))
                    else:
                        self.memset(St.all(), 0.0)
                tiles = range(nt) if d == 0 else range(nt - 1, -1, -1)
                for ti in tiles:
                    pos0 = ti * 128
                    r0 = seq * (L + 2) + 1 + pos0
                    t0 = seq * L + pos0
                    sc = (lambda n: self.nc.named_scope(f"{n}_{gn}{l}_{seq}_{d}_{ti}")) if "scopes" in self.dbg else (lambda n: ExitStack())
                    if "A" in which:
                        with sc("mA"):
                            m_rwkv(self, M, g, l, d, r0, t0)
                    if "B" in which:
                        with sc("mB"):
                            m_hgrn(self, M, g, l, d, r0, t0)
                    if "C" in which:
                        with sc("mC"):
                            m_gdn(self, M, g, l, d, r0, t0)
                    if "D" in which:
                        with sc("mD"):
                            m_ret(self, M, g, l, d, r0, t0, pos0)
                if not g["init"]:
                    for m in which:
                        key, tn, pat = states[m]
                        St = getattr(M, tn)
                        self.dma(self.U(self.ns[key].t.ap()[seq, l, d].rearrange(pat)), St.all())
    self.fw.barrier()


Prog.phase_m = phase_m
```

```python
import numpy as np
import ml_dtypes
from contextlib import ExitStack
import concourse.bass as bass
import concourse.mybir as mybir
from concourse.bass_utils import run_bass_kernel_spmd

F32 = mybir.dt.float32
BF16 = mybir.dt.bfloat16
AF = mybir.ActivationFunctionType
ALU = mybir.AluOpType
AX = mybir.AxisListType

D = 2048
IN_COLS = 8080
A_COLS, B_COLS, C_COLS, D_COLS = 1920, 2560, 2064, 1536
OFF_A, OFF_B, OFF_C, OFF_D = 0, 1920, 4480, 6544
D_FF = 5632
EPS = 1e-6
GN_EPS = 64e-5
NEG = -30000.0


class Buf:
    __slots__ = ("name", "w", "r")

    def __init__(self, name=""):
        self.name = name
        self.w = {}
        self.r = {}


class V:
    __slots__ = ("ap", "b")

    def __init__(self, ap, b):
        self.ap = ap
        self.b = b

    def __getitem__(self, idx):
        return V(self.ap[idx], self.b)


class T:
    def __init__(self, t, name=""):
        self.t = t
        self.b = Buf(name)

    def __getitem__(self, idx):
        return V(self.t[idx], self.b)

    def all(self):
        return V(self.t[:], self.b)


class FW:
    EPOCH = 30000

    def __init__(self, nc):
        self.nc = nc
        self.eng = {"pe": nc.tensor, "dve": nc.vector, "act": nc.scalar, "pool": nc.gpsimd, "sp": nc.sync}
        self.sems = {}
        self.cur = {}
        self.epoch = {e: 0 for e in self.eng}
        for e in self.eng:
            self._new_epoch(e)
        self.waited = {e: {} for e in self.eng}
        self.dmaq = {}
        for q, n in (("sp", 14), ("pool", 8)):
            lst = []
            for i in range(n):
                k = f"d_{q}_{i}"
                self.sems[k] = nc.alloc_semaphore(name=k)
                lst.append([k, 0])
            self.dmaq[q] = [lst, 0]
        self.n_inst = {e: 0 for e in self.eng}

    def _new_epoch(self, e):
        k = f"p_{e}_{self.epoch[e]}"
        self.epoch[e] += 1
        self.sems[k] = self.nc.alloc_semaphore(name=k)
        self.cur[e] = [k, 0]

    def _wait(self, e, deps):
        for k, v in deps.items():
            if self.waited[e].get(k, 0) >= v:
                continue
            if k == self.cur[e][0]:
                if e == "pe":
                    continue
                if v > self.cur[e][1]:
                    continue
            self.eng[e].wait_ge(self.sems[k], v)
            self.waited[e][k] = v

    @staticmethod
    def _merge(dst, src):
        for k, v in src.items():
            if dst.get(k, 0) < v:
                dst[k] = v

    def _collect(self, reads, writes):
        deps = {}
        for b in reads:
            self._merge(deps, b.w)
        for b in writes:
            self._merge(deps, b.w)
            self._merge(deps, b.r)
        return deps

    def op(self, e, fn, reads=(), writes=(), inc=True):
        deps = self._collect(reads, writes)
        self._wait(e, deps)
        ins = fn()
        k, c = self.cur[e]
        if inc:
            c += 1
            ins.then_inc(self.sems[k], 1)
            self.cur[e][1] = c
            dep = {k: c}
        else:
            dep = {k: c + 1}
        for b in writes:
            b.w = dict(dep)
            b.r = {}
        for b in reads:
            self._merge(b.r, dep)
        self.n_inst[e] += 1
        if inc and c >= self.EPOCH:
            self._new_epoch(e)
        return ins

    def dma(self, q, out, in_, **kw):
        deps = self._collect([in_.b], [out.b])
        lst, idx = self.dmaq[q]
        slot = lst[idx % len(lst)]
        self.dmaq[q][1] = idx + 1
        k, v = slot
        if v > 0 and deps.get(k, 0) < v:
            deps[k] = v
        self._wait(q, deps)
        ins = self.eng[q].dma_start(out=out.ap, in_=in_.ap, **kw)
        ins.then_inc(self.sems[k], 16)
        slot[1] = v + 16
        dep = {k: v + 16}
        out.b.w = dict(dep)
        out.b.r = {}
        self._merge(in_.b.r, dep)
        self.n_inst[q] += 1

    def barrier(self):
        deps = {}
        for e in self.eng:
            k, c = self.cur[e]
            if c > 0:
                deps[k] = c
        for q in self.dmaq:
            for k, v in self.dmaq[q][0]:
                if v > 0:
                    deps[k] = v
        for e in self.eng:
            self._wait(e, deps)


def make_consts(dec_seq):
    c = 128
    s = np.arange(c)[:, None]
    t = np.arange(c)[None, :]
    out = {}
    out["ident_f"] = np.eye(c, dtype=np.float32)
    out["ident_b"] = np.eye(c, dtype=np.float32).astype(ml_dtypes.bfloat16)
    out["ones_f"] = np.ones((c, c), np.float32)
    cm = np.zeros((8, c, c), np.float32)
    nm = np.zeros((4, c, c), np.float32)
    dm = np.zeros((2, c, c), np.float32)
    sel3 = np.zeros((2, c, 4), np.float32)
    pos = np.zeros((2, c, 2), np.float32)
    for d in range(2):
        incl = (s <= t) if d == 0 else (s >= t)
        strict = (s < t) if d == 0 else (s > t)
        sel = np.zeros(c, np.float32)
        if d == 0:
            sel[: c // 2] = 1
        else:
            sel[c // 2:] = 1
        cm[d * 4 + 0] = incl
        cm[d * 4 + 1] = strict
        cm[d * 4 + 2] = incl - sel[:, None]
        cm[d * 4 + 3] = strict - sel[:, None]
        nm[d * 2 + 0] = np.where(incl, 0.0, NEG)
        nm[d * 2 + 1] = np.where(strict, 0.0, NEG)
        dist = (t - s) if d == 0 else (s - t)
        dm[d] = np.where(incl, dist, 1.0e6)
        sel3[d, :, 0] = sel
        sel3[d, :, 1] = 1.0
        sel3[d, :, 2] = 1.0 - sel
        tt = np.arange(c)
        pos[d, :, 0] = (tt + 1) if d == 0 else (c - tt)
        pos[d, :, 1] = (c - 1 - tt) if d == 0 else tt
    blk = lambda b: (s // b) == (t // b)
    bm = np.zeros((4, c, c), np.float32)
    bm[0] = blk(16)
    for i, b in enumerate((16, 32, 64)):
        bm[i + 1] = blk(2 * b) & ~blk(b)
    out["bmask"] = bm
    out["cmask"] = cm
    out["nmask"] = nm
    out["dmat"] = dm
    out["sel3"] = sel3
    out["pos"] = pos
    rows = dec_seq // 64
    row = np.broadcast_to(np.arange(rows, dtype=np.float32)[:, None], (rows, 64)).reshape(-1)
    col = np.broadcast_to(np.arange(64, dtype=np.float32)[None, :], (rows, 64)).reshape(-1)
    inv = (10000.0 ** (-np.arange(16, dtype=np.float32) / 16)).astype(np.float32)
    ang = np.concatenate([row[:, None] * inv, col[:, None] * inv], axis=-1).astype(np.float32)
    out["rot_cos"] = np.cos(ang).astype(np.float32)
    out["rot_sin"] = np.sin(ang).astype(np.float32)
    return out


class Cfg:
    def __init__(self, n_pseq=4, pseq=256, dseq=4096, depth=2, debug=None):
        self.n_pseq, self.pseq, self.dseq, self.depth, self.debug = n_pseq, pseq, dseq, depth, debug


W_SPECS = [
    ("ada_w", (D, 6 * D)), ("w_in", (D, IN_COLS)), ("w_merge", (D, 4 * D)), ("w_out", (D, D)),
    ("ffn_up", (D, 2 * D_FF)), ("ffn_down", (D_FF, D)),
]
SMALL = [("ada_b", (6 * D,)), ("norm_mix_pre", (D,)), ("norm_mix_post", (D,)), ("norm_ffn_pre", (D,)),
         ("norm_ffn_post", (D,)), ("rwkv_mu", (2, 1920)), ("rwkv_w_up", (2, 64, 512)), ("rwkv_w0", (2, 512)),
         ("rwkv_a_up", (2, 64, 512)), ("rwkv_a0", (2, 512)), ("rwkv_g_up", (128, 512)), ("rwkv_k_k", (512,)),
         ("rwkv_k_a", (512,)), ("rwkv_r_k", (8, 64)), ("rwkv_ln_w", (512,)), ("rwkv_ln_b", (512,)),
         ("hgrn_lb_logits", (512,)), ("hgrn_norm", (512,)), ("gdn_conv_w", (3, 1536)), ("gdn_a_log", (2, 4)),
         ("gdn_dt_bias", (2, 4)), ("gdn_norm", (512,)), ("ret_decay_logit", (2, 4)), ("ret_norm", (512,)),
         ("ffn_conv_w", (3, 2 * D_FF)), ("ffn_conv_b", (2 * D_FF,))]


class KB:
    def __init__(self, cfg):
        self.cfg = cfg
        nc = self.nc = bass.Bass("TRN2", target_bir_lowering=False)
        self.fw = FW(nc)
        self.E = self.fw.eng
        self.dr = {}
        self.uid = 0

    def dram(self, name, shape, dt, kind="Internal"):
        t = self.nc.dram_tensor(name, list(shape), dt, kind=kind)
        d = T(t, name)
        self.dr[name] = d
        return d

    def sb(self, es, shape, dt, name=None):
        self.uid += 1
        name = f"{name or 's'}_{self.uid}"
        return T(es.enter_context(self.nc.sbuf_tensor(name, list(shape), dt)), name)

    def ps(self, es, shape, dt, name=None):
        self.uid += 1
        name = f"{name or 'p'}_{self.uid}"
        return T(es.enter_context(self.nc.psum_tensor(name, list(shape), dt)), name)

    def mm(self, out, lhsT, rhs, start=True, stop=True, inc=True):
        self.fw.op("pe", lambda: self.nc.tensor.matmul(out.ap, lhsT=lhsT.ap, rhs=rhs.ap, start=start, stop=stop),
                   reads=[lhsT.b, rhs.b], writes=[out.b], inc=inc)

    def tr(self, out, in_, ident, inc=True):
        self.fw.op("pe", lambda: self.nc.tensor.transpose(out=out.ap, in_=in_.ap, identity=ident.ap),
                   reads=[in_.b, ident.b], writes=[out.b], inc=inc)

    def act(self, out, in_, func, scale=1.0, bias=0.0, accum=None):
        reads = [in_.b]
        kw = {}
        if isinstance(scale, V):
            reads.append(scale.b)
            kw["scale"] = scale.ap
        elif scale != 1.0:
            kw["scale"] = scale
        if isinstance(bias, V):
            reads.append(bias.b)
            kw["bias"] = bias.ap
        elif bias != 0.0:
            kw["bias"] = bias
        writes = [out.b]
        if accum is not None:
            kw["accum_out"] = accum.ap
            writes.append(accum.b)
        self.fw.op("act", lambda: self.nc.scalar.activation(out=out.ap, in_=in_.ap, func=func, **kw),
                   reads=reads, writes=writes)

    def _e(self, e):
        return {"dve": self.nc.vector, "pool": self.nc.gpsimd}[e]

    def tt(self, out, a, b, op, e="dve"):
        self.fw.op(e, lambda: self._e(e).tensor_tensor(out=out.ap, in0=a.ap, in1=b.ap, op=op),
                   reads=[a.b, b.b], writes=[out.b])

    def ts(self, out, a, s1, op0, s2=None, op1=None, e="dve", accum=None):
        reads = [a.b]
        kw = {}
        if isinstance(s1, V):
            reads.append(s1.b)
            s1 = s1.ap
        if isinstance(s2, V):
            reads.append(s2.b)
            s2 = s2.ap
        if op1 is not None:
            kw["op1"] = op1
        writes = [out.b]
        if accum is not None:
            kw["accum_out"] = accum.ap
            writes.append(accum.b)
        self.fw.op(e, lambda: self._e(e).tensor_scalar(out=out.ap, in0=a.ap, scalar1=s1, scalar2=s2, op0=op0, **kw),
                   reads=reads, writes=writes)

    def stt(self, out, a, s, b, op0, op1):
        reads = [a.b, b.b]
        if isinstance(s, V):
            reads.append(s.b)
            s = s.ap
        self.fw.op("dve", lambda: self.nc.vector.scalar_tensor_tensor(out=out.ap, in0=a.ap, scalar=s, in1=b.ap,
                                                                       op0=op0, op1=op1),
                   reads=reads, writes=[out.b])

    def cp(self, out, in_, e="dve"):
        if e == "act":
            self.fw.op("act", lambda: self.nc.scalar.copy(out=out.ap, in_=in_.ap), reads=[in_.b], writes=[out.b])
        else:
            self.fw.op(e, lambda: self._e(e).tensor_copy(out=out.ap, in_=in_.ap), reads=[in_.b], writes=[out.b])

    def red(self, out, in_, op=ALU.add, e="dve"):
        self.fw.op(e, lambda: self._e(e).tensor_reduce(out=out.ap, in_=in_.ap, op=op, axis=AX.X),
                   reads=[in_.b], writes=[out.b])

    def recip(self, out, in_):
        self.fw.op("dve", lambda: self.nc.vector.reciprocal(out=out.ap, in_=in_.ap), reads=[in_.b], writes=[out.b])

    def memset(self, out, val, e="dve"):
        self.fw.op(e, lambda: self._e(e).memset(out.ap, val), writes=[out.b])

    def dma(self, out, in_, q="sp", **kw):
        self.fw.dma(q, out, in_, **kw)

    def U(self, ap):
        return V(ap, Buf())

    def rsqrt(self, out, in_, scale, eps):
        self.act(out, in_, AF.Sqrt, scale=scale, bias=eps)
        self.recip(out, out)


class Prog(KB):
    def __init__(self, cfg):
        super().__init__(cfg)
        nc = self.nc
        dbg = cfg.debug or set()
        self.dbg = dbg
        NP, LP, LS, DEP = cfg.n_pseq, cfg.pseq, cfg.dseq, cfg.depth
        self.groups = [dict(name="p", nseq=NP, L=LP, rot=False, init=False),
                       dict(name="s", nseq=1, L=LS, rot=True, init=True)]
        if "only_p" in dbg:
            self.groups = self.groups[:1]
        if "only_s" in dbg:
            self.groups = self.groups[1:]
        ext_in = lambda n, s, dt=F32: self.dram(n, s, dt, kind="ExternalInput")
        ext_out = lambda n, s, dt=F32: self.dram(n, s, dt, kind="ExternalOutput")
        self.x_in = {"p": ext_in("x_p", [NP * LP, D]), "s": ext_in("x_s", [LS, D])}
        self.st_in = {"a": ext_in("st_a", [DEP, 2, 8, 64, 64]), "b": ext_in("st_b", [DEP, 2, 4, 128, 128]),
                      "c": ext_in("st_c", [DEP, 2, 4, 128, 128]), "d": ext_in("st_d", [DEP, 2, 4, 64, 128])}
        self.cvec = ext_in("cvec", [2, D])
        self.w = {}
        for n, s in W_SPECS:
            self.w[n] = ext_in(n, [DEP] + list(s))
        self.w["w_branch"] = ext_in("w_branch", [DEP, 4, 512, D])
        for n, s in SMALL:
            self.w[n] = ext_in(n, [DEP] + list(s))
        self.cst = {}
        for n, a in make_consts(LS).items():
            self.cst[n] = ext_in("c_" + n, list(a.shape), BF16 if a.dtype == ml_dtypes.bfloat16 else F32)
        self.y = {"p": ext_out("y_p", [NP * LP, D]), "s": ext_out("y_s", [LS, D])}
        self.ns = {"a": ext_out("ns_a", [NP, DEP, 2, 8, 64, 64]), "b": ext_out("ns_b", [NP, DEP, 2, 4, 128, 128]),
                   "c": ext_out("ns_c", [NP, DEP, 2, 4, 128, 128]), "d": ext_out("ns_d", [NP, DEP, 2, 4, 64, 128])}
        self.wb = {}
        for n, s in W_SPECS:
            self.wb[n] = self.dram("wb_" + n, [DEP] + list(s), BF16)
        self.wb["w_branch"] = self.dram("wb_w_branch", [DEP, 4, 512, D], BF16)
        kind = lambda f: "ExternalOutput" if f in dbg else "Internal"
        self.scr = {}
        for g in self.groups:
            gn, nt = g["name"], g["nseq"] * g["L"]
            self.scr[gn] = dict(
                proj=self.dram(f"proj_{gn}", [g["nseq"] * (g["L"] + 2), IN_COLS], F32, kind=kind("proj")),
                of=self.dram(f"of_{gn}", [nt, D], F32, kind=kind("of")),
                mix=self.dram(f"mix_{gn}", [nt, D], BF16, kind="ExternalInput" if "mix_in" in dbg else kind("mix")),
                xmid=self.dram(f"xmid_{gn}", [nt, D], F32, kind=kind("xmid")),
                x1=self.dram(f"x1_{gn}", [nt, D], F32, kind=kind("x1")),
                mod=self.dram(f"mod_{gn}", [DEP, 128, 6 * D], F32, kind=kind("mod")),
            )
        self.build()

    def cast_weights(self):
        for n in self.wb:
            src, dst = self.w[n], self.wb[n]
            sh = src.t.shape
            if len(sh) == 4:
                s2 = src.t.ap().rearrange("l n r c -> (l n r) c")
                d2 = dst.t.ap().rearrange("l n r c -> (l n r) c")
            else:
                s2 = src.t.ap().rearrange("l r c -> (l r) c")
                d2 = dst.t.ap().rearrange("l r c -> (l r) c")
            rows = s2.shape[0]
            step = 256
            for r0 in range(0, rows, step):
                r1 = min(rows, r0 + step)
                self.dma(V(d2[r0:r1, :], dst.b), V(s2[r0:r1, :], src.b), q="pool")

    def load_consts(self, es):
        c = {}
        for n in ("ident_f", "ident_b", "ones_f"):
            t = self.sb(es, [128, 128], BF16 if n == "ident_b" else F32, n)
            self.dma(t.all(), V(self.cst[n].t.ap(), self.cst[n].b))
            c[n] = t
        self.C = c

    def bc_load(self, dst, src_ap, src_b, P=128):
        self.dma(dst, V(src_ap.partition_broadcast(P), src_b))

    def setup_mods(self):
        DEP = self.cfg.depth
        with ExitStack() as es:
            cv = self.sb(es, [128, 16], F32, "cv")
            sg = self.sb(es, [128, 16], F32, "sg")
            scb = self.sb(es, [128, 16, 128], BF16, "scb")
            modt = self.sb(es, [128, 6 * D], F32, "modt")
            gt = self.sb(es, [128, D], F32, "gt")
            bt = [self.sb(es, [128, 512], F32, "bt") for _ in range(2)]
            wbuf = [self.sb(es, [128, 16, 512], BF16, "wb") for _ in range(2)]
            pp = [self.ps(es, [128, 512], F32, "pm") for _ in range(2)]
            for g in self.groups:
                gi = 0 if g["name"] == "p" else 1
                with self.nc.allow_non_contiguous_dma(reason="tiny feature-major load of conditioning vector"):
                    self.dma(cv.all(), V(self.cvec.t.ap()[gi, :].rearrange("(k p) -> p k", p=128), self.cvec.b))
                self.act(sg.all(), cv.all(), AF.Sigmoid)
                self.tt(sg.all(), sg.all(), cv.all(), ALU.mult)
                self.cp(scb.all(), V(sg.t[:, :].unsqueeze(2).to_broadcast([128, 16, 128]), sg.b))
                for l in range(DEP):
                    for j in range(24):
                        wt = wbuf[j % 2]
                        srcw = self.wb["ada_w"].t.ap()[l, :, j * 512:(j + 1) * 512].rearrange("(k p) c -> p k c", p=128)
                        merged = {}
                        for pi, k0 in enumerate(range(0, 16, 4)):
                            ov = V(wt.t[:, k0:k0 + 4, :], wt.b if pi == 0 else Buf())
                            self.dma(ov, V(srcw[:, k0:k0 + 4, :], Buf()))
                            FW._merge(merged, ov.b.w)
                        wt.b.w = merged
                        wt.b.r = {}
                        self.bc_load(bt[j % 2].all(), self.w["ada_b"].t.ap()[l, j * 512:(j + 1) * 512], self.w["ada_b"].b)
                        p = pp[j % 2]
                        for k in range(16):
                            self.mm(p.all(), scb[:, k, :], wt[:, k, :], start=(k == 0), stop=(k == 15), inc=(k == 15))
                        self.tt(modt[:, j * 512:(j + 1) * 512], p.all(), bt[j % 2].all(), ALU.add)
                    for (ch, gname) in ((1, "norm_mix_pre"), (4, "norm_ffn_pre")):
                        self.bc_load(gt.all(), self.w[gname].t.ap()[l, :], self.w[gname].b)
                        sl = modt[:, ch * D:(ch + 1) * D]
                        self.stt(sl, sl, 1.0, gt.all(), ALU.add, ALU.mult)
                    for (ch, gname) in ((2, "norm_mix_post"), (5, "norm_ffn_post")):
                        self.bc_load(gt.all(), self.w[gname].t.ap()[l, :], self.w[gname].b)
                        sl = modt[:, ch * D:(ch + 1) * D]
                        self.tt(sl, sl, gt.all(), ALU.mult)
                    md = self.scr[g["name"]]["mod"]
                    self.dma(self.U(md.t.ap()[l]), modt.all())
        self.fw.barrier()

    def norm_rows(self, x, G, sh, u_out, P, tmp, ss, rstd):
        self.act(tmp[:P, :], x, AF.Square, accum=ss[:P, :])
        self.rsqrt(rstd[:P, :], ss[:P, :], 1.0 / D, EPS)
        self.stt(tmp[:P, :], x, rstd[:P, :], G[:P, :], ALU.mult, ALU.mult)
        self.tt(u_out, tmp[:P, :], sh[:P, :], ALU.add, e="pool")

    def to_featT(self, u, P, dst, c0, ptr, KC=16):
        idb = self.C["ident_b"]
        for k0 in range(0, KC, 8):
            pt = ptr[(self.trc) % len(ptr)]
            self.trc += 1
            n = min(8, KC - k0)
            for k in range(n):
                self.tr(pt[:, k, :P], u[:, (k0 + k) * 128:(k0 + k + 1) * 128], idb[:P, :P], inc=(k == n - 1))
            self.cp(dst[:, k0:k0 + n, c0:c0 + P], pt[:, 0:n, :P], e="act")

    def load_w(self, wbufs, Wd, lsel, c0, cw, KC):
        wt = wbufs[self.wrr % len(wbufs)]
        self.wrr += 1
        src = Wd.t.ap()[lsel][:, c0:c0 + cw].rearrange("(k p) c -> p k c", p=128)
        view = V(wt.t[:, 0:KC * cw].rearrange("p (k c) -> p k c", c=cw), wt.b)
        merged = {}
        for pi, k0 in enumerate(range(0, KC, 4)):
            k1 = min(KC, k0 + 4)
            ov = V(view.ap[:, k0:k1, :], wt.b if pi == 0 else Buf())
            self.dma(ov, V(src[:, k0:k1, :], Buf()))
            FW._merge(merged, ov.b.w)
        wt.b.w = merged
        wt.b.r = {}
        return view

    def tile_rows(self, g, ti):
        L = g["L"]
        tps = L // 128
        return ti // tps, (ti % tps) * 128

    def phase_a(self, g, l):
        gn = g["name"]
        nt = g["nseq"] * g["L"]
        xsrc = self.x_in[gn] if l == 0 else self.scr[gn]["x1"]
        proj = self.scr[gn]["proj"]
        mod = self.scr[gn]["mod"]
        TB = min(512, nt)
        nsub = TB // 128
        self.trc = 0
        self.wrr = 0
        with ExitStack() as es:
            G1 = self.sb(es, [128, D], F32, "G1")
            sh1 = self.sb(es, [128, D], F32, "sh1")
            self.dma(sh1.all(), V(mod.t.ap()[l, :, 0:D], mod.b))
            self.dma(G1.all(), V(mod.t.ap()[l, :, D:2 * D], mod.b))
            xt = [self.sb(es, [128, D], F32, "xt") for _ in range(2)]
            tmp = self.sb(es, [128, D], F32, "tmp")
            ss = self.sb(es, [128, 1], F32, "ss")
            rstd = self.sb(es, [128, 1], F32, "rstd")
            ub = [self.sb(es, [128, D], BF16, "ub") for _ in range(2)]
            uT = self.sb(es, [128, 16, TB], BF16, "uT")
            wbufs = [self.sb(es, [128, 8192], BF16, "wbuf") for _ in range(4)]
            stg = [self.sb(es, [128, 512], F32, "stg") for _ in range(4)]
            zrow = self.sb(es, [2, IN_COLS], F32, "zrow")
            ptr = [self.ps(es, [128, 8, 128], BF16, "ptr") for _ in range(2)]
            pbig = [self.ps(es, [128, 512], F32, "pbig") for _ in range(4)]
            self.memset(zrow.all(), 0.0)
            for s in range(g["nseq"]):
                r0 = s * (g["L"] + 2)
                for r in (r0, r0 + g["L"] + 1):
                    self.dma(self.U(proj.t.ap()[r:r + 1, :]), zrow[0:1, :])
            k_ = 0
            for b0 in range(0, nt, TB):
                for sub in range(nsub):
                    t0 = b0 + sub * 128
                    x = xt[sub % 2]
                    self.dma(x.all(), V(xsrc.t.ap()[t0:t0 + 128, :], xsrc.b))
                    u = ub[sub % 2]
                    self.norm_rows(x.all(), G1, sh1, u.all(), 128, tmp, ss, rstd)
                    self.to_featT(u.all(), 128, uT, sub * 128, ptr)
                for c0 in range(0, IN_COLS, 512):
                    cw = min(512, IN_COLS - c0)
                    wv = self.load_w(wbufs, self.wb["w_in"], l, c0, cw, 16)
                    for sub in range(nsub):
                        p = pbig[k_ % 4]
                        st = stg[k_ % 4]
                        k_ += 1
                        for k in range(16):
                            self.mm(p[:, 0:cw], uT[:, k, sub * 128:(sub + 1) * 128], wv[:, k, :],
                                    start=(k == 0), stop=(k == 15), inc=(k == 15))
                        self.cp(st[:, 0:cw], p[:, 0:cw], e=("act" if k_ % 2 else "dve"))
                        seq, pos = self.tile_rows(g, (b0 + sub * 128) // 128)
                        r0 = seq * (g["L"] + 2) + 1 + pos
                        self.dma(self.U(proj.t.ap()[r0:r0 + 128, c0:c0 + cw]), st[:, 0:cw])
        self.fw.barrier()

    def phase_c1(self, g, l):
        gn = g["name"]
        nt = g["nseq"] * g["L"]
        xsrc = self.x_in[gn] if l == 0 else self.scr[gn]["x1"]
        mix, xmid, mod = self.scr[gn]["mix"], self.scr[gn]["xmid"], self.scr[gn]["mod"]
        TB = min(256, nt)
        nsub = TB // 128
        self.trc = 0
        self.wrr = 0
        with ExitStack() as es:
            sh1 = self.sb(es, [128, D], F32, "sh1")
            G1 = self.sb(es, [128, D], F32, "G1")
            GP1 = self.sb(es, [128, D], F32, "GP1")
            self.dma(sh1.all(), self.U(mod.t.ap()[l, :, 0:D]))
            self.dma(G1.all(), self.U(mod.t.ap()[l, :, D:2 * D]))
            self.dma(GP1.all(), self.U(mod.t.ap()[l, :, 2 * D:3 * D]))
            xres = [self.sb(es, [128, D], F32, "xres") for _ in range(nsub)]
            ytmp = [self.sb(es, [128, D], F32, "ytmp") for _ in range(nsub)]
            tmp = self.sb(es, [128, D], F32, "tmp")
            ss = self.sb(es, [128, 1], F32, "ss")
            rstd = self.sb(es, [128, 1], F32, "rstd")
            ub = [self.sb(es, [128, D], BF16, "ub") for _ in range(2)]
            uT = self.sb(es, [128, 16, TB], BF16, "uT")
            mixT = self.sb(es, [128, 16, TB], BF16, "mixT")
            hT = self.sb(es, [128, 16, TB], F32, "hT")
            hTb = self.sb(es, [128, 16, TB], BF16, "hTb")
            gsb = [self.sb(es, [128, TB], F32, "gsb") for _ in range(2)]
            tsb = [self.sb(es, [128, TB], F32, "tsb") for _ in range(2)]
            wbufs = [self.sb(es, [128, 8192], BF16, "wbuf") for _ in range(2)]
            wbr = [self.sb(es, [128, 8192], BF16, "wbr") for _ in range(2)]
            ptr = [self.ps(es, [128, 8, 128], BF16, "ptr") for _ in range(2)]
            pbig = [self.ps(es, [128, 512], F32, "pbig") for _ in range(6)]
            k_ = 0
            for b0 in range(0, nt, TB):
                for sub in range(nsub):
                    t0 = b0 + sub * 128
                    self.dma(xres[sub].all(), self.U(xsrc.t.ap()[t0:t0 + 128, :]))
                    u = ub[sub % 2]
                    self.norm_rows(xres[sub].all(), G1, sh1, u.all(), 128, tmp, ss, rstd)
                    self.to_featT(u.all(), 128, uT, sub * 128, ptr)
                for sub in range(nsub):
                    t0 = b0 + sub * 128
                    u = ub[sub % 2]
                    self.dma(u.all(), self.U(mix.t.ap()[t0:t0 + 128, :]))
                    self.to_featT(u.all(), 128, mixT, sub * 128, ptr)
                for n in range(4):
                    wb_t = wbr[n % 2]
                    wbv = V(wb_t.t[:, :].rearrange("p (j c) -> p j c", c=D), wb_t.b)
                    self.dma(wbv, self.U(self.wb["w_branch"].t.ap()[l, n].rearrange("(j p) c -> p j c", p=128)))
                    for cg in range(4):
                        wv = self.load_w(wbufs, self.wb["w_merge"], l, n * D + cg * 512, 512, 16)
                        for f4 in range(4):
                            fo = cg * 4 + f4
                            p1 = pbig[k_ % 6]
                            p2 = pbig[(k_ + 1) % 6]
                            gs = gsb[(k_ // 2) % 2]
                            tsq = tsb[(k_ // 2) % 2]
                            k_ += 2
                            for k in range(16):
                                self.mm(p1[:, 0:TB], wv[:, k, f4 * 128:(f4 + 1) * 128], uT[:, k, :],
                                        start=(k == 0), stop=(k == 15), inc=(k == 15))
                            for j in range(4):
                                self.mm(p2[:, 0:TB], wbv[:, j, fo * 128:(fo + 1) * 128], mixT[:, n * 4 + j, :],
                                        start=(j == 0), stop=(j == 3), inc=(j == 3))
                            self.act(gs.all(), p1[:, 0:TB], AF.Sigmoid)
                            if n == 0:
                                self.tt(hT[:, fo, :], gs.all(), p2[:, 0:TB], ALU.mult)
                            else:
                                self.tt(tsq.all(), gs.all(), p2[:, 0:TB], ALU.mult)
                                self.tt(hT[:, fo, :], hT[:, fo, :], tsq.all(), ALU.add, e="pool")
                for q4 in range(4):
                    self.cp(hTb[:, q4 * 4:(q4 + 1) * 4, :], hT[:, q4 * 4:(q4 + 1) * 4, :], e=("act" if q4 % 2 else "dve"))
                for cg in range(4):
                    wv = self.load_w(wbufs, self.wb["w_out"], l, cg * 512, 512, 16)
                    for sub in range(nsub):
                        p = pbig[k_ % 6]
                        k_ += 1
                        for k in range(16):
                            self.mm(p.all(), hTb[:, k, sub * 128:(sub + 1) * 128], wv[:, k, :],
                                    start=(k == 0), stop=(k == 15), inc=(k == 15))
                        self.cp(ytmp[sub][:, cg * 512:(cg + 1) * 512], p.all(), e=("act" if k_ % 2 else "dve"))
                for sub in range(nsub):
                    t0 = b0 + sub * 128
                    self.residual(xres[sub], ytmp[sub], GP1, tmp, ss, rstd)
                    self.dma(self.U(xmid.t.ap()[t0:t0 + 128, :]), xres[sub].all())
        self.fw.barrier()

    def residual(self, xres, y, GP, tmp, ss, rstd):
        self.act(tmp.all(), y.all(), AF.Square, accum=ss.all())
        self.rsqrt(rstd.all(), ss.all(), 1.0 / D, EPS)
        self.stt(tmp.all(), y.all(), rstd.all(), GP.all(), ALU.mult, ALU.mult)
        self.tt(xres.all(), xres.all(), tmp.all(), ALU.add, e="pool")

    def phase_c2(self, g, l, last):
        gn = g["name"]
        L = g["L"]
        xmid, mod = self.scr[gn]["xmid"], self.scr[gn]["mod"]
        dst = self.y[gn] if last else self.scr[gn]["x1"]
        TBL = min(256, L)
        nsub = TBL // 128
        NH = 1 if TBL + 2 <= 512 else 2
        HW = (TBL + 2) // NH
        self.trc = 0
        self.wrr = 0
        with ExitStack() as es:
            sh2 = self.sb(es, [128, D], F32, "sh2")
            G2 = self.sb(es, [128, D], F32, "G2")
            GP2 = self.sb(es, [128, D], F32, "GP2")
            self.dma(sh2.all(), self.U(mod.t.ap()[l, :, 3 * D:4 * D]))
            self.dma(G2.all(), self.U(mod.t.ap()[l, :, 4 * D:5 * D]))
            self.dma(GP2.all(), self.U(mod.t.ap()[l, :, 5 * D:6 * D]))
            cw = self.sb(es, [128, 3, 88], F32, "cw")
            cb = self.sb(es, [128, 88], F32, "cb")
            with self.nc.allow_non_contiguous_dma(reason="one-off feature-major load of depthwise conv params"):
                for j in range(3):
                    self.dma(cw[:, j, :], self.U(self.w["ffn_conv_w"].t.ap()[l, j, :].rearrange("(k p) -> p k", p=128)))
                self.dma(cb.all(), self.U(self.w["ffn_conv_b"].t.ap()[l, :].rearrange("(k p) -> p k", p=128)))
            xres = [self.sb(es, [128, D], F32, "xres") for _ in range(nsub)]
            ytmp = [self.sb(es, [128, D], F32, "ytmp") for _ in range(nsub)]
            tmp = self.sb(es, [128, D], F32, "tmp")
            ss = self.sb(es, [128, 1], F32, "ss")
            rstd = self.sb(es, [128, 1], F32, "rstd")
            ub = [self.sb(es, [128, D], BF16, "ub") for _ in range(2)]
            hl = self.sb(es, [2, D], F32, "hl")
            hu = self.sb(es, [2, D], BF16, "hu")
            hT2 = self.sb(es, [128, 16, 2], BF16, "hT2")
            u2T = self.sb(es, [128, 16, TBL + 2], BF16, "u2T")
            aT = self.sb(es, [128, 44, TBL], BF16, "aT")
            hsb = [[self.sb(es, [128, TBL + 2], F32, "hsb") for _ in range(2)] for _ in range(2)]
            cv = [[self.sb(es, [128, TBL], F32, "cv") for _ in range(2)] for _ in range(2)]
            g1 = [self.sb(es, [128, TBL], F32, "g1") for _ in range(2)]
            g2 = [self.sb(es, [128, TBL], F32, "g2") for _ in range(2)]
            wbufs = [self.sb(es, [128, 11264], BF16, "wbuf") for _ in range(3)]
            ptr = [self.ps(es, [128, 8, 128], BF16, "ptr") for _ in range(2)]
            pbig = [self.ps(es, [128, 512], F32, "pbig") for _ in range(6)]
            k_ = 0
            it = 0
            for seq in range(g["nseq"]):
                for p0 in range(0, L, TBL):
                    tb0 = seq * L + p0
                    for sub in range(nsub):
                        t0 = tb0 + sub * 128
                        self.dma(xres[sub].all(), self.U(xmid.t.ap()[t0:t0 + 128, :]))
                        u = ub[sub % 2]
                        self.norm_rows(xres[sub].all(), G2, sh2, u.all(), 128, tmp, ss, rstd)
                        self.to_featT(u.all(), 128, u2T, 1 + sub * 128, ptr)
                    has_l, has_r = p0 > 0, p0 + TBL < L
                    self.memset(hl.all(), 0.0)
                    if has_l:
                        self.dma(hl[0:1, :], self.U(xmid.t.ap()[tb0 - 1:tb0, :]))
                    if has_r:
                        self.dma(hl[1:2, :], self.U(xmid.t.ap()[tb0 + TBL:tb0 + TBL + 1, :]))
                    self.norm_rows(hl.all(), G2, sh2, hu.all(), 2, tmp, ss, rstd)
                    self.to_featT(hu.all(), 2, hT2, 0, ptr)
                    for (c_src, c_dst, ok) in ((0, 0, has_l), (1, TBL + 1, has_r)):
                        if ok:
                            self.cp(u2T[:, :, c_dst:c_dst + 1], hT2[:, :, c_src:c_src + 1])
                        else:
                            self.memset(u2T[:, :, c_dst:c_dst + 1], 0.0)
                    for cgp in range(11):
                        wvs = [self.load_w(wbufs, self.wb["ffn_up"], l, cgp * 512, 512, 16),
                               self.load_w(wbufs, self.wb["ffn_up"], l, D_FF + cgp * 512, 512, 16)]
                        for f4 in range(4):
                            fo = cgp * 4 + f4
                            par = it % 2
                            it += 1
                            for wi in range(2):
                                ch = fo + 44 * wi
                                hs = hsb[wi][par]
                                for half in range(NH):
                                    p = pbig[k_ % 6]
                                    k_ += 1
                                    for k in range(16):
                                        self.mm(p[:, 0:HW], wvs[wi][:, k, f4 * 128:(f4 + 1) * 128],
                                                u2T[:, k, half * HW:(half + 1) * HW],
                                                start=(k == 0), stop=(k == 15), inc=(k == 15))
                                    self.cp(hs[:, half * HW:(half + 1) * HW], p[:, 0:HW], e="act")
                                c = cv[wi][par]
                                self.ts(c.all(), hs[:, 0:TBL], cw[:, 0, ch:ch + 1], ALU.mult, cb[:, ch:ch + 1], ALU.add)
                                self.stt(c.all(), hs[:, 1:TBL + 1], cw[:, 1, ch:ch + 1], c.all(), ALU.mult, ALU.add)
                                self.stt(c.all(), hs[:, 2:TBL + 2], cw[:, 2, ch:ch + 1], c.all(), ALU.mult, ALU.add)
                            val, gt = cv[0][par], cv[1][par]
                            a1, a2 = g1[par], g2[par]
                            self.act(a1.all(), gt.all(), AF.Square)
                            self.act(a1.all(), a1.all(), AF.Copy, scale=0.044715, bias=1.0)
                            self.tt(a1.all(), a1.all(), gt.all(), ALU.mult, e="pool")
                            self.act(a2.all(), a1.all(), AF.Sigmoid, scale=1.5957691216057308)
                            self.tt(a2.all(), a2.all(), gt.all(), ALU.mult, e="pool")
                            self.tt(aT[:, fo, :], a2.all(), val.all(), ALU.mult, e="pool")
                    for cg in range(8):
                        wv = self.load_w(wbufs, self.wb["ffn_down"], l, cg * 256, 256, 44)
                        for sub in range(nsub):
                            p = pbig[k_ % 6]
                            k_ += 1
                            for k in range(44):
                                self.mm(p[:, 0:256], aT[:, k, sub * 128:(sub + 1) * 128], wv[:, k, :],
                                        start=(k == 0), stop=(k == 43), inc=(k == 43))
                            self.cp(ytmp[sub][:, cg * 256:(cg + 1) * 256], p[:, 0:256], e=("act" if k_ % 2 else "dve"))
                    for sub in range(nsub):
                        t0 = tb0 + sub * 128
                        self.residual(xres[sub], ytmp[sub], GP2, tmp, ss, rstd)
                        self.dma(self.U(dst.t.ap()[t0:t0 + 128, :]), xres[sub].all())
        self.fw.barrier()

    def build(self):
        dbg = self.dbg
        with ExitStack() as es:
            sc = (lambda n: self.nc.named_scope(n)) if "scopes" in dbg else (lambda n: ExitStack())
            self.load_consts(es)
            with sc("cast"):
                self.cast_weights()
                self.fw.barrier()
            with sc("mods"):
                self.setup_mods()
            for g in self.groups:
                for l in range(self.cfg.depth):
                    with sc(f"A_{g['name']}{l}"):
                        self.phase_a(g, l)
                    if "mix_in" not in dbg:
                        with sc(f"M_{g['name']}{l}"):
                            self.phase_m(g, l)
                    with sc(f"C1_{g['name']}{l}"):
                        self.phase_c1(g, l)
                    with sc(f"C2_{g['name']}{l}"):
                        self.phase_c2(g, l, last=(l == self.cfg.depth - 1))
            self.fw.barrier()


_PROG_CACHE = {}


def _prog(cfg_key):
    if cfg_key not in _PROG_CACHE:
        _PROG_CACHE[cfg_key] = Prog(Cfg(*cfg_key))
    return _PROG_CACHE[cfg_key]


def run(inputs, n_cores=8, debug=None, extra_in=None):
    x_prompt, x_sample = inputs["x_prompt"], inputs["x_sample"]
    B, LP = x_prompt.shape[0], x_prompt.shape[1]
    DB, LS = x_sample.shape[0], x_sample.shape[1]
    depth = inputs["w_in"].shape[0]
    NP = B // n_cores
    prog = _prog((NP, LP, LS, depth, frozenset(debug) if debug else None))
    consts = make_consts(LS)
    f32 = lambda a: np.ascontiguousarray(np.asarray(a, dtype=np.float32))
    shared = {}
    for n, _ in W_SPECS:
        shared[n] = f32(inputs[n])
    shared["w_branch"] = f32(inputs["w_branch"])
    for n, _ in SMALL:
        shared[n] = f32(inputs[n])
    for n, a in consts.items():
        shared["c_" + n] = a
    in_maps = []
    for c in range(n_cores):
        b = c % DB
        m = dict(shared)
        m["x_p"] = f32(x_prompt[c * NP:(c + 1) * NP]).reshape(NP * LP, D)
        m["x_s"] = f32(x_sample[b])
        m["st_a"] = f32(inputs["state_rwkv"][b])
        m["st_b"] = f32(inputs["state_hgrn"][b])
        m["st_c"] = f32(inputs["state_gdn"][b])
        m["st_d"] = f32(inputs["state_ret"][b])
        m["cvec"] = np.stack([f32(inputs["c_ctx"]), f32(inputs["c"][b])], 0)
        if extra_in:
            for k, v in extra_in(c).items():
                m[k] = v
        in_maps.append(m)
    res = run_bass_kernel_spmd(prog.nc, in_maps, core_ids=list(range(n_cores)))
    return res.results


def kernel(**inputs):
    r = run(inputs, 8)
    B, LP = inputs["x_prompt"].shape[:2]
    DB, LS = inputs["x_sample"].shape[:2]
    NP = B // 8
    y_p = np.concatenate([r[c]["y_p"].reshape(NP, LP, D) for c in range(8)], 0)
    y_s = np.stack([r[b]["y_s"] for b in range(DB)], 0)
    outs = [y_p.astype(np.float32), y_s.astype(np.float32)]
    for k in "abcd":
        outs.append(np.concatenate([r[c]["ns_" + k] for c in range(8)], 0).astype(np.float32))
    return tuple(outs)


class _NS:
    pass


def _b3(v, n, w):
    return V(v.ap.unsqueeze(2).to_broadcast([v.ap.shape[0], n, w]), v.b)


def _r3(v, n, w):
    return V(v.ap.rearrange("p (n w) -> p n w", w=w), v.b)


def m_alloc(self, es, g, l):
    M = _NS()
    nc = self.nc
    M.es = es
    sb = lambda shape, dt, n: self.sb(es, shape, dt, n)
    M.cm = [sb([128, 128], F32, "cm") for _ in range(8)]
    for i in range(8):
        self.dma(M.cm[i].all(), self.U(self.cst["cmask"].t.ap()[i]))
    M.nm = [sb([128, 128], F32, "nm") for _ in range(4)]
    for i in range(4):
        self.dma(M.nm[i].all(), self.U(self.cst["nmask"].t.ap()[i]))
    M.bm = [sb([128, 128], F32, "bm") for _ in range(4)]
    for i in range(4):
        self.dma(M.bm[i].all(), self.U(self.cst["bmask"].t.ap()[i]))
    M.dmat = [sb([128, 128], F32, "dmat") for _ in range(2)]
    M.sel3 = [sb([128, 4], F32, "sel3") for _ in range(2)]
    M.pos = [sb([128, 2], F32, "pos") for _ in range(2)]
    for d in range(2):
        self.dma(M.dmat[d].all(), self.U(self.cst["dmat"].t.ap()[d]))
        self.dma(M.sel3[d].all(), self.U(self.cst["sel3"].t.ap()[d]))
        self.dma(M.pos[d].all(), self.U(self.cst["pos"].t.ap()[d]))
    M.maskI = lambda d: M.cm[d * 4 + 0]
    M.maskS = lambda d: M.cm[d * 4 + 1]
    M.TmS = lambda d: M.cm[d * 4 + 2]
    M.TsS = lambda d: M.cm[d * 4 + 3]
    M.nmI = lambda d: M.nm[d * 2 + 0]
    M.nmS = lambda d: M.nm[d * 2 + 1]

    def bc(name, idx, n, dt=F32, P=128, q="sp"):
        t = sb([P, n], dt, "bc_" + name)
        src = self.w[name].t.ap()[l]
        for i in idx:
            src = src[i]
        self.dma(t.all(), self.U(src.partition_broadcast(P)), q=q)
        return t
    M.bc = bc
    M.mu = [bc("rwkv_mu", (d,), 1920, BF16, q="pool") for d in range(2)]
    M.k_k = bc("rwkv_k_k", (), 512)
    M.k_a = bc("rwkv_k_a", (), 512)
    t = sb([128, 512], F32, "bc_r_k")
    self.dma(t.all(), self.U(self.w["rwkv_r_k"].t.ap()[l].rearrange("h k -> (h k)").partition_broadcast(128)))
    M.r_k = t
    M.ln_w = bc("rwkv_ln_w", (), 512)
    M.ln_b = bc("rwkv_ln_b", (), 512)
    M.w0 = [bc("rwkv_w0", (d,), 512) for d in range(2)]
    M.a0 = [bc("rwkv_a0", (d,), 512) for d in range(2)]
    M.w_up, M.a_up = [], []
    for d in range(2):
        for (lst, nm_) in ((M.w_up, "rwkv_w_up"), (M.a_up, "rwkv_a_up")):
            t = sb([64, 512], BF16, nm_)
            self.dma(t.all(), self.U(self.w[nm_].t.ap()[l, d]), q="pool")
            lst.append(t)
    M.g_up = sb([128, 512], BF16, "g_up")
    self.dma(M.g_up.all(), self.U(self.w["rwkv_g_up"].t.ap()[l]), q="pool")
    M.hnorm = bc("hgrn_norm", (), 512, P=32)
    M.lb = sb([32, 512], F32, "lb")
    M.oml = sb([32, 512], F32, "oml")
    if l == 0:
        self.memset(M.lb.all(), 0.0)
        self.memset(M.oml.all(), 1.0)
    else:
        self.dma(M.lb.all(), self.U(self.w["hgrn_lb_logits"].t.ap()[1].partition_broadcast(32)))
        self.dma(M.oml.all(), self.U(self.w["hgrn_lb_logits"].t.ap()[0].partition_broadcast(32)))
        self.tt(M.lb.all(), M.lb.all(), M.oml.all(), ALU.subtract)
        self.act(M.lb.all(), M.lb.all(), AF.Sigmoid)
        self.ts(M.oml.all(), M.lb.all(), -1.0, ALU.mult, 1.0, ALU.add)
    M.cw = [bc("gdn_conv_w", (j,), 1536, BF16, q="pool") for j in range(3)]
    M.gnorm = bc("gdn_norm", (), 512)
    t = sb([128, 8], F32, "negA")
    self.dma(t.all(), self.U(self.w["gdn_a_log"].t.ap()[l].rearrange("d h -> (d h)").partition_broadcast(128)))
    self.act(t.all(), t.all(), AF.Exp)
    self.ts(t.all(), t.all(), -1.0, ALU.mult)
    M.negA = t
    t = sb([128, 8], F32, "dtb")
    self.dma(t.all(), self.U(self.w["gdn_dt_bias"].t.ap()[l].rearrange("d h -> (d h)").partition_broadcast(128)))
    M.dtb = t
    M.rnorm = bc("ret_norm", (), 512)
    lg = sb([128, 8], F32, "lg")
    self.dma(lg.all(), self.U(self.w["ret_decay_logit"].t.ap()[l].rearrange("d h -> (d h)").partition_broadcast(128)))
    self.act(lg.all(), lg.all(), AF.Exp, scale=-1.0)
    self.act(lg.all(), lg.all(), AF.Ln, bias=1.0)
    self.ts(lg.all(), lg.all(), -1.0, ALU.mult)
    M.lg = lg
    M.tabR, M.tabK, M.f1r, M.GR = [], [], [], []
    for d in range(2):
        tr_ = sb([128, 4], F32, "tabR")
        tk_ = sb([128, 4], F32, "tabK")
        f1_ = sb([128, 4], F32, "f1r")
        self.ts(tr_.all(), lg[:, 4 * d:4 * d + 4], M.pos[d][:, 0:1], ALU.mult)
        self.act(tr_.all(), tr_.all(), AF.Exp)
        self.ts(tk_.all(), lg[:, 4 * d:4 * d + 4], M.pos[d][:, 1:2], ALU.mult)
        self.act(tk_.all(), tk_.all(), AF.Exp)
        self.act(f1_.all(), lg[:, 4 * d:4 * d + 4], AF.Exp, scale=128.0)
        M.tabR.append(tr_)
        M.tabK.append(tk_)
        M.f1r.append(f1_)
        gr = sb([128, 4, 128], F32, "GR")
        for h in range(4):
            self.act(gr[:, h, :], M.dmat[d].all(), AF.Exp, scale=lg[:, 4 * d + h:4 * d + h + 1])
        M.GR.append(gr)
    M.W = [sb([128, 1920], F32, "W") for _ in range(3)]
    M.F = [sb([128, 512], F32, "F") for _ in range(13)]
    M.H = [sb([128, 512], BF16, "H") for _ in range(10)]
    M.TP = [sb([128, 8, 128], BF16, "TP") for _ in range(6)]
    M.sm = [sb([128, 16], F32, "sm") for _ in range(6)]
    M.lT = [sb([128, 128], BF16, "lT") for _ in range(3)]
    M.cols = sb([128, 32], F32, "cols")
    M.Gia = sb([128, 4, 128], F32, "Gia")
    M.Gea = sb([128, 4, 128], F32, "Gea")
    M.Gb = [sb([128, 128], F32, "Gb") for _ in range(2)]
    M.rot = [sb([128, 32], F32, "rot") for _ in range(2)]
    mk = lambda n: sb([128, 8, 128], BF16, n)
    M.AkT, M.NT, M.AbT, M.Lf, M.LT, M.TT, M.NoT, M.A1 = [mk(n) for n in ("AkT", "NT", "AbT", "Lf", "LT", "TT", "NoT", "A1")]
    M.Q = [mk("Q0"), mk("Q1")]
    M.QT = [mk("QT0"), mk("QT1")]
    M.P = [mk("P0"), mk("P1")]
    M.Xa = sb([128, 512], BF16, "Xa")
    M.nEa = sb([128, 512], BF16, "nEa")
    M.tSa = sb([128, 512], F32, "tSa")
    M.SA = sb([64, 8, 64], F32, "SA")
    M.SAp = sb([64, 8, 64], BF16, "SAp")
    M.SB = sb([128, 4, 128], F32, "SB")
    M.SBp = sb([128, 4, 128], BF16, "SBp")
    M.SC = sb([128, 4, 128], F32, "SC")
    M.SCp = sb([128, 4, 128], BF16, "SCp")
    M.SD = sb([64, 4, 128], F32, "SD")
    M.SDp = sb([64, 4, 128], BF16, "SDp")
    M.pw = [self.ps(es, [128, 512], F32, "pw") for _ in range(2)]
    M.ptr = [self.ps(es, [128, 8, 128], BF16, "ptr") for _ in range(2)]
    M.slots = []
    for i in range(4):
        bank = es.enter_context(nc.psum_tensor(f"slotbank_{l}_{g['name']}_{i}", [128, 4, 128], F32))
        M.slots.append(V(bank[:, :, :], Buf()))
    M.si = 0
    M.pti = 0
    M.pwi = 0
    M.fi = 0
    M.hi = 0
    return M


def m_slot(self, M):
    s = M.slots[M.si % len(M.slots)]
    M.si += 1
    return s


def m_pw(self, M):
    p = M.pw[M.pwi % 2]
    M.pwi += 1
    return p


def m_trT(self, M, src, c, nblk, w, dst):
    idb = self.C["ident_b"]
    pt = M.ptr[M.pti % 2]
    M.pti += 1
    for k in range(nblk):
        self.tr(pt[:w, k, :c], src[:, k * w:(k + 1) * w], idb[:c, :c], inc=(k == nblk - 1))
    self.cp(dst[:w, 0:nblk, :c], pt[:w, 0:nblk, :c], e="act")


def _bm(mask_v, c, n):
    return V(mask_v.ap.unsqueeze(1).to_broadcast([c, n, c]), mask_v.b)


def m_core(self, M, c, K, Vd, Hn, delta, RaT, RbT, KaT, BaT, CaT, CbT, Kc, Bc, Vv, Mii, Mei, f1, f2, S, S0p, O):
    idb = self.C["ident_b"]
    gsz = 4
    groups = [(h0, min(gsz, Hn - h0)) for h0 in range(0, Hn, gsz)]
    sl = lambda: m_slot(self, M)

    def vbank():
        b = sl()
        return V(b.ap.rearrange("p a b -> p (a b)").rearrange("p (h v) -> p h v", v=Vd), b.b)

    def scores(dst, lf, rf, mult, neg=False):
        for (h0, n) in groups:
            b = sl()
            for i in range(n):
                self.mm(b[:c, i, :c], lf(h0 + i), rf(h0 + i), inc=(i == n - 1))
            if neg:
                self.stt(dst[:c, h0:h0 + n, :c], b[:c, 0:n, :c], -1.0, mult(h0, n), ALU.mult, ALU.mult)
            else:
                self.tt(dst[:c, h0:h0 + n, :c], b[:c, 0:n, :c], mult(h0, n), ALU.mult)

    def mm_stage(dst, lt, rt, eng):
        for gi, (h0, n) in enumerate(groups):
            b = sl()
            for i in range(n):
                self.mm(b[:, i, :], lt[:, h0 + i, :], rt[:, h0 + i, :], inc=(i == n - 1))
            self.cp(dst[:, h0:h0 + n, :], b[:, 0:n, :], e=(eng if gi % 2 == 0 else ("dve" if eng == "act" else "act")))

    def upd_stage(dst, lt, rt, base):
        for gi, (h0, n) in enumerate(groups):
            b = sl()
            for i in range(n):
                self.mm(b[:, i, :], lt[:, h0 + i, :], rt[:, h0 + i, :], inc=(i == n - 1))
            self.tt(dst[:, h0:h0 + n, :], base[:, h0:h0 + n, :], b[:, 0:n, :], ALU.add)

    def transp(dst, src):
        for h0 in range(0, Hn, 8):
            n = min(8, Hn - h0)
            pt = M.ptr[M.pti % 2]
            M.pti += 1
            for i in range(n):
                self.tr(pt[:, i, :], src[:, h0 + i, :], idb.all(), inc=(i == n - 1))
            self.cp(dst[:, h0:h0 + n, :], pt[:, 0:n, :], e="act")

    scores(M.AkT, KaT, RaT, Mii)
    if delta:
        assert c == 128
        scores(M.NT, KaT, CaT, Mei)
        scores(M.Lf, BaT, CaT, Mei, neg=True)
        transp(M.LT, M.Lf)
        hs = slice(0, Hn)
        self.tt(M.Q[0][:, hs, :], M.Lf[:, hs, :], _bm(M.bm[0].all(), 128, Hn), ALU.mult)
        self.tt(M.QT[0][:, hs, :], M.LT[:, hs, :], _bm(M.bm[0].all(), 128, Hn), ALU.mult, e="pool")
        self.tt(M.P[0][:, hs, :], M.Q[0][:, hs, :], _bm(idb.all(), 128, Hn), ALU.add, e="pool")
        cur = 0
        for i in range(1, 4):
            nxt = 1 - cur
            if i < 3:
                mm_stage(M.Q[nxt], M.QT[cur], M.Q[cur], "dve")
            mm_stage(M.QT[nxt], M.Q[cur], M.QT[cur], "act")
            upd_stage(M.P[nxt], M.QT[nxt], M.P[cur], M.P[cur])
            cur = nxt
        transp(M.TT, M.P[cur])
        for bi in range(3):
            nxt = 1 - cur
            self.tt(M.NoT[:, hs, :], M.LT[:, hs, :], _bm(M.bm[bi + 1].all(), 128, Hn), ALU.mult, e="pool")
            mm_stage(M.A1, M.NoT, M.P[cur], "act")
            upd_stage(M.P[nxt], M.TT, M.A1, M.P[cur])
            cur = nxt
            if bi < 2:
                transp(M.TT, M.P[cur])
        Pf = M.P[cur]
        Xa = V(M.Xa.t[:, :].rearrange("p (h v) -> p h v", v=Vd), M.Xa.b)
        nE = V(M.nEa.t[:, :].rearrange("p (h v) -> p h v", v=Vd), M.nEa.b)
        bv = vbank()
        for h in range(Hn):
            self.mm(bv[:c, h, :], CbT(h), S0p(h), start=True, stop=False, inc=False)
            self.mm(bv[:c, h, :], M.NT[:c, h, :c], Vv(h), start=False, stop=True, inc=(h == Hn - 1))
        self.cp(Xa[:c, :, :], bv[:c, :, :], e="act")
        bv = vbank()
        for h in range(Hn):
            self.mm(bv[:c, h, :], Pf[:c, h, :c], Xa[:c, h, :], inc=(h == Hn - 1))
        self.ts(nE[:c, :, :], bv[:c, :, :], -1.0, ALU.mult)
        scores(M.AbT, BaT, RaT, Mii)
    bv = vbank()
    for h in range(Hn):
        self.mm(bv[:c, h, :], RbT(h), S0p(h), start=True, stop=False, inc=False)
        self.mm(bv[:c, h, :], M.AkT[:c, h, :c], Vv(h), start=False, stop=(not delta), inc=((not delta) and h == Hn - 1))
        if delta:
            self.mm(bv[:c, h, :], M.AbT[:c, h, :c], nE[:c, h, :], start=False, stop=True, inc=(h == Hn - 1))
    self.cp(O, bv[:c, :, :], e="act")
    bv = vbank()
    for h in range(Hn):
        self.mm(bv[:K, h, :], Kc(h), Vv(h), start=True, stop=(not delta), inc=((not delta) and h == Hn - 1))
        if delta:
            self.mm(bv[:K, h, :], Bc(h), nE[:c, h, :], start=False, stop=True, inc=(h == Hn - 1))
    f1b = _b3(f1, Hn, Vd)
    if f2 is not None:
        tS = V(M.tSa.t[:K, :].rearrange("p (h v) -> p h v", v=Vd), M.tSa.b)
        self.tt(tS, bv[:K, :, :], _b3(f2, Hn, Vd), ALU.mult)
        self.tt(S, S, f1b, ALU.mult, e="pool")
        self.tt(S, S, tS, ALU.add, e="pool")
    else:
        self.tt(S, S, f1b, ALU.mult, e="pool")
        self.tt(S, S, bv[:K, :, :], ALU.add)


def m_F(self, M):
    t = M.F[M.fi % len(M.F)]
    M.fi += 1
    return t


def m_H(self, M):
    t = M.H[M.hi % len(M.H)]
    M.hi += 1
    return t


def m_out(self, M, g, d, t0, P, col0, O, fin, rows=None):
    gn = g["name"]
    of, mix = self.scr[gn]["of"], self.scr[gn]["mix"]
    key = (gn, t0, col0)
    if d == 0:
        b = Buf()
        self.ofb[key] = b
        self.dma(V(of.t.ap()[t0:t0 + P, col0:col0 + 512], b), O)
    else:
        b = self.ofb[key]
        ofl = m_F(self, M)
        self.dma(ofl[:P, :], V(of.t.ap()[t0:t0 + P, col0:col0 + 512], b))
        self.tt(O, O, ofl[:P, :], ALU.add)
        ob = fin()
        self.dma(self.U(mix.t.ap()[t0:t0 + P, col0:col0 + 512]), ob)


def m_headnorm_gate(self, M, O, P, gate, normw):
    sq = m_F(self, M)
    self.tt(sq[:P, :], O, O, ALU.mult, e="pool")
    s4 = M.sm[5]
    self.red(s4[:P, 0:4], _r3(sq[:P, :], 4, 128))
    self.rsqrt(s4[:P, 0:4], s4[:P, 0:4], 1.0 / 128, EPS)
    self.tt(_r3(O, 4, 128), _r3(O, 4, 128), _b3(s4[:P, 0:4], 4, 128), ALU.mult)
    self.tt(O, O, normw[:P, :], ALU.mult)
    sg = m_F(self, M)
    self.act(sg[:P, :], gate, AF.Sigmoid)
    self.tt(sg[:P, :], sg[:P, :], gate, ALU.mult, e="pool")
    ob = m_H(self, M)
    self.tt(ob[:P, :], O, sg[:P, :], ALU.mult)
    return ob[:P, :]


def m_rwkv(self, M, g, l, d, r0, t0):
    proj = self.scr[g["name"]]["proj"]
    W0, W1, W2 = M.W
    for j, Wt in enumerate(M.W):
        self.dma(Wt[:, 0:1920], self.U(proj.t.ap()[r0 - 1 + j:r0 - 1 + j + 128, OFF_A:OFF_A + 1920]))
    self.tt(W0.all(), W0.all(), W1.all(), ALU.subtract, e="pool")
    self.tt(W0.all(), W0.all(), M.mu[0].all(), ALU.mult, e="pool")
    self.tt(W2.all(), W2.all(), W1.all(), ALU.subtract)
    self.tt(W2.all(), W2.all(), M.mu[1].all(), ALU.mult)
    self.tt(W1.all(), W1.all(), W0.all(), ALU.add, e="pool")
    self.tt(W1.all(), W1.all(), W2.all(), ALU.add)
    r, k, v = W1[:, 0:512], W1[:, 512:1024], W1[:, 1024:1536]
    wl = W1[:, 1536 + 64 * d:1600 + 64 * d]
    al = W1[:, 1664 + 64 * d:1728 + 64 * d]
    gl = W1[:, 1792:1920]
    F = lambda: m_F(self, M)
    Hh = lambda: m_H(self, M)
    idb = self.C["ident_b"]
    kk = F()
    self.tt(kk.all(), k, M.k_k.all(), ALU.mult)
    sq = F()
    self.tt(sq.all(), kk.all(), kk.all(), ALU.mult, e="pool")
    s8 = M.sm[0]
    self.red(s8[:, 0:8], _r3(sq.all(), 8, 64))
    self.rsqrt(s8[:, 0:8], s8[:, 0:8], 1.0, EPS)
    self.tt(_r3(kk.all(), 8, 64), _r3(kk.all(), 8, 64), _b3(s8[:, 0:8], 8, 64), ALU.mult)
    th = M.lT[0]
    self.act(th[:, 0:64], wl, AF.Tanh)
    pt = M.ptr[M.pti % 2]
    M.pti += 1
    self.tr(pt[:64, 0, :], th[:, 0:64], idb.all())
    thT = M.lT[1]
    self.cp(thT[:64, :], pt[:64, 0, :], e="act")
    pw = m_pw(self, M)
    self.mm(pw.all(), thT[:64, :], M.w_up[d].all())
    LW = F()
    self.tt(LW.all(), pw.all(), M.w0[d].all(), ALU.add)
    self.act(LW.all(), LW.all(), AF.Sigmoid)
    self.ts(LW.all(), LW.all(), -float(np.exp(-0.5)), ALU.mult, e="pool")
    ab_ = M.lT[0]
    self.cp(ab_[:, 64:128], al)
    pt = M.ptr[M.pti % 2]
    M.pti += 1
    self.tr(pt[:64, 0, :], ab_[:, 64:128], idb.all())
    alT = M.lT[2]
    self.cp(alT[:64, :], pt[:64, 0, :], e="act")
    pw = m_pw(self, M)
    self.mm(pw.all(), alT[:64, :], M.a_up[d].all())
    a = F()
    self.tt(a.all(), pw.all(), M.a0[d].all(), ALU.add)
    self.act(a.all(), a.all(), AF.Sigmoid)
    kd = F()
    self.stt(kd.all(), a.all(), -1.0, M.k_a.all(), ALU.add, ALU.mult)
    self.stt(kd.all(), kd.all(), 1.0, k, ALU.add, ALU.mult)
    bb = F()
    self.tt(bb.all(), a.all(), kk.all(), ALU.mult, e="pool")
    pw = m_pw(self, M)
    self.mm(pw.all(), M.TmS(d).all(), LW.all())
    e1, e1n = F(), F()
    self.act(e1.all(), pw.all(), AF.Exp)
    self.act(e1n.all(), pw.all(), AF.Exp, scale=-1.0)
    pw = m_pw(self, M)
    self.mm(pw.all(), M.TsS(d).all(), LW.all())
    e2 = sq
    self.act(e2.all(), pw.all(), AF.Exp)
    pc = m_slot(self, M)
    for h in range(8):
        self.mm(pc[:64, 0, h * 4:h * 4 + 4], LW[:, h * 64:(h + 1) * 64], M.sel3[d].all())
    self.act(M.cols[:64, :], pc[:64, 0, 0:32], AF.Exp)
    colv = V(M.cols.t[:64, 0:32].rearrange("p (h j) -> p h j", j=4), M.cols.b)
    rt, kt, bt, ct, vb = Hh(), Hh(), Hh(), Hh(), Hh()
    self.tt(rt.all(), r, e1.all(), ALU.mult)
    self.tt(kt.all(), kd.all(), e1n.all(), ALU.mult)
    self.tt(bt.all(), bb.all(), e1n.all(), ALU.mult, e="pool")
    self.tt(ct.all(), kk.all(), e2.all(), ALU.mult, e="pool")
    self.cp(vb.all(), v, e="act")
    rtT, ktT, btT, ctT = M.TP[0], M.TP[1], M.TP[2], M.TP[3]
    for (src, dst) in ((rt, rtT), (kt, ktT), (bt, btT), (ct, ctT)):
        m_trT(self, M, src.all(), 128, 8, 64, dst)
    self.tt(M.SAp.all(), M.SA.all(), V(M.cols.t[:64, :].rearrange("p (h j) -> p h j", j=4)[:, :, 0:1].to_broadcast([64, 8, 64]), M.cols.b), ALU.mult)
    O = F()
    m_core(self, M, 128, 64, 64, 8, True,
           RaT=lambda h: rtT[:64, h, :], RbT=lambda h: rtT[:64, h, :], KaT=lambda h: ktT[:64, h, :],
           BaT=lambda h: btT[:64, h, :], CaT=lambda h: ctT[:64, h, :], CbT=lambda h: ctT[:64, h, :],
           Kc=lambda h: kt[:, h * 64:(h + 1) * 64], Bc=lambda h: bt[:, h * 64:(h + 1) * 64],
           Vv=lambda h: vb[:, h * 64:(h + 1) * 64],
           Mii=lambda h0, n: _bm(M.maskI(d).all(), 128, n), Mei=lambda h0, n: _bm(M.maskS(d).all(), 128, n),
           f1=colv[:, :, 1], f2=colv[:, :, 2],
           S=M.SA.all(), S0p=lambda h: M.SAp[:, h, :],
           O=_r3(O.all(), 8, 64))

    def fin():
        m8 = M.sm[1]
        self.red(m8[:, 0:8], _r3(O.all(), 8, 64))
        self.ts(m8[:, 0:8], m8[:, 0:8], 1.0 / 64, ALU.mult)
        cen = F()
        self.tt(_r3(cen.all(), 8, 64), _r3(O.all(), 8, 64), _b3(m8[:, 0:8], 8, 64), ALU.subtract)
        sq2 = F()
        self.tt(sq2.all(), cen.all(), cen.all(), ALU.mult, e="pool")
        v8 = M.sm[2]
        self.red(v8[:, 0:8], _r3(sq2.all(), 8, 64))
        self.rsqrt(v8[:, 0:8], v8[:, 0:8], 1.0 / 64, GN_EPS)
        self.tt(_r3(cen.all(), 8, 64), _r3(cen.all(), 8, 64), _b3(v8[:, 0:8], 8, 64), ALU.mult)
        self.tt(cen.all(), cen.all(), M.ln_w.all(), ALU.mult)
        self.tt(cen.all(), cen.all(), M.ln_b.all(), ALU.add, e="pool")
        rk = sq2
        self.tt(rk.all(), r, k, ALU.mult)
        self.tt(rk.all(), rk.all(), M.r_k.all(), ALU.mult, e="pool")
        b8 = M.sm[3]
        self.red(b8[:, 0:8], _r3(rk.all(), 8, 64))
        self.tt(_r3(rk.all(), 8, 64), _r3(v, 8, 64), _b3(b8[:, 0:8], 8, 64), ALU.mult)
        self.tt(cen.all(), cen.all(), rk.all(), ALU.add, e="pool")
        sg = M.lT[0]
        self.act(sg.all(), gl, AF.Sigmoid)
        pt_ = M.ptr[M.pti % 2]
        M.pti += 1
        self.tr(pt_[:, 0, :], sg.all(), idb.all())
        sgT = M.lT[1]
        self.cp(sgT.all(), pt_[:, 0, :], e="act")
        pw_ = m_pw(self, M)
        self.mm(pw_.all(), sgT.all(), M.g_up.all())
        ob = Hh()
        self.tt(ob.all(), cen.all(), pw_.all(), ALU.mult)
        return ob.all()
    m_out(self, M, g, d, t0, 128, 0, O.all(), fin)


def m_hgrn(self, M, g, l, d, r0, t0):
    proj = self.scr[g["name"]]["proj"]
    F = lambda: m_F(self, M)
    Hh = lambda: m_H(self, M)
    WB = M.W[0]
    WB2 = M.W[1]
    subs = range(4) if d == 0 else range(3, -1, -1)
    c = 32
    for j in subs:
        rr = r0 + 32 * j
        self.dma(WB[:32, 0:1920], self.U(proj.t.ap()[rr:rr + 32, OFF_B:OFF_B + 1920]))
        self.dma(WB2[:32, 0:640], self.U(proj.t.ap()[rr:rr + 32, OFF_B + 1920:OFF_B + 2560]))
        qx = WB[:32, 0:512]
        fl = WB[:32, 512 + 512 * d:1024 + 512 * d]
        iv = WB[:32, 1536:1920]
        q = F()
        self.act(q[:32, :], qx, AF.Sigmoid)
        self.tt(q[:32, :], q[:32, :], qx, ALU.mult)
        f = F()
        self.act(f[:32, :], fl, AF.Sigmoid)
        self.tt(f[:32, :], f[:32, :], M.oml.all(), ALU.mult, e="pool")
        self.tt(f[:32, :], f[:32, :], M.lb.all(), ALU.add, e="pool")
        LF = F()
        self.act(LF[:32, :], f[:32, :], AF.Ln)
        kin = F()
        self.act(kin[:32, :], fl, AF.Sigmoid, scale=-1.0)
        self.tt(kin[:32, :], kin[:32, :], M.oml.all(), ALU.mult, e="pool")
        pw = m_pw(self, M)
        self.mm(pw[:32, :], M.maskI(d)[:32, :32], LF[:32, :])
        e1, e1n = F(), F()
        self.act(e1[:32, :], pw[:32, :], AF.Exp)
        self.ts(e1n[:32, :], pw[:32, :], -1.0, ALU.mult, 80.0, ALU.min)
        self.act(e1n[:32, :], e1n[:32, :], AF.Exp)
        rt, kt, vb = Hh(), Hh(), Hh()
        self.tt(rt[:32, :], q[:32, :], e1[:32, :], ALU.mult)
        self.tt(kt[:32, :], kin[:32, :], e1n[:32, :], ALU.mult)
        self.cp(vb[:32, 0:384], iv, e="act")
        self.cp(vb[:32, 384:512], WB2[:32, 0:128], e="act")
        pc = m_slot(self, M)
        for h in range(4):
            self.mm(pc[:, 0, 2 * h:2 * h + 2], LF[:32, h * 128:(h + 1) * 128], self.C["ones_f"][:32, 0:2])
        self.act(M.cols[:, 0:8], pc[:, 0, 0:8], AF.Exp)
        etot = V(M.cols.t[:, 0:8].rearrange("p (h j) -> p h j", j=2), M.cols.b)[:, :, 0]
        rtT, ktT = M.TP[4], M.TP[5]
        m_trT(self, M, rt[:32, :], 32, 4, 128, rtT)
        m_trT(self, M, kt[:32, :], 32, 4, 128, ktT)
        self.cp(M.SBp.all(), M.SB.all(), e="pool")
        O = F()
        m_core(self, M, 32, 128, 128, 4, False,
               RaT=lambda h: rtT[:, h, :32], RbT=lambda h: rtT[:, h, :32], KaT=lambda h: ktT[:, h, :32],
               BaT=None, CaT=None, CbT=None,
               Kc=lambda h: kt[:32, h * 128:(h + 1) * 128], Bc=None,
               Vv=lambda h: vb[:32, h * 128:(h + 1) * 128],
               Mii=lambda h0, n: _bm(M.maskI(d)[:32, :32], 32, n), Mei=None,
               f1=etot, f2=etot,
               S=M.SB.all(), S0p=lambda h: M.SBp[:, h, :],
               O=_r3(O[:32, :], 4, 128))
        gb = WB2[:32, 128:640]
        m_out(self, M, g, d, t0 + 32 * j, 32, 512, O[:32, :],
              lambda: m_headnorm_gate(self, M, O[:32, :], 32, gb, M.hnorm))


def m_gdn(self, M, g, l, d, r0, t0):
    proj = self.scr[g["name"]]["proj"]
    F = lambda: m_F(self, M)
    Hh = lambda: m_H(self, M)
    W0, W1, W2 = M.W
    for j, Wt in enumerate(M.W):
        self.dma(Wt[:, 0:1536], self.U(proj.t.ap()[r0 - 1 + j:r0 - 1 + j + 128, OFF_C:OFF_C + 1536]))
    gatec = F()
    self.dma(gatec.all(), self.U(proj.t.ap()[r0:r0 + 128, OFF_C + 1536:OFF_C + 2048]))
    ab = M.sm[0]
    self.dma(ab[:, 0:16], self.U(proj.t.ap()[r0:r0 + 128, OFF_C + 2048:OFF_C + 2064]))
    self.tt(W0[:, 0:1536], W0[:, 0:1536], M.cw[0].all(), ALU.mult, e="pool")
    self.tt(W2[:, 0:1536], W2[:, 0:1536], M.cw[2].all(), ALU.mult)
    self.tt(W1[:, 0:1536], W1[:, 0:1536], M.cw[1].all(), ALU.mult)
    self.tt(W1[:, 0:1536], W1[:, 0:1536], W0[:, 0:1536], ALU.add, e="pool")
    self.tt(W1[:, 0:1536], W1[:, 0:1536], W2[:, 0:1536], ALU.add)
    self.act(W0[:, 0:1536], W1[:, 0:1536], AF.Sigmoid)
    self.tt(W1[:, 0:1536], W1[:, 0:1536], W0[:, 0:1536], ALU.mult)
    q, k, v = W1[:, 0:512], W1[:, 512:1024], W1[:, 1024:1536]
    qn, kn = F(), F()
    sq = F()
    s4 = M.sm[1]
    for (x, xn, scl, c0) in ((q, qn, 128.0 ** -0.5, 0), (k, kn, 1.0, 4)):
        self.tt(sq.all(), x, x, ALU.mult, e="pool")
        self.red(s4[:, c0:c0 + 4], _r3(sq.all(), 4, 128))
        self.rsqrt(s4[:, c0:c0 + 4], s4[:, c0:c0 + 4], 1.0, EPS)
        if scl != 1.0:
            self.ts(s4[:, c0:c0 + 4], s4[:, c0:c0 + 4], scl, ALU.mult)
        self.tt(_r3(xn.all(), 4, 128), _r3(x, 4, 128), _b3(s4[:, c0:c0 + 4], 4, 128), ALU.mult)
    gs = M.sm[2]
    gg, beta, eg = gs[:, 0:4], gs[:, 4:8], gs[:, 8:12]
    self.tt(gg, ab[:, 4 * d:4 * d + 4], M.dtb[:, 4 * d:4 * d + 4], ALU.add)
    self.act(gg, gg, AF.Exp)
    self.act(gg, gg, AF.Ln, bias=1.0)
    self.tt(gg, gg, M.negA[:, 4 * d:4 * d + 4], ALU.mult)
    self.act(beta, ab[:, 8 + 4 * d:12 + 4 * d], AF.Sigmoid)
    self.act(eg, gg, AF.Exp)
    kp, bp = F(), F()
    self.tt(_r3(kp.all(), 4, 128), _r3(kn.all(), 4, 128), _b3(beta, 4, 128), ALU.mult)
    self.tt(_r3(bp.all(), 4, 128), _r3(kp.all(), 4, 128), _b3(eg, 4, 128), ALU.mult)
    pc = m_slot(self, M)
    self.mm(pc[:, 0, 0:4], M.maskI(d).all(), gg)
    self.mm(pc[:, 0, 4:8], M.maskS(d).all(), gg)
    self.mm(pc[:, 0, 8:12], self.C["ones_f"].all(), gg)
    cs = M.sm[3]
    self.cp(cs[:, 0:12], pc[:, 0, 0:12])
    cum, cume, tot = cs[:, 0:4], cs[:, 4:8], cs[:, 8:12]
    ex = M.sm[4]
    ecum, ecume, etc_, f1c = ex[:, 0:4], ex[:, 4:8], ex[:, 8:12], ex[:, 12:16]
    self.act(ex[:, 0:8], cs[:, 0:8], AF.Exp)
    self.tt(etc_, tot, cum, ALU.subtract)
    self.act(etc_, etc_, AF.Exp)
    self.act(f1c, tot, AF.Exp)
    negcum = cs[:, 12:16]
    self.ts(negcum, cum, -1.0, ALU.mult)
    Ra, Rb, Ca, Cb, Ka, Kc, Ba, Bc, vb = [Hh() for _ in range(9)]
    self.cp(Ra.all(), qn.all(), e="act")
    self.tt(_r3(Rb.all(), 4, 128), _r3(qn.all(), 4, 128), _b3(ecum, 4, 128), ALU.mult)
    self.cp(Ca.all(), kn.all(), e="act")
    self.tt(_r3(Cb.all(), 4, 128), _r3(kn.all(), 4, 128), _b3(ecume, 4, 128), ALU.mult)
    self.cp(Ka.all(), kp.all(), e="act")
    self.tt(_r3(Kc.all(), 4, 128), _r3(kp.all(), 4, 128), _b3(etc_, 4, 128), ALU.mult)
    self.cp(Ba.all(), bp.all(), e="act")
    self.tt(_r3(Bc.all(), 4, 128), _r3(bp.all(), 4, 128), _b3(etc_, 4, 128), ALU.mult)
    self.cp(vb.all(), v, e="act")
    for h in range(4):
        Gb = M.Gb[h % 2]
        self.ts(Gb.all(), self.C["ones_f"].all(), gg[:, h:h + 1], ALU.mult)
        for (mask, nmask, dst) in ((M.maskI(d), M.nmI(d), M.Gia[:, h, :]), (M.maskS(d), M.nmS(d), M.Gea[:, h, :])):
            p = m_slot(self, M)
            self.mm(p[:, 0, :], Gb.all(), mask.all())
            self.stt(dst, p[:, 0, :], negcum[:, h:h + 1], nmask.all(), ALU.add, ALU.add)
    self.act(M.Gia.all(), M.Gia.all(), AF.Exp)
    self.act(M.Gea.all(), M.Gea.all(), AF.Exp)
    if True:
        if True:
            pass
    TPs = M.TP
    for (src, dst) in ((Ra, TPs[0]), (Rb, TPs[1]), (Ka, TPs[2]), (Ba, TPs[3]), (Ca, TPs[4]), (Cb, TPs[5])):
        m_trT(self, M, src.all(), 128, 4, 128, dst)
    self.cp(M.SCp.all(), M.SC.all(), e="pool")
    O = F()
    m_core(self, M, 128, 128, 128, 4, True,
           RaT=lambda h: TPs[0][:, h, :], RbT=lambda h: TPs[1][:, h, :], KaT=lambda h: TPs[2][:, h, :],
           BaT=lambda h: TPs[3][:, h, :], CaT=lambda h: TPs[4][:, h, :], CbT=lambda h: TPs[5][:, h, :],
           Kc=lambda h: Kc[:, h * 128:(h + 1) * 128], Bc=lambda h: Bc[:, h * 128:(h + 1) * 128],
           Vv=lambda h: vb[:, h * 128:(h + 1) * 128],
           Mii=lambda h0, n: M.Gia[:, h0:h0 + n, :], Mei=lambda h0, n: M.Gea[:, h0:h0 + n, :],
           f1=f1c, f2=None,
           S=M.SC.all(), S0p=lambda h: M.SCp[:, h, :],
           O=_r3(O.all(), 4, 128))
    m_out(self, M, g, d, t0, 128, 1024, O.all(),
          lambda: m_headnorm_gate(self, M, O.all(), 128, gatec.all(), M.gnorm))


def m_ret(self, M, g, l, d, r0, t0, pos0):
    proj = self.scr[g["name"]]["proj"]
    F = lambda: m_F(self, M)
    Hh = lambda: m_H(self, M)
    WD = M.W[2]
    self.dma(WD[:, 0:1536], self.U(proj.t.ap()[r0:r0 + 128, OFF_D:OFF_D + 1536]))
    q, k, v, gate = WD[:, 0:256], WD[:, 256:512], WD[:, 512:1024], WD[:, 1024:1536]
    if g["rot"]:
        cs_, sn_ = M.rot
        self.dma(cs_.all(), self.U(self.cst["rot_cos"].t.ap()[pos0:pos0 + 128, :]))
        self.dma(sn_.all(), self.U(self.cst["rot_sin"].t.ap()[pos0:pos0 + 128, :]))
        cb_ = V(cs_.t[:, :].unsqueeze(1).to_broadcast([128, 4, 32]), cs_.b)
        sb_ = V(sn_.t[:, :].unsqueeze(1).to_broadcast([128, 4, 32]), sn_.b)
        qk = F()
        t1, t2 = F(), F()
        for (x, c0) in ((q, 0), (k, 256)):
            x3 = _r3(x, 4, 64)
            o3 = _r3(qk[:, c0:c0 + 256], 4, 64)
            a3 = _r3(t1[:, 0:128], 4, 32)
            b3 = _r3(t2[:, 0:128], 4, 32)
            self.tt(a3, x3[:, :, 0:32], cb_, ALU.mult)
            self.tt(b3, x3[:, :, 32:64], sb_, ALU.mult, e="pool")
            self.tt(o3[:, :, 0:32], a3, b3, ALU.subtract)
            a3 = _r3(t1[:, 128:256], 4, 32)
            b3 = _r3(t2[:, 128:256], 4, 32)
            self.tt(a3, x3[:, :, 0:32], sb_, ALU.mult)
            self.tt(b3, x3[:, :, 32:64], cb_, ALU.mult, e="pool")
            self.tt(o3[:, :, 32:64], a3, b3, ALU.add)
        q, k = qk[:, 0:256], qk[:, 256:512]
    Ra, Rb, Ka, Kc, vb = Hh(), Hh(), Hh(), Hh(), Hh()
    self.cp(Ra[:, 0:256], q, e="act")
    self.tt(_r3(Rb[:, 0:256], 4, 64), _r3(q, 4, 64), _b3(M.tabR[d].all(), 4, 64), ALU.mult)
    self.ts(Ka[:, 0:256], k, 0.125, ALU.mult, e="pool")
    self.stt(_r3(Kc[:, 0:256], 4, 64), _r3(k, 4, 64), 0.125, _b3(M.tabK[d].all(), 4, 64), ALU.mult, ALU.mult)
    self.cp(vb.all(), v, e="act")
    RaT, RbT, KaT = M.TP[0], M.TP[1], M.TP[2]
    for (src, dst) in ((Ra, RaT), (Rb, RbT), (Ka, KaT)):
        m_trT(self, M, src[:, 0:256], 128, 4, 64, dst)
    self.cp(M.SDp.all(), M.SD.all(), e="pool")
    O = F()
    m_core(self, M, 128, 64, 128, 4, False,
           RaT=lambda h: RaT[:64, h, :], RbT=lambda h: RbT[:64, h, :], KaT=lambda h: KaT[:64, h, :],
           BaT=None, CaT=None, CbT=None,
           Kc=lambda h: Kc[:, h * 64:(h + 1) * 64], Bc=None,
           Vv=lambda h: vb[:, h * 128:(h + 1) * 128],
           Mii=lambda h0, n: M.GR[d][:, h0:h0 + n, :], Mei=None,
           f1=M.f1r[d][:64, :], f2=None,
           S=M.SD.all(), S0p=lambda h: M.SDp[:, h, :],
           O=_r3(O.all(), 4, 128))
    m_out(self, M, g, d, t0, 128, 1536, O.all(),
          lambda: m_headnorm_gate(self, M, O.all(), 128, gate, M.rnorm))


def phase_m(self, g, l):
    gn = g["name"]
    L = g["L"]
    nt = L // 128
    self.ofb = {}
    which = [m for m in "ABCD" if ("m" + m) in self.dbg] or list("ABCD")
    states = {"A": ("a", "SA", "h k v -> k h v"), "B": ("b", "SB", "h k v -> k h v"),
              "C": ("c", "SC", "h k v -> k h v"), "D": ("d", "SD", "h k v -> k h v")}
    with ExitStack() as es:
        M = m_alloc(self, es, g, l)
        for seq in range(g["nseq"]):
            for d in range(2):
                for m in which:
                    key, tn, pat = states[m]
                    St = getattr(M, tn)
                    if g["init"]:
                        self.dma(St.all(), self.U(self.st_in[key].t.ap()[l, d].rearrange(pat)))
                    else:
                        self.memset(St.all(), 0.0)
                tiles = range(nt) if d == 0 else range(nt - 1, -1, -1)
                for ti in tiles:
                    pos0 = ti * 128
                    r0 = seq * (L + 2) + 1 + pos0
                    t0 = seq * L + pos0
                    sc = (lambda n: self.nc.named_scope(f"{n}_{gn}{l}_{seq}_{d}_{ti}")) if "scopes" in self.dbg else (lambda n: ExitStack())
                    if "A" in which:
                        with sc("mA"):
                            m_rwkv(self, M, g, l, d, r0, t0)
                    if "B" in which:
                        with sc("mB"):
                            m_hgrn(self, M, g, l, d, r0, t0)
                    if "C" in which:
                        with sc("mC"):
                            m_gdn(self, M, g, l, d, r0, t0)
                    if "D" in which:
                        with sc("mD"):
                            m_ret(self, M, g, l, d, r0, t0, pos0)
                if not g["init"]:
                    for m in which:
                        key, tn, pat = states[m]
                        St = getattr(M, tn)
                        self.dma(self.U(self.ns[key].t.ap()[seq, l, d].rearrange(pat)), St.all())
    self.fw.barrier()


Prog.phase_m = phase_m
```

```python
import numpy as np
import ml_dtypes
from contextlib import ExitStack
import concourse.bass as bass
import concourse.mybir as mybir
from concourse.bass_utils import run_bass_kernel_spmd

F32 = mybir.dt.float32
BF16 = mybir.dt.bfloat16
AF = mybir.ActivationFunctionType
ALU = mybir.AluOpType
AX = mybir.AxisListType

D = 2048
IN_COLS = 8080
A_COLS, B_COLS, C_COLS, D_COLS = 1920, 2560, 2064, 1536
OFF_A, OFF_B, OFF_C, OFF_D = 0, 1920, 4480, 6544
D_FF = 5632
EPS = 1e-6
GN_EPS = 64e-5
NEG = -30000.0


class Buf:
    __slots__ = ("name", "w", "r")

    def __init__(self, name=""):
        self.name = name
        self.w = {}
        self.r = {}


class V:
    __slots__ = ("ap", "b")

    def __init__(self, ap, b):
        self.ap = ap
        self.b = b

    def __getitem__(self, idx):
        return V(self.ap[idx], self.b)


class T:
    def __init__(self, t, name=""):
        self.t = t
        self.b = Buf(name)

    def __getitem__(self, idx):
        return V(self.t[idx], self.b)

    def all(self):
        return V(self.t[:], self.b)


class FW:
    EPOCH = 30000

    def __init__(self, nc):
        self.nc = nc
        self.eng = {"pe": nc.tensor, "dve": nc.vector, "act": nc.scalar, "pool": nc.gpsimd, "sp": nc.sync}
        self.sems = {}
        self.cur = {}
        self.epoch = {e: 0 for e in self.eng}
        for e in self.eng:
            self._new_epoch(e)
        self.waited = {e: {} for e in self.eng}
        self.dmaq = {}
        for q, n in (("sp", 14), ("pool", 8)):
            lst = []
            for i in range(n):
                k = f"d_{q}_{i}"
                self.sems[k] = nc.alloc_semaphore(name=k)
                lst.append([k, 0])
            self.dmaq[q] = [lst, 0]
        self.n_inst = {e: 0 for e in self.eng}

    def _new_epoch(self, e):
        k = f"p_{e}_{self.epoch[e]}"
        self.epoch[e] += 1
        self.sems[k] = self.nc.alloc_semaphore(name=k)
        self.cur[e] = [k, 0]

    def _wait(self, e, deps):
        for k, v in deps.items():
            if self.waited[e].get(k, 0) >= v:
                continue
            if k == self.cur[e][0]:
                if e == "pe":
                    continue
                if v > self.cur[e][1]:
                    continue
            self.eng[e].wait_ge(self.sems[k], v)
            self.waited[e][k] = v

    @staticmethod
    def _merge(dst, src):
        for k, v in src.items():
            if dst.get(k, 0) < v:
                dst[k] = v

    def _collect(self, reads, writes):
        deps = {}
        for b in reads:
            self._merge(deps, b.w)
        for b in writes:
            self._merge(deps, b.w)
            self._merge(deps, b.r)
        return deps

    def op(self, e, fn, reads=(), writes=(), inc=True):
        deps = self._collect(reads, writes)
        self._wait(e, deps)
        ins = fn()
        k, c = self.cur[e]
        if inc:
            c += 1
            ins.then_inc(self.sems[k], 1)
            self.cur[e][1] = c
            dep = {k: c}
        else:
            dep = {k: c + 1}
        for b in writes:
            b.w = dict(dep)
            b.r = {}
        for b in reads:
            self._merge(b.r, dep)
        self.n_inst[e] += 1
        if inc and c >= self.EPOCH:
            self._new_epoch(e)
        return ins

    def dma(self, q, out, in_, **kw):
        deps = self._collect([in_.b], [out.b])
        lst, idx = self.dmaq[q]
        slot = lst[idx % len(lst)]
        self.dmaq[q][1] = idx + 1
        k, v = slot
        if v > 0 and deps.get(k, 0) < v:
            deps[k] = v
        self._wait(q, deps)
        ins = self.eng[q].dma_start(out=out.ap, in_=in_.ap, **kw)
        ins.then_inc(self.sems[k], 16)
        slot[1] = v + 16
        dep = {k: v + 16}
        out.b.w = dict(dep)
        out.b.r = {}
        self._merge(in_.b.r, dep)
        self.n_inst[q] += 1

    def barrier(self):
        deps = {}
        for e in self.eng:
            k, c = self.cur[e]
            if c > 0:
                deps[k] = c
        for q in self.dmaq:
            for k, v in self.dmaq[q][0]:
                if v > 0:
                    deps[k] = v
        for e in self.eng:
            self._wait(e, deps)


def make_consts(dec_seq):
    c = 128
    s = np.arange(c)[:, None]
    t = np.arange(c)[None, :]
    out = {}
    out["ident_f"] = np.eye(c, dtype=np.float32)
    out["ident_b"] = np.eye(c, dtype=np.float32).astype(ml_dtypes.bfloat16)
    out["ones_f"] = np.ones((c, c), np.float32)
    cm = np.zeros((8, c, c), np.float32)
    nm = np.zeros((4, c, c), np.float32)
    dm = np.zeros((2, c, c), np.float32)
    sel3 = np.zeros((2, c, 4), np.float32)
    pos = np.zeros((2, c, 2), np.float32)
    for d in range(2):
        incl = (s <= t) if d == 0 else (s >= t)
        strict = (s < t) if d == 0 else (s > t)
        sel = np.zeros(c, np.float32)
        if d == 0:
            sel[: c // 2] = 1
        else:
            sel[c // 2:] = 1
        cm[d * 4 + 0] = incl
        cm[d * 4 + 1] = strict
        cm[d * 4 + 2] = incl - sel[:, None]
        cm[d * 4 + 3] = strict - sel[:, None]
        nm[d * 2 + 0] = np.where(incl, 0.0, NEG)
        nm[d * 2 + 1] = np.where(strict, 0.0, NEG)
        dist = (t - s) if d == 0 else (s - t)
        dm[d] = np.where(incl, dist, 1.0e6)
        sel3[d, :, 0] = sel
        sel3[d, :, 1] = 1.0
        sel3[d, :, 2] = 1.0 - sel
        tt = np.arange(c)
        pos[d, :, 0] = (tt + 1) if d == 0 else (c - tt)
        pos[d, :, 1] = (c - 1 - tt) if d == 0 else tt
    blk = lambda b: (s // b) == (t // b)
    bm = np.zeros((4, c, c), np.float32)
    bm[0] = blk(16)
    for i, b in enumerate((16, 32, 64)):
        bm[i + 1] = blk(2 * b) & ~blk(b)
    out["bmask"] = bm
    bd = np.zeros((2, c, c), np.float32)
    bd[0] = (s <= t) & blk(32)
    bd[1] = (s >= t) & blk(32)
    out["bd32"] = bd
    out["sub4"] = (np.arange(c)[:, None] // 32 == np.arange(4)[None, :]).astype(np.float32)
    out["colm"] = np.broadcast_to((np.arange(4)[:, None] == (np.arange(c)[None, :] // 32)).astype(np.float32)[None], (c, 4, c)).copy()
    out["cmask"] = cm
    out["nmask"] = nm
    out["dmat"] = dm
    out["sel3"] = sel3
    out["pos"] = pos
    rows = dec_seq // 64
    row = np.broadcast_to(np.arange(rows, dtype=np.float32)[:, None], (rows, 64)).reshape(-1)
    col = np.broadcast_to(np.arange(64, dtype=np.float32)[None, :], (rows, 64)).reshape(-1)
    inv = (10000.0 ** (-np.arange(16, dtype=np.float32) / 16)).astype(np.float32)
    ang = np.concatenate([row[:, None] * inv, col[:, None] * inv], axis=-1).astype(np.float32)
    out["rot_cos"] = np.cos(ang).astype(np.float32)
    out["rot_sin"] = np.sin(ang).astype(np.float32)
    return out


class Cfg:
    def __init__(self, n_pseq=4, pseq=256, dseq=4096, depth=2, debug=None):
        self.n_pseq, self.pseq, self.dseq, self.depth, self.debug = n_pseq, pseq, dseq, depth, debug


W_SPECS = [
    ("ada_w", (D, 6 * D)), ("w_in", (D, IN_COLS)), ("w_merge", (D, 4 * D)), ("w_out", (D, D)),
    ("ffn_up", (D, 2 * D_FF)), ("ffn_down", (D_FF, D)),
]
SMALL = [("ada_b", (6 * D,)), ("norm_mix_pre", (D,)), ("norm_mix_post", (D,)), ("norm_ffn_pre", (D,)),
         ("norm_ffn_post", (D,)), ("rwkv_mu", (2, 1920)), ("rwkv_w_up", (2, 64, 512)), ("rwkv_w0", (2, 512)),
         ("rwkv_a_up", (2, 64, 512)), ("rwkv_a0", (2, 512)), ("rwkv_g_up", (128, 512)), ("rwkv_k_k", (512,)),
         ("rwkv_k_a", (512,)), ("rwkv_r_k", (8, 64)), ("rwkv_ln_w", (512,)), ("rwkv_ln_b", (512,)),
         ("hgrn_lb_logits", (512,)), ("hgrn_norm", (512,)), ("gdn_conv_w", (3, 1536)), ("gdn_a_log", (2, 4)),
         ("gdn_dt_bias", (2, 4)), ("gdn_norm", (512,)), ("ret_decay_logit", (2, 4)), ("ret_norm", (512,)),
         ("ffn_conv_w", (3, 2 * D_FF)), ("ffn_conv_b", (2 * D_FF,))]


class KB:
    def __init__(self, cfg):
        self.cfg = cfg
        nc = self.nc = bass.Bass("TRN2", target_bir_lowering=False)
        self.fw = FW(nc)
        self.E = self.fw.eng
        self.dr = {}
        self.uid = 0

    def dram(self, name, shape, dt, kind="Internal"):
        t = self.nc.dram_tensor(name, list(shape), dt, kind=kind)
        d = T(t, name)
        self.dr[name] = d
        return d

    def sb(self, es, shape, dt, name=None):
        self.uid += 1
        name = f"{name or 's'}_{self.uid}"
        return T(es.enter_context(self.nc.sbuf_tensor(name, list(shape), dt)), name)

    def ps(self, es, shape, dt, name=None):
        self.uid += 1
        name = f"{name or 'p'}_{self.uid}"
        return T(es.enter_context(self.nc.psum_tensor(name, list(shape), dt)), name)

    def mm(self, out, lhsT, rhs, start=True, stop=True, inc=True):
        self.fw.op("pe", lambda: self.nc.tensor.matmul(out.ap, lhsT=lhsT.ap, rhs=rhs.ap, start=start, stop=stop),
                   reads=[lhsT.b, rhs.b], writes=[out.b], inc=inc)

    def tr(self, out, in_, ident, inc=True):
        self.fw.op("pe", lambda: self.nc.tensor.transpose(out=out.ap, in_=in_.ap, identity=ident.ap),
                   reads=[in_.b, ident.b], writes=[out.b], inc=inc)

    def act(self, out, in_, func, scale=1.0, bias=0.0, accum=None):
        reads = [in_.b]
        kw = {}
        if isinstance(scale, V):
            reads.append(scale.b)
            kw["scale"] = scale.ap
        elif scale != 1.0:
            kw["scale"] = scale
        if isinstance(bias, V):
            reads.append(bias.b)
            kw["bias"] = bias.ap
        elif bias != 0.0:
            kw["bias"] = bias
        writes = [out.b]
        if accum is not None:
            kw["accum_out"] = accum.ap
            writes.append(accum.b)
        self.fw.op("act", lambda: self.nc.scalar.activation(out=out.ap, in_=in_.ap, func=func, **kw),
                   reads=reads, writes=writes)

    def _e(self, e):
        return {"dve": self.nc.vector, "pool": self.nc.gpsimd}[e]

    def tt(self, out, a, b, op, e="dve"):
        self.fw.op(e, lambda: self._e(e).tensor_tensor(out=out.ap, in0=a.ap, in1=b.ap, op=op),
                   reads=[a.b, b.b], writes=[out.b])

    def ts(self, out, a, s1, op0, s2=None, op1=None, e="dve", accum=None):
        reads = [a.b]
        kw = {}
        if isinstance(s1, V):
            reads.append(s1.b)
            s1 = s1.ap
        if isinstance(s2, V):
            reads.append(s2.b)
            s2 = s2.ap
        if op1 is not None:
            kw["op1"] = op1
        writes = [out.b]
        if accum is not None:
            kw["accum_out"] = accum.ap
            writes.append(accum.b)
        self.fw.op(e, lambda: self._e(e).tensor_scalar(out=out.ap, in0=a.ap, scalar1=s1, scalar2=s2, op0=op0, **kw),
                   reads=reads, writes=writes)

    def stt(self, out, a, s, b, op0, op1):
        reads = [a.b, b.b]
        if isinstance(s, V):
            reads.append(s.b)
            s = s.ap
        self.fw.op("dve", lambda: self.nc.vector.scalar_tensor_tensor(out=out.ap, in0=a.ap, scalar=s, in1=b.ap,
                                                                       op0=op0, op1=op1),
                   reads=reads, writes=[out.b])

    def cp(self, out, in_, e="dve"):
        if e == "act":
            self.fw.op("act", lambda: self.nc.scalar.copy(out=out.ap, in_=in_.ap), reads=[in_.b], writes=[out.b])
        else:
            self.fw.op(e, lambda: self._e(e).tensor_copy(out=out.ap, in_=in_.ap), reads=[in_.b], writes=[out.b])

    def red(self, out, in_, op=ALU.add, e="dve"):
        self.fw.op(e, lambda: self._e(e).tensor_reduce(out=out.ap, in_=in_.ap, op=op, axis=AX.X),
                   reads=[in_.b], writes=[out.b])

    def recip(self, out, in_):
        self.fw.op("dve", lambda: self.nc.vector.reciprocal(out=out.ap, in_=in_.ap), reads=[in_.b], writes=[out.b])

    def memset(self, out, val, e="dve"):
        self.fw.op(e, lambda: self._e(e).memset(out.ap, val), writes=[out.b])

    def dma(self, out, in_, q="sp", **kw):
        self.fw.dma(q, out, in_, **kw)

    def U(self, ap):
        return V(ap, Buf())

    def rsqrt(self, out, in_, scale, eps):
        self.act(out, in_, AF.Sqrt, scale=scale, bias=eps)
        self.recip(out, out)


class Prog(KB):
    def __init__(self, cfg):
        super().__init__(cfg)
        nc = self.nc
        dbg = cfg.debug or set()
        self.dbg = dbg
        NP, LP, LS, DEP = cfg.n_pseq, cfg.pseq, cfg.dseq, cfg.depth
        self.groups = [dict(name="p", nseq=NP, L=LP, rot=False, init=False),
                       dict(name="s", nseq=1, L=LS, rot=True, init=True)]
        if "only_p" in dbg:
            self.groups = self.groups[:1]
        if "only_s" in dbg:
            self.groups = self.groups[1:]
        ext_in = lambda n, s, dt=F32: self.dram(n, s, dt, kind="ExternalInput")
        ext_out = lambda n, s, dt=F32: self.dram(n, s, dt, kind="ExternalOutput")
        self.x_in = {"p": ext_in("x_p", [NP * LP, D]), "s": ext_in("x_s", [LS, D])}
        self.st_in = {"a": ext_in("st_a", [DEP, 2, 8, 64, 64]), "b": ext_in("st_b", [DEP, 2, 4, 128, 128]),
                      "c": ext_in("st_c", [DEP, 2, 4, 128, 128]), "d": ext_in("st_d", [DEP, 2, 4, 64, 128])}
        self.cvec = ext_in("cvec", [2, D])
        self.w = {}
        for n, s in W_SPECS:
            self.w[n] = ext_in(n, [DEP] + list(s))
        self.w["w_branch"] = ext_in("w_branch", [DEP, 4, 512, D])
        for n, s in SMALL:
            self.w[n] = ext_in(n, [DEP] + list(s))
        self.cst = {}
        for n, a in make_consts(LS).items():
            self.cst[n] = ext_in("c_" + n, list(a.shape), BF16 if a.dtype == ml_dtypes.bfloat16 else F32)
        self.y = {"p": ext_out("y_p", [NP * LP, D]), "s": ext_out("y_s", [LS, D])}
        self.ns = {"a": ext_out("ns_a", [NP, DEP, 2, 8, 64, 64]), "b": ext_out("ns_b", [NP, DEP, 2, 4, 128, 128]),
                   "c": ext_out("ns_c", [NP, DEP, 2, 4, 128, 128]), "d": ext_out("ns_d", [NP, DEP, 2, 4, 64, 128])}
        self.wb = {}
        for n, s in W_SPECS:
            self.wb[n] = self.dram("wb_" + n, [DEP] + list(s), BF16)
        self.wb["w_branch"] = self.dram("wb_w_branch", [DEP, 4, 512, D], BF16)
        kind = lambda f: "ExternalOutput" if f in dbg else "Internal"
        self.scr = {}
        for g in self.groups:
            gn, nt = g["name"], g["nseq"] * g["L"]
            self.scr[gn] = dict(
                proj=self.dram(f"proj_{gn}", [g["nseq"] * (g["L"] + 2), IN_COLS], F32, kind=kind("proj")),
                of=self.dram(f"of_{gn}", [nt, D], F32, kind=kind("of")),
                mix=self.dram(f"mix_{gn}", [nt, D], BF16, kind="ExternalInput" if "mix_in" in dbg else kind("mix")),
                xmid=self.dram(f"xmid_{gn}", [nt, D], F32, kind=kind("xmid")),
                x1=self.dram(f"x1_{gn}", [nt, D], F32, kind=kind("x1")),
                mod=self.dram(f"mod_{gn}", [DEP, 128, 6 * D], F32, kind=kind("mod")),
            )
        self.build()

    def cast_weights(self):
        for n in self.wb:
            src, dst = self.w[n], self.wb[n]
            sh = src.t.shape
            if len(sh) == 4:
                s2 = src.t.ap().rearrange("l n r c -> (l n r) c")
                d2 = dst.t.ap().rearrange("l n r c -> (l n r) c")
            else:
                s2 = src.t.ap().rearrange("l r c -> (l r) c")
                d2 = dst.t.ap().rearrange("l r c -> (l r) c")
            rows = s2.shape[0]
            step = 256
            for r0 in range(0, rows, step):
                r1 = min(rows, r0 + step)
                self.dma(V(d2[r0:r1, :], dst.b), V(s2[r0:r1, :], src.b), q="pool")

    def load_consts(self, es):
        c = {}
        for n in ("ident_f", "ident_b", "ones_f"):
            t = self.sb(es, [128, 128], BF16 if n == "ident_b" else F32, n)
            self.dma(t.all(), V(self.cst[n].t.ap(), self.cst[n].b))
            c[n] = t
        self.C = c

    def bc_load(self, dst, src_ap, src_b, P=128):
        self.dma(dst, V(src_ap.partition_broadcast(P), src_b))

    def setup_mods(self):
        DEP = self.cfg.depth
        with ExitStack() as es:
            cv = self.sb(es, [128, 16], F32, "cv")
            sg = self.sb(es, [128, 16], F32, "sg")
            scb = self.sb(es, [128, 16, 128], BF16, "scb")
            modt = self.sb(es, [128, 6 * D], F32, "modt")
            gt = self.sb(es, [128, D], F32, "gt")
            bt = [self.sb(es, [128, 512], F32, "bt") for _ in range(2)]
            wbuf = [self.sb(es, [128, 16, 512], BF16, "wb") for _ in range(2)]
            pp = [self.ps(es, [128, 512], F32, "pm") for _ in range(2)]
            for g in self.groups:
                gi = 0 if g["name"] == "p" else 1
                with self.nc.allow_non_contiguous_dma(reason="tiny feature-major load of conditioning vector"):
                    self.dma(cv.all(), V(self.cvec.t.ap()[gi, :].rearrange("(k p) -> p k", p=128), self.cvec.b))
                self.act(sg.all(), cv.all(), AF.Sigmoid)
                self.tt(sg.all(), sg.all(), cv.all(), ALU.mult)
                self.cp(scb.all(), V(sg.t[:, :].unsqueeze(2).to_broadcast([128, 16, 128]), sg.b))
                for l in range(DEP):
                    for j in range(24):
                        wt = wbuf[j % 2]
                        srcw = self.wb["ada_w"].t.ap()[l, :, j * 512:(j + 1) * 512].rearrange("(k p) c -> p k c", p=128)
                        merged = {}
                        for pi, k0 in enumerate(range(0, 16, 4)):
                            ov = V(wt.t[:, k0:k0 + 4, :], wt.b if pi == 0 else Buf())
                            self.dma(ov, V(srcw[:, k0:k0 + 4, :], Buf()))
                            FW._merge(merged, ov.b.w)
                        wt.b.w = merged
                        wt.b.r = {}
                        self.bc_load(bt[j % 2].all(), self.w["ada_b"].t.ap()[l, j * 512:(j + 1) * 512], self.w["ada_b"].b)
                        p = pp[j % 2]
                        for k in range(16):
                            self.mm(p.all(), scb[:, k, :], wt[:, k, :], start=(k == 0), stop=(k == 15), inc=(k == 15))
                        self.tt(modt[:, j * 512:(j + 1) * 512], p.all(), bt[j % 2].all(), ALU.add)
                    for (ch, gname) in ((1, "norm_mix_pre"), (4, "norm_ffn_pre")):
                        self.bc_load(gt.all(), self.w[gname].t.ap()[l, :], self.w[gname].b)
                        sl = modt[:, ch * D:(ch + 1) * D]
                        self.stt(sl, sl, 1.0, gt.all(), ALU.add, ALU.mult)
                    for (ch, gname) in ((2, "norm_mix_post"), (5, "norm_ffn_post")):
                        self.bc_load(gt.all(), self.w[gname].t.ap()[l, :], self.w[gname].b)
                        sl = modt[:, ch * D:(ch + 1) * D]
                        self.tt(sl, sl, gt.all(), ALU.mult)
                    md = self.scr[g["name"]]["mod"]
                    self.dma(self.U(md.t.ap()[l]), modt.all())
        self.fw.barrier()

    def norm_rows(self, x, G, sh, u_out, P, tmp, ss, rstd):
        self.act(tmp[:P, :], x, AF.Square, accum=ss[:P, :])
        self.rsqrt(rstd[:P, :], ss[:P, :], 1.0 / D, EPS)
        self.stt(tmp[:P, :], x, rstd[:P, :], G[:P, :], ALU.mult, ALU.mult)
        self.tt(u_out, tmp[:P, :], sh[:P, :], ALU.add, e="pool")

    def to_featT(self, u, P, dst, c0, ptr, KC=16):
        idb = self.C["ident_b"]
        for k0 in range(0, KC, 8):
            pt = ptr[(self.trc) % len(ptr)]
            self.trc += 1
            n = min(8, KC - k0)
            for k in range(n):
                self.tr(pt[:, k, :P], u[:, (k0 + k) * 128:(k0 + k + 1) * 128], idb[:P, :P], inc=(k == n - 1))
            self.cp(dst[:, k0:k0 + n, c0:c0 + P], pt[:, 0:n, :P], e="act")

    def load_w(self, wbufs, Wd, lsel, c0, cw, KC):
        wt = wbufs[self.wrr % len(wbufs)]
        self.wrr += 1
        src = Wd.t.ap()[lsel][:, c0:c0 + cw].rearrange("(k p) c -> p k c", p=128)
        view = V(wt.t[:, 0:KC * cw].rearrange("p (k c) -> p k c", c=cw), wt.b)
        merged = {}
        for pi, k0 in enumerate(range(0, KC, 4)):
            k1 = min(KC, k0 + 4)
            ov = V(view.ap[:, k0:k1, :], wt.b if pi == 0 else Buf())
            self.dma(ov, V(src[:, k0:k1, :], Buf()))
            FW._merge(merged, ov.b.w)
        wt.b.w = merged
        wt.b.r = {}
        return view

    def tile_rows(self, g, ti):
        L = g["L"]
        tps = L // 128
        return ti // tps, (ti % tps) * 128

    def phase_a(self, g, l):
        gn = g["name"]
        nt = g["nseq"] * g["L"]
        xsrc = self.x_in[gn] if l == 0 else self.scr[gn]["x1"]
        proj = self.scr[gn]["proj"]
        mod = self.scr[gn]["mod"]
        TB = min(512, nt)
        nsub = TB // 128
        self.trc = 0
        self.wrr = 0
        with ExitStack() as es:
            G1 = self.sb(es, [128, D], F32, "G1")
            sh1 = self.sb(es, [128, D], F32, "sh1")
            self.dma(sh1.all(), V(mod.t.ap()[l, :, 0:D], mod.b))
            self.dma(G1.all(), V(mod.t.ap()[l, :, D:2 * D], mod.b))
            xt = [self.sb(es, [128, D], F32, "xt") for _ in range(2)]
            tmp = self.sb(es, [128, D], F32, "tmp")
            ss = self.sb(es, [128, 1], F32, "ss")
            rstd = self.sb(es, [128, 1], F32, "rstd")
            ub = [self.sb(es, [128, D], BF16, "ub") for _ in range(2)]
            uT = self.sb(es, [128, 16, TB], BF16, "uT")
            wbufs = [self.sb(es, [128, 8192], BF16, "wbuf") for _ in range(4)]
            stg = [self.sb(es, [128, 512], F32, "stg") for _ in range(4)]
            zrow = self.sb(es, [2, IN_COLS], F32, "zrow")
            ptr = [self.ps(es, [128, 8, 128], BF16, "ptr") for _ in range(2)]
            pbig = [self.ps(es, [128, 512], F32, "pbig") for _ in range(4)]
            self.memset(zrow.all(), 0.0)
            for s in range(g["nseq"]):
                r0 = s * (g["L"] + 2)
                for r in (r0, r0 + g["L"] + 1):
                    self.dma(self.U(proj.t.ap()[r:r + 1, :]), zrow[0:1, :])
            k_ = 0
            for b0 in range(0, nt, TB):
                for sub in range(nsub):
                    t0 = b0 + sub * 128
                    x = xt[sub % 2]
                    self.dma(x.all(), V(xsrc.t.ap()[t0:t0 + 128, :], xsrc.b))
                    u = ub[sub % 2]
                    self.norm_rows(x.all(), G1, sh1, u.all(), 128, tmp, ss, rstd)
                    self.to_featT(u.all(), 128, uT, sub * 128, ptr)
                for c0 in range(0, IN_COLS, 512):
                    cw = min(512, IN_COLS - c0)
                    wv = self.load_w(wbufs, self.wb["w_in"], l, c0, cw, 16)
                    for sub in range(nsub):
                        p = pbig[k_ % 4]
                        st = stg[k_ % 4]
                        k_ += 1
                        for k in range(16):
                            self.mm(p[:, 0:cw], uT[:, k, sub * 128:(sub + 1) * 128], wv[:, k, :],
                                    start=(k == 0), stop=(k == 15), inc=(k == 15))
                        self.cp(st[:, 0:cw], p[:, 0:cw], e=("act" if k_ % 2 else "dve"))
                        seq, pos = self.tile_rows(g, (b0 + sub * 128) // 128)
                        r0 = seq * (g["L"] + 2) + 1 + pos
                        self.dma(self.U(proj.t.ap()[r0:r0 + 128, c0:c0 + cw]), st[:, 0:cw])
        self.fw.barrier()

    def phase_c1(self, g, l):
        gn = g["name"]
        nt = g["nseq"] * g["L"]
        xsrc = self.x_in[gn] if l == 0 else self.scr[gn]["x1"]
        mix, xmid, mod = self.scr[gn]["mix"], self.scr[gn]["xmid"], self.scr[gn]["mod"]
        TB = min(256, nt)
        nsub = TB // 128
        self.trc = 0
        self.wrr = 0
        with ExitStack() as es:
            sh1 = self.sb(es, [128, D], F32, "sh1")
            G1 = self.sb(es, [128, D], F32, "G1")
            GP1 = self.sb(es, [128, D], F32, "GP1")
            self.dma(sh1.all(), self.U(mod.t.ap()[l, :, 0:D]))
            self.dma(G1.all(), self.U(mod.t.ap()[l, :, D:2 * D]))
            self.dma(GP1.all(), self.U(mod.t.ap()[l, :, 2 * D:3 * D]))
            xres = [self.sb(es, [128, D], F32, "xres") for _ in range(nsub)]
            ytmp = [self.sb(es, [128, D], F32, "ytmp") for _ in range(nsub)]
            tmp = self.sb(es, [128, D], F32, "tmp")
            ss = self.sb(es, [128, 1], F32, "ss")
            rstd = self.sb(es, [128, 1], F32, "rstd")
            ub = [self.sb(es, [128, D], BF16, "ub") for _ in range(2)]
            uT = self.sb(es, [128, 16, TB], BF16, "uT")
            mixT = self.sb(es, [128, 16, TB], BF16, "mixT")
            hT = self.sb(es, [128, 16, TB], F32, "hT")
            hTb = self.sb(es, [128, 16, TB], BF16, "hTb")
            gsb = [self.sb(es, [128, TB], F32, "gsb") for _ in range(2)]
            tsb = [self.sb(es, [128, TB], F32, "tsb") for _ in range(2)]
            wbufs = [self.sb(es, [128, 8192], BF16, "wbuf") for _ in range(2)]
            wbr = [self.sb(es, [128, 8192], BF16, "wbr") for _ in range(2)]
            ptr = [self.ps(es, [128, 8, 128], BF16, "ptr") for _ in range(2)]
            pbig = [self.ps(es, [128, 512], F32, "pbig") for _ in range(6)]
            k_ = 0
            for b0 in range(0, nt, TB):
                for sub in range(nsub):
                    t0 = b0 + sub * 128
                    self.dma(xres[sub].all(), self.U(xsrc.t.ap()[t0:t0 + 128, :]))
                    u = ub[sub % 2]
                    self.norm_rows(xres[sub].all(), G1, sh1, u.all(), 128, tmp, ss, rstd)
                    self.to_featT(u.all(), 128, uT, sub * 128, ptr)
                for sub in range(nsub):
                    t0 = b0 + sub * 128
                    u = ub[sub % 2]
                    self.dma(u.all(), self.U(mix.t.ap()[t0:t0 + 128, :]))
                    self.to_featT(u.all(), 128, mixT, sub * 128, ptr)
                for n in range(4):
                    wb_t = wbr[n % 2]
                    wbv = V(wb_t.t[:, :].rearrange("p (j c) -> p j c", c=D), wb_t.b)
                    self.dma(wbv, self.U(self.wb["w_branch"].t.ap()[l, n].rearrange("(j p) c -> p j c", p=128)))
                    for cg in range(4):
                        wv = self.load_w(wbufs, self.wb["w_merge"], l, n * D + cg * 512, 512, 16)
                        for f4 in range(4):
                            fo = cg * 4 + f4
                            p1 = pbig[k_ % 6]
                            p2 = pbig[(k_ + 1) % 6]
                            gs = gsb[(k_ // 2) % 2]
                            tsq = tsb[(k_ // 2) % 2]
                            k_ += 2
                            for k in range(16):
                                self.mm(p1[:, 0:TB], wv[:, k, f4 * 128:(f4 + 1) * 128], uT[:, k, :],
                                        start=(k == 0), stop=(k == 15), inc=(k == 15))
                            for j in range(4):
                                self.mm(p2[:, 0:TB], wbv[:, j, fo * 128:(fo + 1) * 128], mixT[:, n * 4 + j, :],
                                        start=(j == 0), stop=(j == 3), inc=(j == 3))
                            self.act(gs.all(), p1[:, 0:TB], AF.Sigmoid)
                            if n == 0:
                                self.tt(hT[:, fo, :], gs.all(), p2[:, 0:TB], ALU.mult)
                            else:
                                self.tt(tsq.all(), gs.all(), p2[:, 0:TB], ALU.mult)
                                self.tt(hT[:, fo, :], hT[:, fo, :], tsq.all(), ALU.add, e="pool")
                for q4 in range(4):
                    self.cp(hTb[:, q4 * 4:(q4 + 1) * 4, :], hT[:, q4 * 4:(q4 + 1) * 4, :], e=("act" if q4 % 2 else "dve"))
                for cg in range(4):
                    wv = self.load_w(wbufs, self.wb["w_out"], l, cg * 512, 512, 16)
                    for sub in range(nsub):
                        p = pbig[k_ % 6]
                        k_ += 1
                        for k in range(16):
                            self.mm(p.all(), hTb[:, k, sub * 128:(sub + 1) * 128], wv[:, k, :],
                                    start=(k == 0), stop=(k == 15), inc=(k == 15))
                        self.cp(ytmp[sub][:, cg * 512:(cg + 1) * 512], p.all(), e=("act" if k_ % 2 else "dve"))
                for sub in range(nsub):
                    t0 = b0 + sub * 128
                    self.residual(xres[sub], ytmp[sub], GP1, tmp, ss, rstd)
                    self.dma(self.U(xmid.t.ap()[t0:t0 + 128, :]), xres[sub].all())
        self.fw.barrier()

    def residual(self, xres, y, GP, tmp, ss, rstd):
        self.act(tmp.all(), y.all(), AF.Square, accum=ss.all())
        self.rsqrt(rstd.all(), ss.all(), 1.0 / D, EPS)
        self.stt(tmp.all(), y.all(), rstd.all(), GP.all(), ALU.mult, ALU.mult)
        self.tt(xres.all(), xres.all(), tmp.all(), ALU.add, e="pool")

    def phase_c2(self, g, l, last):
        gn = g["name"]
        L = g["L"]
        xmid, mod = self.scr[gn]["xmid"], self.scr[gn]["mod"]
        dst = self.y[gn] if last else self.scr[gn]["x1"]
        TBL = min(256, L)
        nsub = TBL // 128
        NH = 1 if TBL + 2 <= 512 else 2
        HW = (TBL + 2) // NH
        self.trc = 0
        self.wrr = 0
        with ExitStack() as es:
            sh2 = self.sb(es, [128, D], F32, "sh2")
            G2 = self.sb(es, [128, D], F32, "G2")
            GP2 = self.sb(es, [128, D], F32, "GP2")
            self.dma(sh2.all(), self.U(mod.t.ap()[l, :, 3 * D:4 * D]))
            self.dma(G2.all(), self.U(mod.t.ap()[l, :, 4 * D:5 * D]))
            self.dma(GP2.all(), self.U(mod.t.ap()[l, :, 5 * D:6 * D]))
            cw = self.sb(es, [128, 3, 88], F32, "cw")
            cb = self.sb(es, [128, 88], F32, "cb")
            with self.nc.allow_non_contiguous_dma(reason="one-off feature-major load of depthwise conv params"):
                for j in range(3):
                    self.dma(cw[:, j, :], self.U(self.w["ffn_conv_w"].t.ap()[l, j, :].rearrange("(k p) -> p k", p=128)))
                self.dma(cb.all(), self.U(self.w["ffn_conv_b"].t.ap()[l, :].rearrange("(k p) -> p k", p=128)))
            xres = [self.sb(es, [128, D], F32, "xres") for _ in range(nsub)]
            ytmp = [self.sb(es, [128, D], F32, "ytmp") for _ in range(nsub)]
            tmp = self.sb(es, [128, D], F32, "tmp")
            ss = self.sb(es, [128, 1], F32, "ss")
            rstd = self.sb(es, [128, 1], F32, "rstd")
            ub = [self.sb(es, [128, D], BF16, "ub") for _ in range(2)]
            hl = self.sb(es, [2, D], F32, "hl")
            hu = self.sb(es, [2, D], BF16, "hu")
            hT2 = self.sb(es, [128, 16, 2], BF16, "hT2")
            u2T = self.sb(es, [128, 16, TBL + 2], BF16, "u2T")
            aT = self.sb(es, [128, 44, TBL], BF16, "aT")
            hsb = [[self.sb(es, [128, TBL + 2], F32, "hsb") for _ in range(2)] for _ in range(2)]
            cv = [[self.sb(es, [128, TBL], F32, "cv") for _ in range(2)] for _ in range(2)]
            g1 = [self.sb(es, [128, TBL], F32, "g1") for _ in range(2)]
            g2 = [self.sb(es, [128, TBL], F32, "g2") for _ in range(2)]
            wbufs = [self.sb(es, [128, 11264], BF16, "wbuf") for _ in range(3)]
            ptr = [self.ps(es, [128, 8, 128], BF16, "ptr") for _ in range(2)]
            pbig = [self.ps(es, [128, 512], F32, "pbig") for _ in range(6)]
            k_ = 0
            it = 0
            for seq in range(g["nseq"]):
                for p0 in range(0, L, TBL):
                    tb0 = seq * L + p0
                    for sub in range(nsub):
                        t0 = tb0 + sub * 128
                        self.dma(xres[sub].all(), self.U(xmid.t.ap()[t0:t0 + 128, :]))
                        u = ub[sub % 2]
                        self.norm_rows(xres[sub].all(), G2, sh2, u.all(), 128, tmp, ss, rstd)
                        self.to_featT(u.all(), 128, u2T, 1 + sub * 128, ptr)
                    has_l, has_r = p0 > 0, p0 + TBL < L
                    self.memset(hl.all(), 0.0)
                    if has_l:
                        self.dma(hl[0:1, :], self.U(xmid.t.ap()[tb0 - 1:tb0, :]))
                    if has_r:
                        self.dma(hl[1:2, :], self.U(xmid.t.ap()[tb0 + TBL:tb0 + TBL + 1, :]))
                    self.norm_rows(hl.all(), G2, sh2, hu.all(), 2, tmp, ss, rstd)
                    self.to_featT(hu.all(), 2, hT2, 0, ptr)
                    for (c_src, c_dst, ok) in ((0, 0, has_l), (1, TBL + 1, has_r)):
                        if ok:
                            self.cp(u2T[:, :, c_dst:c_dst + 1], hT2[:, :, c_src:c_src + 1])
                        else:
                            self.memset(u2T[:, :, c_dst:c_dst + 1], 0.0)
                    for cgp in range(11):
                        wvs = [self.load_w(wbufs, self.wb["ffn_up"], l, cgp * 512, 512, 16),
                               self.load_w(wbufs, self.wb["ffn_up"], l, D_FF + cgp * 512, 512, 16)]
                        for f4 in range(4):
                            fo = cgp * 4 + f4
                            par = it % 2
                            it += 1
                            for wi in range(2):
                                ch = fo + 44 * wi
                                hs = hsb[wi][par]
                                for half in range(NH):
                                    p = pbig[k_ % 6]
                                    k_ += 1
                                    for k in range(16):
                                        self.mm(p[:, 0:HW], wvs[wi][:, k, f4 * 128:(f4 + 1) * 128],
                                                u2T[:, k, half * HW:(half + 1) * HW],
                                                start=(k == 0), stop=(k == 15), inc=(k == 15))
                                    self.cp(hs[:, half * HW:(half + 1) * HW], p[:, 0:HW], e="act")
                                c = cv[wi][par]
                                self.ts(c.all(), hs[:, 0:TBL], cw[:, 0, ch:ch + 1], ALU.mult, cb[:, ch:ch + 1], ALU.add)
                                self.stt(c.all(), hs[:, 1:TBL + 1], cw[:, 1, ch:ch + 1], c.all(), ALU.mult, ALU.add)
                                self.stt(c.all(), hs[:, 2:TBL + 2], cw[:, 2, ch:ch + 1], c.all(), ALU.mult, ALU.add)
                            val, gt = cv[0][par], cv[1][par]
                            a1, a2 = g1[par], g2[par]
                            self.tt(a1.all(), gt.all(), gt.all(), ALU.mult, e="pool")
                            self.ts(a1.all(), a1.all(), 0.044715, ALU.mult, 1.0, ALU.add, e="pool")
                            self.tt(a1.all(), a1.all(), gt.all(), ALU.mult, e="pool")
                            self.act(a2.all(), a1.all(), AF.Sigmoid, scale=1.5957691216057308)
                            self.tt(a2.all(), a2.all(), gt.all(), ALU.mult, e="pool")
                            self.tt(aT[:, fo, :], a2.all(), val.all(), ALU.mult, e="pool")
                    for cg in range(8):
                        wv = self.load_w(wbufs, self.wb["ffn_down"], l, cg * 256, 256, 44)
                        for sub in range(nsub):
                            p = pbig[k_ % 6]
                            k_ += 1
                            for k in range(44):
                                self.mm(p[:, 0:256], aT[:, k, sub * 128:(sub + 1) * 128], wv[:, k, :],
                                        start=(k == 0), stop=(k == 43), inc=(k == 43))
                            self.cp(ytmp[sub][:, cg * 256:(cg + 1) * 256], p[:, 0:256], e=("act" if k_ % 2 else "dve"))
                    for sub in range(nsub):
                        t0 = tb0 + sub * 128
                        self.residual(xres[sub], ytmp[sub], GP2, tmp, ss, rstd)
                        self.dma(self.U(dst.t.ap()[t0:t0 + 128, :]), xres[sub].all())
        self.fw.barrier()

    def build(self):
        dbg = self.dbg
        with ExitStack() as es:
            sc = (lambda n: self.nc.named_scope(n)) if "scopes" in dbg else (lambda n: ExitStack())
            self.load_consts(es)
            with sc("cast"):
                self.cast_weights()
                self.fw.barrier()
            with sc("mods"):
                self.setup_mods()
            for g in self.groups:
                for l in range(self.cfg.depth):
                    with sc(f"A_{g['name']}{l}"):
                        self.phase_a(g, l)
                    if "mix_in" not in dbg:
                        with sc(f"M_{g['name']}{l}"):
                            self.phase_m(g, l)
                    with sc(f"C1_{g['name']}{l}"):
                        self.phase_c1(g, l)
                    with sc(f"C2_{g['name']}{l}"):
                        self.phase_c2(g, l, last=(l == self.cfg.depth - 1))
            self.fw.barrier()


_PROG_CACHE = {}


def _prog(cfg_key):
    if cfg_key not in _PROG_CACHE:
        _PROG_CACHE[cfg_key] = Prog(Cfg(*cfg_key))
    return _PROG_CACHE[cfg_key]


def run(inputs, n_cores=8, debug=None, extra_in=None):
    x_prompt, x_sample = inputs["x_prompt"], inputs["x_sample"]
    B, LP = x_prompt.shape[0], x_prompt.shape[1]
    DB, LS = x_sample.shape[0], x_sample.shape[1]
    depth = inputs["w_in"].shape[0]
    NP = B // n_cores
    prog = _prog((NP, LP, LS, depth, frozenset(debug) if debug else None))
    consts = make_consts(LS)
    f32 = lambda a: np.ascontiguousarray(np.asarray(a, dtype=np.float32))
    shared = {}
    for n, _ in W_SPECS:
        shared[n] = f32(inputs[n])
    shared["w_branch"] = f32(inputs["w_branch"])
    for n, _ in SMALL:
        shared[n] = f32(inputs[n])
    for n, a in consts.items():
        shared["c_" + n] = a
    in_maps = []
    for c in range(n_cores):
        b = c % DB
        m = dict(shared)
        m["x_p"] = f32(x_prompt[c * NP:(c + 1) * NP]).reshape(NP * LP, D)
        m["x_s"] = f32(x_sample[b])
        m["st_a"] = f32(inputs["state_rwkv"][b])
        m["st_b"] = f32(inputs["state_hgrn"][b])
        m["st_c"] = f32(inputs["state_gdn"][b])
        m["st_d"] = f32(inputs["state_ret"][b])
        m["cvec"] = np.stack([f32(inputs["c_ctx"]), f32(inputs["c"][b])], 0)
        if extra_in:
            for k, v in extra_in(c).items():
                m[k] = v
        in_maps.append(m)
    res = run_bass_kernel_spmd(prog.nc, in_maps, core_ids=list(range(n_cores)))
    return res.results


def kernel(**inputs):
    r = run(inputs, 8)
    B, LP = inputs["x_prompt"].shape[:2]
    DB, LS = inputs["x_sample"].shape[:2]
    NP = B // 8
    y_p = np.concatenate([r[c]["y_p"].reshape(NP, LP, D) for c in range(8)], 0)
    y_s = np.stack([r[b]["y_s"] for b in range(DB)], 0)
    outs = [y_p.astype(np.float32), y_s.astype(np.float32)]
    for k in "abcd":
        outs.append(np.concatenate([r[c]["ns_" + k] for c in range(8)], 0).astype(np.float32))
    return tuple(outs)


class _NS:
    pass


def _b3(v, n, w):
    return V(v.ap.unsqueeze(2).to_broadcast([v.ap.shape[0], n, w]), v.b)


def _r3(v, n, w):
    return V(v.ap.rearrange("p (n w) -> p n w", w=w), v.b)


def m_alloc(self, es, g, l):
    M = _NS()
    nc = self.nc
    M.es = es
    sb = lambda shape, dt, n: self.sb(es, shape, dt, n)
    M.cm = [sb([128, 128], F32, "cm") for _ in range(8)]
    for i in range(8):
        self.dma(M.cm[i].all(), self.U(self.cst["cmask"].t.ap()[i]))
    M.nm = [sb([128, 128], F32, "nm") for _ in range(4)]
    for i in range(4):
        self.dma(M.nm[i].all(), self.U(self.cst["nmask"].t.ap()[i]))
    M.bm = [sb([128, 128], F32, "bm") for _ in range(4)]
    for i in range(4):
        self.dma(M.bm[i].all(), self.U(self.cst["bmask"].t.ap()[i]))
    M.dmat = [sb([128, 128], F32, "dmat") for _ in range(2)]
    M.sel3 = [sb([128, 4], F32, "sel3") for _ in range(2)]
    M.pos = [sb([128, 2], F32, "pos") for _ in range(2)]
    for d in range(2):
        self.dma(M.dmat[d].all(), self.U(self.cst["dmat"].t.ap()[d]))
        self.dma(M.sel3[d].all(), self.U(self.cst["sel3"].t.ap()[d]))
        self.dma(M.pos[d].all(), self.U(self.cst["pos"].t.ap()[d]))
    M.maskI = lambda d: M.cm[d * 4 + 0]
    M.maskS = lambda d: M.cm[d * 4 + 1]
    M.TmS = lambda d: M.cm[d * 4 + 2]
    M.TsS = lambda d: M.cm[d * 4 + 3]
    M.nmI = lambda d: M.nm[d * 2 + 0]
    M.nmS = lambda d: M.nm[d * 2 + 1]

    def bc(name, idx, n, dt=F32, P=128, q="sp"):
        t = sb([P, n], dt, "bc_" + name)
        src = self.w[name].t.ap()[l]
        for i in idx:
            src = src[i]
        self.dma(t.all(), self.U(src.partition_broadcast(P)), q=q)
        return t
    M.bc = bc
    M.mu = [bc("rwkv_mu", (d,), 1920, BF16, q="pool") for d in range(2)]
    M.k_k = bc("rwkv_k_k", (), 512)
    M.k_a = bc("rwkv_k_a", (), 512)
    t = sb([128, 512], F32, "bc_r_k")
    self.dma(t.all(), self.U(self.w["rwkv_r_k"].t.ap()[l].rearrange("h k -> (h k)").partition_broadcast(128)))
    M.r_k = t
    M.ln_w = bc("rwkv_ln_w", (), 512)
    M.ln_b = bc("rwkv_ln_b", (), 512)
    M.w0 = [bc("rwkv_w0", (d,), 512) for d in range(2)]
    M.a0 = [bc("rwkv_a0", (d,), 512) for d in range(2)]
    M.w_up, M.a_up = [], []
    for d in range(2):
        for (lst, nm_) in ((M.w_up, "rwkv_w_up"), (M.a_up, "rwkv_a_up")):
            t = sb([64, 512], BF16, nm_)
            self.dma(t.all(), self.U(self.w[nm_].t.ap()[l, d]), q="pool")
            lst.append(t)
    M.g_up = sb([128, 512], BF16, "g_up")
    self.dma(M.g_up.all(), self.U(self.w["rwkv_g_up"].t.ap()[l]), q="pool")
    M.hnorm = bc("hgrn_norm", (), 512)
    M.lb = sb([128, 512], F32, "lb")
    M.oml = sb([128, 512], F32, "oml")
    M.bd32 = [sb([128, 128], F32, "bd32") for _ in range(2)]
    for d_ in range(2):
        self.dma(M.bd32[d_].all(), self.U(self.cst["bd32"].t.ap()[d_]))
    M.sub4 = sb([128, 4], F32, "sub4")
    self.dma(M.sub4.all(), self.U(self.cst["sub4"].t.ap()))
    M.colm = sb([128, 4, 128], F32, "colm")
    self.dma(M.colm.all(), self.U(self.cst["colm"].t.ap()))
    M.rtTm = sb([128, 4, 4, 128], BF16, "rtTm")
    M.ktm = sb([128, 4, 512], BF16, "ktm")
    M.S0pAll = sb([128, 4, 4, 128], BF16, "S0pAll")
    if l == 0:
        self.memset(M.lb.all(), 0.0)
        self.memset(M.oml.all(), 1.0)
    else:
        self.dma(M.lb.all(), self.U(self.w["hgrn_lb_logits"].t.ap()[1].partition_broadcast(128)))
        self.dma(M.oml.all(), self.U(self.w["hgrn_lb_logits"].t.ap()[0].partition_broadcast(128)))
        self.tt(M.lb.all(), M.lb.all(), M.oml.all(), ALU.subtract)
        self.act(M.lb.all(), M.lb.all(), AF.Sigmoid)
        self.ts(M.oml.all(), M.lb.all(), -1.0, ALU.mult, 1.0, ALU.add)
    M.cw = [bc("gdn_conv_w", (j,), 1536, BF16, q="pool") for j in range(3)]
    M.gnorm = bc("gdn_norm", (), 512)
    t = sb([128, 8], F32, "negA")
    self.dma(t.all(), self.U(self.w["gdn_a_log"].t.ap()[l].rearrange("d h -> (d h)").partition_broadcast(128)))
    self.act(t.all(), t.all(), AF.Exp)
    self.ts(t.all(), t.all(), -1.0, ALU.mult)
    M.negA = t
    t = sb([128, 8], F32, "dtb")
    self.dma(t.all(), self.U(self.w["gdn_dt_bias"].t.ap()[l].rearrange("d h -> (d h)").partition_broadcast(128)))
    M.dtb = t
    M.rnorm = bc("ret_norm", (), 512)
    lg = sb([128, 8], F32, "lg")
    self.dma(lg.all(), self.U(self.w["ret_decay_logit"].t.ap()[l].rearrange("d h -> (d h)").partition_broadcast(128)))
    self.act(lg.all(), lg.all(), AF.Exp, scale=-1.0)
    self.act(lg.all(), lg.all(), AF.Ln, bias=1.0)
    self.ts(lg.all(), lg.all(), -1.0, ALU.mult)
    M.lg = lg
    M.tabR, M.tabK, M.f1r, M.GR = [], [], [], []
    for d in range(2):
        tr_ = sb([128, 4], F32, "tabR")
        tk_ = sb([128, 4], F32, "tabK")
        f1_ = sb([128, 4], F32, "f1r")
        self.ts(tr_.all(), lg[:, 4 * d:4 * d + 4], M.pos[d][:, 0:1], ALU.mult)
        self.act(tr_.all(), tr_.all(), AF.Exp)
        self.ts(tk_.all(), lg[:, 4 * d:4 * d + 4], M.pos[d][:, 1:2], ALU.mult)
        self.act(tk_.all(), tk_.all(), AF.Exp)
        self.act(f1_.all(), lg[:, 4 * d:4 * d + 4], AF.Exp, scale=128.0)
        M.tabR.append(tr_)
        M.tabK.append(tk_)
        M.f1r.append(f1_)
        gr = sb([128, 4, 128], F32, "GR")
        for h in range(4):
            self.act(gr[:, h, :], M.dmat[d].all(), AF.Exp, scale=lg[:, 4 * d + h:4 * d + h + 1])
        M.GR.append(gr)
    M.W = [sb([128, 1920], F32, "W") for _ in range(3)]
    M.F = [sb([128, 512], F32, "F") for _ in range(13)]
    M.H = [sb([128, 512], BF16, "H") for _ in range(10)]
    M.TP = [sb([128, 8, 128], BF16, "TP") for _ in range(6)]
    M.sm = [sb([128, 16], F32, "sm") for _ in range(6)]
    M.lT = [sb([128, 128], BF16, "lT") for _ in range(3)]
    M.cols = sb([128, 32], F32, "cols")
    M.Gia = sb([128, 4, 128], F32, "Gia")
    M.Gea = sb([128, 4, 128], F32, "Gea")
    M.Gb = [sb([128, 128], F32, "Gb") for _ in range(2)]
    M.rot = [sb([128, 32], F32, "rot") for _ in range(2)]
    mk = lambda n: sb([128, 8, 128], BF16, n)
    M.AkT, M.NT, M.AbT, M.Lf, M.LT, M.TT, M.NoT, M.A1 = [mk(n) for n in ("AkT", "NT", "AbT", "Lf", "LT", "TT", "NoT", "A1")]
    M.Q = [mk("Q0"), mk("Q1")]
    M.QT = [mk("QT0"), mk("QT1")]
    M.P = [mk("P0"), mk("P1")]
    M.Xa = sb([128, 512], BF16, "Xa")
    M.nEa = sb([128, 512], BF16, "nEa")
    M.tSa = sb([128, 512], F32, "tSa")
    M.SA = sb([64, 8, 64], F32, "SA")
    M.SAp = sb([64, 8, 64], BF16, "SAp")
    M.SB = sb([128, 4, 128], F32, "SB")
    M.SC = sb([128, 4, 128], F32, "SC")
    M.SCp = sb([128, 4, 128], BF16, "SCp")
    M.SD = sb([64, 4, 128], F32, "SD")
    M.SDp = sb([64, 4, 128], BF16, "SDp")
    M.pw = [self.ps(es, [128, 512], F32, "pw") for _ in range(2)]
    M.ptr = [self.ps(es, [128, 8, 128], BF16, "ptr") for _ in range(2)]
    M.slots = []
    for i in range(4):
        bank = es.enter_context(nc.psum_tensor(f"slotbank_{l}_{g['name']}_{i}", [128, 4, 128], F32))
        M.slots.append(V(bank[:, :, :], Buf()))
    M.si = 0
    M.pti = 0
    M.pwi = 0
    M.fi = 0
    M.hi = 0
    return M


def m_slot(self, M):
    s = M.slots[M.si % len(M.slots)]
    M.si += 1
    return s


def m_pw(self, M):
    p = M.pw[M.pwi % 2]
    M.pwi += 1
    return p


def m_trT(self, M, src, c, nblk, w, dst):
    idb = self.C["ident_b"]
    pt = M.ptr[M.pti % 2]
    M.pti += 1
    for k in range(nblk):
        self.tr(pt[:w, k, :c], src[:, k * w:(k + 1) * w], idb[:c, :c], inc=(k == nblk - 1))
    self.cp(dst[:w, 0:nblk, :c], pt[:w, 0:nblk, :c], e="act")


def _bm(mask_v, c, n):
    return V(mask_v.ap.unsqueeze(1).to_broadcast([c, n, c]), mask_v.b)


def m_core(self, M, c, K, Vd, Hn, delta, RaT, RbT, KaT, BaT, CaT, CbT, Kc, Bc, Vv, Mii, Mei, f1, f2, S, S0p, O):
    idb = self.C["ident_b"]
    gsz = 4
    groups = [(h0, min(gsz, Hn - h0)) for h0 in range(0, Hn, gsz)]
    sl = lambda: m_slot(self, M)

    def vbank():
        b = sl()
        return V(b.ap.rearrange("p a b -> p (a b)").rearrange("p (h v) -> p h v", v=Vd), b.b)

    def scores(dst, lf, rf, mult, neg=False):
        for (h0, n) in groups:
            b = sl()
            for i in range(n):
                self.mm(b[:c, i, :c], lf(h0 + i), rf(h0 + i), inc=(i == n - 1))
            if neg:
                self.stt(dst[:c, h0:h0 + n, :c], b[:c, 0:n, :c], -1.0, mult(h0, n), ALU.mult, ALU.mult)
            else:
                self.tt(dst[:c, h0:h0 + n, :c], b[:c, 0:n, :c], mult(h0, n), ALU.mult)

    def mm_stage(dst, lt, rt, eng):
        for gi, (h0, n) in enumerate(groups):
            b = sl()
            for i in range(n):
                self.mm(b[:, i, :], lt[:, h0 + i, :], rt[:, h0 + i, :], inc=(i == n - 1))
            self.cp(dst[:, h0:h0 + n, :], b[:, 0:n, :], e=(eng if gi % 2 == 0 else ("dve" if eng == "act" else "act")))

    def upd_stage(dst, lt, rt, base):
        for gi, (h0, n) in enumerate(groups):
            b = sl()
            for i in range(n):
                self.mm(b[:, i, :], lt[:, h0 + i, :], rt[:, h0 + i, :], inc=(i == n - 1))
            self.tt(dst[:, h0:h0 + n, :], base[:, h0:h0 + n, :], b[:, 0:n, :], ALU.add)

    def transp(dst, src):
        for h0 in range(0, Hn, 8):
            n = min(8, Hn - h0)
            pt = M.ptr[M.pti % 2]
            M.pti += 1
            for i in range(n):
                self.tr(pt[:, i, :], src[:, h0 + i, :], idb.all(), inc=(i == n - 1))
            self.cp(dst[:, h0:h0 + n, :], pt[:, 0:n, :], e="act")

    scores(M.AkT, KaT, RaT, Mii)
    if delta:
        assert c == 128
        scores(M.NT, KaT, CaT, Mei)
        scores(M.Lf, BaT, CaT, Mei, neg=True)
        transp(M.LT, M.Lf)
        hs = slice(0, Hn)
        self.tt(M.Q[0][:, hs, :], M.Lf[:, hs, :], _bm(M.bm[0].all(), 128, Hn), ALU.mult)
        self.tt(M.QT[0][:, hs, :], M.LT[:, hs, :], _bm(M.bm[0].all(), 128, Hn), ALU.mult, e="pool")
        self.tt(M.P[0][:, hs, :], M.Q[0][:, hs, :], _bm(idb.all(), 128, Hn), ALU.add, e="pool")
        cur = 0
        for i in range(1, 4):
            nxt = 1 - cur
            if i < 3:
                mm_stage(M.Q[nxt], M.QT[cur], M.Q[cur], "dve")
            mm_stage(M.QT[nxt], M.Q[cur], M.QT[cur], "act")
            upd_stage(M.P[nxt], M.QT[nxt], M.P[cur], M.P[cur])
            cur = nxt
        transp(M.TT, M.P[cur])
        for bi in range(3):
            nxt = 1 - cur
            self.tt(M.NoT[:, hs, :], M.LT[:, hs, :], _bm(M.bm[bi + 1].all(), 128, Hn), ALU.mult, e="pool")
            mm_stage(M.A1, M.NoT, M.P[cur], "act")
            upd_stage(M.P[nxt], M.TT, M.A1, M.P[cur])
            cur = nxt
            if bi < 2:
                transp(M.TT, M.P[cur])
        Pf = M.P[cur]
        Xa = V(M.Xa.t[:, :].rearrange("p (h v) -> p h v", v=Vd), M.Xa.b)
        nE = V(M.nEa.t[:, :].rearrange("p (h v) -> p h v", v=Vd), M.nEa.b)
        bv = vbank()
        for h in range(Hn):
            self.mm(bv[:c, h, :], CbT(h), S0p(h), start=True, stop=False, inc=False)
            self.mm(bv[:c, h, :], M.NT[:c, h, :c], Vv(h), start=False, stop=True, inc=(h == Hn - 1))
        self.cp(Xa[:c, :, :], bv[:c, :, :], e="act")
        bv = vbank()
        for h in range(Hn):
            self.mm(bv[:c, h, :], Pf[:c, h, :c], Xa[:c, h, :], inc=(h == Hn - 1))
        self.ts(nE[:c, :, :], bv[:c, :, :], -1.0, ALU.mult)
        scores(M.AbT, BaT, RaT, Mii)
    bv = vbank()
    for h in range(Hn):
        self.mm(bv[:c, h, :], RbT(h), S0p(h), start=True, stop=False, inc=False)
        self.mm(bv[:c, h, :], M.AkT[:c, h, :c], Vv(h), start=False, stop=(not delta), inc=((not delta) and h == Hn - 1))
        if delta:
            self.mm(bv[:c, h, :], M.AbT[:c, h, :c], nE[:c, h, :], start=False, stop=True, inc=(h == Hn - 1))
    self.cp(O, bv[:c, :, :], e="act")
    bv = vbank()
    for h in range(Hn):
        self.mm(bv[:K, h, :], Kc(h), Vv(h), start=True, stop=(not delta), inc=((not delta) and h == Hn - 1))
        if delta:
            self.mm(bv[:K, h, :], Bc(h), nE[:c, h, :], start=False, stop=True, inc=(h == Hn - 1))
    f1b = _b3(f1, Hn, Vd)
    if f2 is not None:
        tS = V(M.tSa.t[:K, :].rearrange("p (h v) -> p h v", v=Vd), M.tSa.b)
        self.tt(tS, bv[:K, :, :], _b3(f2, Hn, Vd), ALU.mult)
        self.tt(S, S, f1b, ALU.mult, e="pool")
        self.tt(S, S, tS, ALU.add, e="pool")
    else:
        self.tt(S, S, f1b, ALU.mult, e="pool")
        self.tt(S, S, bv[:K, :, :], ALU.add)


def m_F(self, M):
    t = M.F[M.fi % len(M.F)]
    M.fi += 1
    return t


def m_H(self, M):
    t = M.H[M.hi % len(M.H)]
    M.hi += 1
    return t


def m_out(self, M, g, d, t0, P, col0, O, fin, rows=None):
    gn = g["name"]
    of, mix = self.scr[gn]["of"], self.scr[gn]["mix"]
    key = (gn, t0, col0)
    if d == 0:
        b = Buf()
        self.ofb[key] = b
        self.dma(V(of.t.ap()[t0:t0 + P, col0:col0 + 512], b), O)
    else:
        b = self.ofb[key]
        ofl = m_F(self, M)
        self.dma(ofl[:P, :], V(of.t.ap()[t0:t0 + P, col0:col0 + 512], b))
        self.tt(O, O, ofl[:P, :], ALU.add)
        ob = fin()
        self.dma(self.U(mix.t.ap()[t0:t0 + P, col0:col0 + 512]), ob)


def m_headnorm_gate(self, M, O, P, gate, normw):
    sq = m_F(self, M)
    self.tt(sq[:P, :], O, O, ALU.mult, e="pool")
    s4 = M.sm[5]
    self.red(s4[:P, 0:4], _r3(sq[:P, :], 4, 128))
    self.rsqrt(s4[:P, 0:4], s4[:P, 0:4], 1.0 / 128, EPS)
    self.tt(_r3(O, 4, 128), _r3(O, 4, 128), _b3(s4[:P, 0:4], 4, 128), ALU.mult)
    self.tt(O, O, normw[:P, :], ALU.mult)
    sg = m_F(self, M)
    self.act(sg[:P, :], gate, AF.Sigmoid)
    self.tt(sg[:P, :], sg[:P, :], gate, ALU.mult, e="pool")
    ob = m_H(self, M)
    self.tt(ob[:P, :], O, sg[:P, :], ALU.mult)
    return ob[:P, :]


def m_rwkv(self, M, g, l, d, r0, t0):
    proj = self.scr[g["name"]]["proj"]
    W0, W1, W2 = M.W
    for j, Wt in enumerate(M.W):
        self.dma(Wt[:, 0:1920], self.U(proj.t.ap()[r0 - 1 + j:r0 - 1 + j + 128, OFF_A:OFF_A + 1920]))
    self.tt(W0.all(), W0.all(), W1.all(), ALU.subtract, e="pool")
    self.tt(W0.all(), W0.all(), M.mu[0].all(), ALU.mult, e="pool")
    self.tt(W2.all(), W2.all(), W1.all(), ALU.subtract)
    self.tt(W2.all(), W2.all(), M.mu[1].all(), ALU.mult)
    self.tt(W1.all(), W1.all(), W0.all(), ALU.add, e="pool")
    self.tt(W1.all(), W1.all(), W2.all(), ALU.add)
    r, k, v = W1[:, 0:512], W1[:, 512:1024], W1[:, 1024:1536]
    wl = W1[:, 1536 + 64 * d:1600 + 64 * d]
    al = W1[:, 1664 + 64 * d:1728 + 64 * d]
    gl = W1[:, 1792:1920]
    F = lambda: m_F(self, M)
    Hh = lambda: m_H(self, M)
    idb = self.C["ident_b"]
    kk = F()
    self.tt(kk.all(), k, M.k_k.all(), ALU.mult)
    sq = F()
    self.tt(sq.all(), kk.all(), kk.all(), ALU.mult, e="pool")
    s8 = M.sm[0]
    self.red(s8[:, 0:8], _r3(sq.all(), 8, 64))
    self.rsqrt(s8[:, 0:8], s8[:, 0:8], 1.0, EPS)
    self.tt(_r3(kk.all(), 8, 64), _r3(kk.all(), 8, 64), _b3(s8[:, 0:8], 8, 64), ALU.mult)
    th = M.lT[0]
    self.act(th[:, 0:64], wl, AF.Tanh)
    pt = M.ptr[M.pti % 2]
    M.pti += 1
    self.tr(pt[:64, 0, :], th[:, 0:64], idb.all())
    thT = M.lT[1]
    self.cp(thT[:64, :], pt[:64, 0, :], e="act")
    pw = m_pw(self, M)
    self.mm(pw.all(), thT[:64, :], M.w_up[d].all())
    LW = F()
    self.tt(LW.all(), pw.all(), M.w0[d].all(), ALU.add)
    self.act(LW.all(), LW.all(), AF.Sigmoid)
    self.ts(LW.all(), LW.all(), -float(np.exp(-0.5)), ALU.mult, e="pool")
    ab_ = M.lT[0]
    self.cp(ab_[:, 64:128], al)
    pt = M.ptr[M.pti % 2]
    M.pti += 1
    self.tr(pt[:64, 0, :], ab_[:, 64:128], idb.all())
    alT = M.lT[2]
    self.cp(alT[:64, :], pt[:64, 0, :], e="act")
    pw = m_pw(self, M)
    self.mm(pw.all(), alT[:64, :], M.a_up[d].all())
    a = F()
    self.tt(a.all(), pw.all(), M.a0[d].all(), ALU.add)
    self.act(a.all(), a.all(), AF.Sigmoid)
    kd = F()
    self.stt(kd.all(), a.all(), -1.0, M.k_a.all(), ALU.add, ALU.mult)
    self.stt(kd.all(), kd.all(), 1.0, k, ALU.add, ALU.mult)
    bb = F()
    self.tt(bb.all(), a.all(), kk.all(), ALU.mult, e="pool")
    pw = m_pw(self, M)
    self.mm(pw.all(), M.TmS(d).all(), LW.all())
    e1, e1n = F(), F()
    self.act(e1.all(), pw.all(), AF.Exp)
    self.act(e1n.all(), pw.all(), AF.Exp, scale=-1.0)
    pw = m_pw(self, M)
    self.mm(pw.all(), M.TsS(d).all(), LW.all())
    e2 = sq
    self.act(e2.all(), pw.all(), AF.Exp)
    pc = m_slot(self, M)
    for h in range(8):
        self.mm(pc[:64, 0, h * 4:h * 4 + 4], LW[:, h * 64:(h + 1) * 64], M.sel3[d].all())
    self.act(M.cols[:64, :], pc[:64, 0, 0:32], AF.Exp)
    colv = V(M.cols.t[:64, 0:32].rearrange("p (h j) -> p h j", j=4), M.cols.b)
    rt, kt, bt, ct, vb = Hh(), Hh(), Hh(), Hh(), Hh()
    self.tt(rt.all(), r, e1.all(), ALU.mult)
    self.tt(kt.all(), kd.all(), e1n.all(), ALU.mult)
    self.tt(bt.all(), bb.all(), e1n.all(), ALU.mult, e="pool")
    self.tt(ct.all(), kk.all(), e2.all(), ALU.mult, e="pool")
    self.cp(vb.all(), v, e="act")
    rtT, ktT, btT, ctT = M.TP[0], M.TP[1], M.TP[2], M.TP[3]
    for (src, dst) in ((rt, rtT), (kt, ktT), (bt, btT), (ct, ctT)):
        m_trT(self, M, src.all(), 128, 8, 64, dst)
    self.tt(M.SAp.all(), M.SA.all(), V(M.cols.t[:64, :].rearrange("p (h j) -> p h j", j=4)[:, :, 0:1].to_broadcast([64, 8, 64]), M.cols.b), ALU.mult)
    O = F()
    m_core(self, M, 128, 64, 64, 8, True,
           RaT=lambda h: rtT[:64, h, :], RbT=lambda h: rtT[:64, h, :], KaT=lambda h: ktT[:64, h, :],
           BaT=lambda h: btT[:64, h, :], CaT=lambda h: ctT[:64, h, :], CbT=lambda h: ctT[:64, h, :],
           Kc=lambda h: kt[:, h * 64:(h + 1) * 64], Bc=lambda h: bt[:, h * 64:(h + 1) * 64],
           Vv=lambda h: vb[:, h * 64:(h + 1) * 64],
           Mii=lambda h0, n: _bm(M.maskI(d).all(), 128, n), Mei=lambda h0, n: _bm(M.maskS(d).all(), 128, n),
           f1=colv[:, :, 1], f2=colv[:, :, 2],
           S=M.SA.all(), S0p=lambda h: M.SAp[:, h, :],
           O=_r3(O.all(), 8, 64))

    def fin():
        m8 = M.sm[1]
        self.red(m8[:, 0:8], _r3(O.all(), 8, 64))
        self.ts(m8[:, 0:8], m8[:, 0:8], 1.0 / 64, ALU.mult)
        cen = F()
        self.tt(_r3(cen.all(), 8, 64), _r3(O.all(), 8, 64), _b3(m8[:, 0:8], 8, 64), ALU.subtract)
        sq2 = F()
        self.tt(sq2.all(), cen.all(), cen.all(), ALU.mult, e="pool")
        v8 = M.sm[2]
        self.red(v8[:, 0:8], _r3(sq2.all(), 8, 64))
        self.rsqrt(v8[:, 0:8], v8[:, 0:8], 1.0 / 64, GN_EPS)
        self.tt(_r3(cen.all(), 8, 64), _r3(cen.all(), 8, 64), _b3(v8[:, 0:8], 8, 64), ALU.mult)
        self.tt(cen.all(), cen.all(), M.ln_w.all(), ALU.mult)
        self.tt(cen.all(), cen.all(), M.ln_b.all(), ALU.add, e="pool")
        rk = sq2
        self.tt(rk.all(), r, k, ALU.mult)
        self.tt(rk.all(), rk.all(), M.r_k.all(), ALU.mult, e="pool")
        b8 = M.sm[3]
        self.red(b8[:, 0:8], _r3(rk.all(), 8, 64))
        self.tt(_r3(rk.all(), 8, 64), _r3(v, 8, 64), _b3(b8[:, 0:8], 8, 64), ALU.mult)
        self.tt(cen.all(), cen.all(), rk.all(), ALU.add, e="pool")
        sg = M.lT[0]
        self.act(sg.all(), gl, AF.Sigmoid)
        pt_ = M.ptr[M.pti % 2]
        M.pti += 1
        self.tr(pt_[:, 0, :], sg.all(), idb.all())
        sgT = M.lT[1]
        self.cp(sgT.all(), pt_[:, 0, :], e="act")
        pw_ = m_pw(self, M)
        self.mm(pw_.all(), sgT.all(), M.g_up.all())
        ob = Hh()
        self.tt(ob.all(), cen.all(), pw_.all(), ALU.mult)
        return ob.all()
    m_out(self, M, g, d, t0, 128, 0, O.all(), fin)


def m_hgrn(self, M, g, l, d, r0, t0):
    proj = self.scr[g["name"]]["proj"]
    F = lambda: m_F(self, M)
    Hh = lambda: m_H(self, M)
    WB, WB2 = M.W[0], M.W[1]
    self.dma(WB[:, 0:1920], self.U(proj.t.ap()[r0:r0 + 128, OFF_B:OFF_B + 1920]))
    self.dma(WB2[:, 0:640], self.U(proj.t.ap()[r0:r0 + 128, OFF_B + 1920:OFF_B + 2560]))
    qx = WB[:, 0:512]
    fl = WB[:, 512 + 512 * d:1024 + 512 * d]
    iv = WB[:, 1536:1920]
    gb = WB2[:, 128:640]
    q = F()
    self.act(q.all(), qx, AF.Sigmoid)
    self.tt(q.all(), q.all(), qx, ALU.mult)
    f = F()
    self.act(f.all(), fl, AF.Sigmoid)
    self.tt(f.all(), f.all(), M.oml.all(), ALU.mult, e="pool")
    self.tt(f.all(), f.all(), M.lb.all(), ALU.add, e="pool")
    LF = F()
    self.act(LF.all(), f.all(), AF.Ln)
    kin = F()
    self.act(kin.all(), fl, AF.Sigmoid, scale=-1.0)
    self.tt(kin.all(), kin.all(), M.oml.all(), ALU.mult, e="pool")
    pw = m_pw(self, M)
    self.mm(pw.all(), M.bd32[d].all(), LF.all())
    e1, e1n = F(), F()
    self.act(e1.all(), pw.all(), AF.Exp)
    self.ts(e1n.all(), pw.all(), -1.0, ALU.mult, 80.0, ALU.min)
    self.act(e1n.all(), e1n.all(), AF.Exp)
    rt, kt, vb = Hh(), Hh(), Hh()
    self.tt(rt.all(), q.all(), e1.all(), ALU.mult)
    self.tt(kt.all(), kin.all(), e1n.all(), ALU.mult)
    self.cp(vb[:, 0:384], iv, e="act")
    self.cp(vb[:, 384:512], WB2[:, 0:128], e="act")
    pc = m_slot(self, M)
    for h in range(4):
        self.mm(pc[:, 0, 4 * h:4 * h + 4], LF[:, h * 128:(h + 1) * 128], M.sub4.all())
    self.act(M.cols[:, 0:16], pc[:, 0, 0:16], AF.Exp)
    etot = V(M.cols.t[:, 0:16].rearrange("p (h j) -> p h j", j=4), M.cols.b)
    rtT, ktT = M.TP[4], M.TP[5]
    m_trT(self, M, rt.all(), 128, 4, 128, rtT)
    m_trT(self, M, kt.all(), 128, 4, 128, ktT)
    self.tt(M.rtTm.all(), V(rtT.t[:, 0:4, :].unsqueeze(2).to_broadcast([128, 4, 4, 128]), rtT.b),
            V(M.colm.t[:, :, :].unsqueeze(1).to_broadcast([128, 4, 4, 128]), M.colm.b), ALU.mult)
    self.tt(M.ktm.all(), V(kt.t[:, :].unsqueeze(1).to_broadcast([128, 4, 512]), kt.b),
            V(M.sub4.t[:, :].unsqueeze(2).to_broadcast([128, 4, 512]), M.sub4.b), ALU.mult, e="pool")
    b = m_slot(self, M)
    for h in range(4):
        self.mm(b[:, h, :], ktT[:, h, :], rtT[:, h, :], inc=(h == 3))
    self.tt(M.AkT[:, 0:4, :], b[:, 0:4, :], _bm(M.bd32[d].all(), 128, 4), ALU.mult)
    order = [0, 1, 2, 3] if d == 0 else [3, 2, 1, 0]
    for j in order:
        bj = m_slot(self, M)
        for h in range(4):
            self.mm(bj[:, h, :], M.ktm[:, j, h * 128:(h + 1) * 128], vb[:, h * 128:(h + 1) * 128], inc=(h == 3))
        self.cp(M.S0pAll[:, j, :, :], M.SB.all(), e="pool")
        self.tt(M.SB.all(), M.SB.all(), bj[:, 0:4, :], ALU.add)
        self.tt(M.SB.all(), M.SB.all(), _b3(etot[:, :, j], 4, 128), ALU.mult, e="pool")
    bo = m_slot(self, M)
    for h in range(4):
        self.mm(bo[:, h, :], M.AkT[:, h, :], vb[:, h * 128:(h + 1) * 128], start=True, stop=False, inc=False)
        for idx, j in enumerate(order):
            self.mm(bo[:, h, :], M.rtTm[:, h, j, :], M.S0pAll[:, j, h, :], start=False, stop=(idx == 3),
                    inc=(idx == 3 and h == 3))
    O = F()
    self.cp(_r3(O.all(), 4, 128), bo[:, 0:4, :], e="act")
    m_out(self, M, g, d, t0, 128, 512, O.all(),
          lambda: m_headnorm_gate(self, M, O.all(), 128, gb, M.hnorm))


def m_gdn(self, M, g, l, d, r0, t0):
    proj = self.scr[g["name"]]["proj"]
    F = lambda: m_F(self, M)
    Hh = lambda: m_H(self, M)
    W0, W1, W2 = M.W
    for j, Wt in enumerate(M.W):
        self.dma(Wt[:, 0:1536], self.U(proj.t.ap()[r0 - 1 + j:r0 - 1 + j + 128, OFF_C:OFF_C + 1536]))
    gatec = F()
    self.dma(gatec.all(), self.U(proj.t.ap()[r0:r0 + 128, OFF_C + 1536:OFF_C + 2048]))
    ab = M.sm[0]
    self.dma(ab[:, 0:16], self.U(proj.t.ap()[r0:r0 + 128, OFF_C + 2048:OFF_C + 2064]))
    self.tt(W0[:, 0:1536], W0[:, 0:1536], M.cw[0].all(), ALU.mult, e="pool")
    self.tt(W2[:, 0:1536], W2[:, 0:1536], M.cw[2].all(), ALU.mult)
    self.tt(W1[:, 0:1536], W1[:, 0:1536], M.cw[1].all(), ALU.mult)
    self.tt(W1[:, 0:1536], W1[:, 0:1536], W0[:, 0:1536], ALU.add, e="pool")
    self.tt(W1[:, 0:1536], W1[:, 0:1536], W2[:, 0:1536], ALU.add)
    self.act(W0[:, 0:1536], W1[:, 0:1536], AF.Sigmoid)
    self.tt(W1[:, 0:1536], W1[:, 0:1536], W0[:, 0:1536], ALU.mult)
    q, k, v = W1[:, 0:512], W1[:, 512:1024], W1[:, 1024:1536]
    qn, kn = F(), F()
    sq = F()
    s4 = M.sm[1]
    for (x, xn, scl, c0) in ((q, qn, 128.0 ** -0.5, 0), (k, kn, 1.0, 4)):
        self.tt(sq.all(), x, x, ALU.mult, e="pool")
        self.red(s4[:, c0:c0 + 4], _r3(sq.all(), 4, 128))
        self.rsqrt(s4[:, c0:c0 + 4], s4[:, c0:c0 + 4], 1.0, EPS)
        if scl != 1.0:
            self.ts(s4[:, c0:c0 + 4], s4[:, c0:c0 + 4], scl, ALU.mult)
        self.tt(_r3(xn.all(), 4, 128), _r3(x, 4, 128), _b3(s4[:, c0:c0 + 4], 4, 128), ALU.mult)
    gs = M.sm[2]
    gg, beta, eg = gs[:, 0:4], gs[:, 4:8], gs[:, 8:12]
    self.tt(gg, ab[:, 4 * d:4 * d + 4], M.dtb[:, 4 * d:4 * d + 4], ALU.add)
    self.act(gg, gg, AF.Exp)
    self.act(gg, gg, AF.Ln, bias=1.0)
    self.tt(gg, gg, M.negA[:, 4 * d:4 * d + 4], ALU.mult)
    self.act(beta, ab[:, 8 + 4 * d:12 + 4 * d], AF.Sigmoid)
    self.act(eg, gg, AF.Exp)
    kp, bp = F(), F()
    self.tt(_r3(kp.all(), 4, 128), _r3(kn.all(), 4, 128), _b3(beta, 4, 128), ALU.mult)
    self.tt(_r3(bp.all(), 4, 128), _r3(kp.all(), 4, 128), _b3(eg, 4, 128), ALU.mult)
    pc = m_slot(self, M)
    self.mm(pc[:, 0, 0:4], M.maskI(d).all(), gg)
    self.mm(pc[:, 0, 4:8], M.maskS(d).all(), gg)
    self.mm(pc[:, 0, 8:12], self.C["ones_f"].all(), gg)
    cs = M.sm[3]
    self.cp(cs[:, 0:12], pc[:, 0, 0:12])
    cum, cume, tot = cs[:, 0:4], cs[:, 4:8], cs[:, 8:12]
    ex = M.sm[4]
    ecum, ecume, etc_, f1c = ex[:, 0:4], ex[:, 4:8], ex[:, 8:12], ex[:, 12:16]
    self.act(ex[:, 0:8], cs[:, 0:8], AF.Exp)
    self.tt(etc_, tot, cum, ALU.subtract)
    self.act(etc_, etc_, AF.Exp)
    self.act(f1c, tot, AF.Exp)
    negcum = cs[:, 12:16]
    self.ts(negcum, cum, -1.0, ALU.mult)
    Ra, Rb, Ca, Cb, Ka, Kc, Ba, Bc, vb = [Hh() for _ in range(9)]
    self.cp(Ra.all(), qn.all(), e="act")
    self.tt(_r3(Rb.all(), 4, 128), _r3(qn.all(), 4, 128), _b3(ecum, 4, 128), ALU.mult)
    self.cp(Ca.all(), kn.all(), e="act")
    self.tt(_r3(Cb.all(), 4, 128), _r3(kn.all(), 4, 128), _b3(ecume, 4, 128), ALU.mult)
    self.cp(Ka.all(), kp.all(), e="act")
    self.tt(_r3(Kc.all(), 4, 128), _r3(kp.all(), 4, 128), _b3(etc_, 4, 128), ALU.mult)
    self.cp(Ba.all(), bp.all(), e="act")
    self.tt(_r3(Bc.all(), 4, 128), _r3(bp.all(), 4, 128), _b3(etc_, 4, 128), ALU.mult)
    self.cp(vb.all(), v, e="act")
    for h in range(4):
        Gb = M.Gb[h % 2]
        self.ts(Gb.all(), self.C["ones_f"].all(), gg[:, h:h + 1], ALU.mult)
        for (mask, nmask, dst) in ((M.maskI(d), M.nmI(d), M.Gia[:, h, :]), (M.maskS(d), M.nmS(d), M.Gea[:, h, :])):
            p = m_slot(self, M)
            self.mm(p[:, 0, :], Gb.all(), mask.all())
            self.stt(dst, p[:, 0, :], negcum[:, h:h + 1], nmask.all(), ALU.add, ALU.add)
    self.act(M.Gia.all(), M.Gia.all(), AF.Exp)
    self.act(M.Gea.all(), M.Gea.all(), AF.Exp)
    if True:
        if True:
            pass
    TPs = M.TP
    for (src, dst) in ((Ra, TPs[0]), (Rb, TPs[1]), (Ka, TPs[2]), (Ba, TPs[3]), (Ca, TPs[4]), (Cb, TPs[5])):
        m_trT(self, M, src.all(), 128, 4, 128, dst)
    self.cp(M.SCp.all(), M.SC.all(), e="pool")
    O = F()
    m_core(self, M, 128, 128, 128, 4, True,
           RaT=lambda h: TPs[0][:, h, :], RbT=lambda h: TPs[1][:, h, :], KaT=lambda h: TPs[2][:, h, :],
           BaT=lambda h: TPs[3][:, h, :], CaT=lambda h: TPs[4][:, h, :], CbT=lambda h: TPs[5][:, h, :],
           Kc=lambda h: Kc[:, h * 128:(h + 1) * 128], Bc=lambda h: Bc[:, h * 128:(h + 1) * 128],
           Vv=lambda h: vb[:, h * 128:(h + 1) * 128],
           Mii=lambda h0, n: M.Gia[:, h0:h0 + n, :], Mei=lambda h0, n: M.Gea[:, h0:h0 + n, :],
           f1=f1c, f2=None,
           S=M.SC.all(), S0p=lambda h: M.SCp[:, h, :],
           O=_r3(O.all(), 4, 128))
    m_out(self, M, g, d, t0, 128, 1024, O.all(),
          lambda: m_headnorm_gate(self, M, O.all(), 128, gatec.all(), M.gnorm))


def m_ret(self, M, g, l, d, r0, t0, pos0):
    proj = self.scr[g["name"]]["proj"]
    F = lambda: m_F(self, M)
    Hh = lambda: m_H(self, M)
    WD = M.W[2]
    self.dma(WD[:, 0:1536], self.U(proj.t.ap()[r0:r0 + 128, OFF_D:OFF_D + 1536]))
    q, k, v, gate = WD[:, 0:256], WD[:, 256:512], WD[:, 512:1024], WD[:, 1024:1536]
    if g["rot"]:
        cs_, sn_ = M.rot
        self.dma(cs_.all(), self.U(self.cst["rot_cos"].t.ap()[pos0:pos0 + 128, :]))
        self.dma(sn_.all(), self.U(self.cst["rot_sin"].t.ap()[pos0:pos0 + 128, :]))
        cb_ = V(cs_.t[:, :].unsqueeze(1).to_broadcast([128, 4, 32]), cs_.b)
        sb_ = V(sn_.t[:, :].unsqueeze(1).to_broadcast([128, 4, 32]), sn_.b)
        qk = F()
        t1, t2 = F(), F()
        for (x, c0) in ((q, 0), (k, 256)):
            x3 = _r3(x, 4, 64)
            o3 = _r3(qk[:, c0:c0 + 256], 4, 64)
            a3 = _r3(t1[:, 0:128], 4, 32)
            b3 = _r3(t2[:, 0:128], 4, 32)
            self.tt(a3, x3[:, :, 0:32], cb_, ALU.mult)
            self.tt(b3, x3[:, :, 32:64], sb_, ALU.mult, e="pool")
            self.tt(o3[:, :, 0:32], a3, b3, ALU.subtract)
            a3 = _r3(t1[:, 128:256], 4, 32)
            b3 = _r3(t2[:, 128:256], 4, 32)
            self.tt(a3, x3[:, :, 0:32], sb_, ALU.mult)
            self.tt(b3, x3[:, :, 32:64], cb_, ALU.mult, e="pool")
            self.tt(o3[:, :, 32:64], a3, b3, ALU.add)
        q, k = qk[:, 0:256], qk[:, 256:512]
    Ra, Rb, Ka, Kc, vb = Hh(), Hh(), Hh(), Hh(), Hh()
    self.cp(Ra[:, 0:256], q, e="act")
    self.tt(_r3(Rb[:, 0:256], 4, 64), _r3(q, 4, 64), _b3(M.tabR[d].all(), 4, 64), ALU.mult)
    self.ts(Ka[:, 0:256], k, 0.125, ALU.mult, e="pool")
    self.stt(_r3(Kc[:, 0:256], 4, 64), _r3(k, 4, 64), 0.125, _b3(M.tabK[d].all(), 4, 64), ALU.mult, ALU.mult)
    self.cp(vb.all(), v, e="act")
    RaT, RbT, KaT = M.TP[0], M.TP[1], M.TP[2]
    for (src, dst) in ((Ra, RaT), (Rb, RbT), (Ka, KaT)):
        m_trT(self, M, src[:, 0:256], 128, 4, 64, dst)
    self.cp(M.SDp.all(), M.SD.all(), e="pool")
    O = F()
    m_core(self, M, 128, 64, 128, 4, False,
           RaT=lambda h: RaT[:64, h, :], RbT=lambda h: RbT[:64, h, :], KaT=lambda h: KaT[:64, h, :],
           BaT=None, CaT=None, CbT=None,
           Kc=lambda h: Kc[:, h * 64:(h + 1) * 64], Bc=None,
           Vv=lambda h: vb[:, h * 128:(h + 1) * 128],
           Mii=lambda h0, n: M.GR[d][:, h0:h0 + n, :], Mei=None,
           f1=M.f1r[d][:64, :], f2=None,
           S=M.SD.all(), S0p=lambda h: M.SDp[:, h, :],
           O=_r3(O.all(), 4, 128))
    m_out(self, M, g, d, t0, 128, 1536, O.all(),
          lambda: m_headnorm_gate(self, M, O.all(), 128, gate, M.rnorm))


def phase_m(self, g, l):
    gn = g["name"]
    L = g["L"]
    nt = L // 128
    self.ofb = {}
    which = [m for m in "ABCD" if ("m" + m) in self.dbg] or list("ABCD")
    states = {"A": ("a", "SA", "h k v -> k h v"), "B": ("b", "SB", "h k v -> k h v"),
              "C": ("c", "SC", "h k v -> k h v"), "D": ("d", "SD", "h k v -> k h v")}
    with ExitStack() as es:
        M = m_alloc(self, es, g, l)
        for seq in range(g["nseq"]):
            for d in range(2):
                for m in which:
                    key, tn, pat = states[m]
                    St = getattr(M, tn)
                    if g["init"]:
                        self.dma(St.all(), self.U(self.st_in[key].t.ap()[l, d].rearrange(pat)))
                    else:
                        self.memset(St.all(), 0.0)
                tiles = range(nt) if d == 0 else range(nt - 1, -1, -1)
                for ti in tiles:
                    pos0 = ti * 128
                    r0 = seq * (L + 2) + 1 + pos0
                    t0 = seq * L + pos0
                    sc = (lambda n: self.nc.named_scope(f"{n}_{gn}{l}_{seq}_{d}_{ti}")) if "scopes" in self.dbg else (lambda n: ExitStack())
                    if "A" in which:
                        with sc("mA"):
                            m_rwkv(self, M, g, l, d, r0, t0)
                    if "B" in which:
                        with sc("mB"):
                            m_hgrn(self, M, g, l, d, r0, t0)
                    if "C" in which:
                        with sc("mC"):
                            m_gdn(self, M, g, l, d, r0, t0)
                    if "D" in which:
                        with sc("mD"):
                            m_ret(self, M, g, l, d, r0, t0, pos0)
                if not g["init"]:
                    for m in which:
                        key, tn, pat = states[m]
                        St = getattr(M, tn)
                        self.dma(self.U(self.ns[key].t.ap()[seq, l, d].rearrange(pat)), St.all())
    self.fw.barrier()


Prog.phase_m = phase_m
```
